# Optimizing a Trainium2 kernel written in Bass

```python
import jax, jax.numpy as jnp
from jax import lax
import numpy as np

D_MODEL = 1024
BATCH = 8
SEQ = 4096
DEPTH = 2

CTX_LEN = 256
GRID_W = 64

ATT_HEADS = 8
ATT_KV_HEADS = 2
ATT_GROUP = ATT_HEADS // ATT_KV_HEADS
ATT_HEAD_DIM = 64
WINDOW = 128
ATT_BLOCK = 128
BAND_SIDE = WINDOW // ATT_BLOCK
ROPE_BASE = 10000.0

ML_HEADS = 4
ML_DK = 64
ML_DV = 128
ML_CONV_W = 3

HG_HEADS = 4
HG_DK = 64
HG_DV = 128

CHUNK = 64
D_FF = ((8 * D_MODEL // 3 + 127) // 128) * 128
N_BRANCH = 3
N_SUB = 3
N_MOD = 3 * N_SUB
EPS = 1e-6

ATT_Q = ATT_HEADS * ATT_HEAD_DIM
ATT_KV = ATT_KV_HEADS * ATT_HEAD_DIM
ML_QK = ML_HEADS * ML_DK
ML_V = ML_HEADS * ML_DV
HG_K = HG_HEADS * HG_DK
HG_V = HG_HEADS * HG_DV
COLUMN_SIZES = (ATT_Q, ATT_KV, ATT_KV,
                ML_QK, ML_QK, ML_V, ML_V, 2 * ML_HEADS, 2 * ML_HEADS,
                HG_K, HG_K, HG_K, HG_V, HG_V,
                N_BRANCH * D_MODEL)
D_IN = sum(COLUMN_SIZES)

kernel_name = "hybrid_swa_mlstm_hgrn2_diffusion_block"


def rms_norm(x, g):
    x32 = x.astype(jnp.float32)
    y = x32 * lax.rsqrt(jnp.mean(x32 * x32, axis=-1, keepdims=True) + EPS)
    return (y * g.astype(jnp.float32)).astype(x.dtype)


def ada_pre(x, mod, j, g):
    return rms_norm(x, g) * (1.0 + mod[..., 3 * j + 1, :]) + mod[..., 3 * j, :]


def ada_post(x, y, mod, j, g, w):
    return x + w * mod[..., 3 * j + 2, :] * rms_norm(y, g)


def macaron_ffn(x, mod, j, g_pre, g_post, w1, w3, w2):
    h = ada_pre(x, mod, j, g_pre)
    y = (jax.nn.silu(h @ w1) * (h @ w3)) @ w2
    return ada_post(x, y, mod, j, g_post, 0.5)


def split_columns(p):
    idx = np.cumsum(COLUMN_SIZES)[:-1].tolist()
    return jnp.split(p, idx, axis=-1)


def short_conv(x, w):
    K = w.shape[0]
    L = x.shape[1]
    r = K // 2
    xp = jnp.pad(x, ((0, 0), (r, r), (0, 0)))
    out = xp[:, 0:L] * w[0]
    for j in range(1, K):
        out = out + xp[:, j:j + L] * w[j]
    return out


def axial_angles(L):
    rows = L // GRID_W
    row = jnp.repeat(jnp.arange(rows, dtype=jnp.float32), GRID_W)
    col = jnp.tile(jnp.arange(GRID_W, dtype=jnp.float32), rows)
    n_freq = ATT_HEAD_DIM // 4
    inv = ROPE_BASE ** (-jnp.arange(n_freq, dtype=jnp.float32) / n_freq)
    return row[:, None] * inv, col[:, None] * inv


def rope_axis(x, ang):
    F = ang.shape[-1]
    shape = (ang.shape[0],) + (1,) * (x.ndim - 3) + (F,)
    cos = jnp.cos(ang).reshape(shape).astype(x.dtype)
    sin = jnp.sin(ang).reshape(shape).astype(x.dtype)
    x1, x2 = x[..., :F], x[..., F:]
    return jnp.concatenate([x1 * cos - x2 * sin, x2 * cos + x1 * sin], axis=-1)


def rope_2d(x, ang_row, ang_col):
    half = ATT_HEAD_DIM // 2
    return jnp.concatenate([rope_axis(x[..., :half], ang_row), rope_axis(x[..., half:], ang_col)], axis=-1)


def sink_softmax(s, sink):
    sk = sink.astype(jnp.float32)[:, :, None, None]
    m = jnp.maximum(jnp.max(s, axis=-1, keepdims=True), sk)
    p = jnp.exp(s - m)
    return p / (jnp.sum(p, axis=-1, keepdims=True) + jnp.exp(sk - m))


def context_attention(q, k, v, sink):
    B, Lc = q.shape[:2]
    s = jnp.einsum('bqgrd,bkgd->bgrqk', q, k).astype(jnp.float32)
    p = sink_softmax(s, sink.reshape(ATT_KV_HEADS, ATT_GROUP)).astype(v.dtype)
    return jnp.einsum('bgrqk,bkgd->bqgrd', p, v).reshape(B, Lc, ATT_Q)


def banded_attention(q, k, v, k_ctx, v_ctx, sink):
    B, L, G, R, d = q.shape
    nb = L // ATT_BLOCK
    pad = BAND_SIDE * ATT_BLOCK
    n_band = (2 * BAND_SIDE + 1) * ATT_BLOCK

    def band(a):
        ap = jnp.pad(a, ((0, 0), (pad, pad), (0, 0), (0, 0)))
        pieces = [ap[:, s * ATT_BLOCK: s * ATT_BLOCK + L].reshape(B, nb, ATT_BLOCK, G, d)
                  for s in range(2 * BAND_SIDE + 1)]
        return jnp.moveaxis(jnp.concatenate(pieces, axis=2), 1, 0)

    qb = jnp.moveaxis(q.reshape(B, nb, ATT_BLOCK, G, R, d), 1, 0)
    kb, vb = band(k), band(v)
    rel = (jnp.arange(n_band) - pad)[None, :] - jnp.arange(ATT_BLOCK)[:, None]
    in_window = jnp.abs(rel) <= WINDOW
    sink_gr = sink.reshape(G, R)

    def block(inp):
        i, qi, ki, vi = inp
        kpos = i * ATT_BLOCK - pad + jnp.arange(n_band)
        valid = in_window & ((kpos >= 0) & (kpos < L))[None, :]
        s_lat = jnp.einsum('bqgrd,bkgd->bgrqk', qi, ki).astype(jnp.float32)
        s_lat = jnp.where(valid, s_lat, -jnp.inf)
        s_ctx = jnp.einsum('bqgrd,bkgd->bgrqk', qi, k_ctx).astype(jnp.float32)
        p = sink_softmax(jnp.concatenate([s_lat, s_ctx], axis=-1), sink_gr).astype(vi.dtype)
        return jnp.einsum('bgrqk,bkgd->bqgrd', p, jnp.concatenate([vi, v_ctx], axis=1))

    out = lax.map(block, (jnp.arange(nb), qb, kb, vb))
    return jnp.moveaxis(out, 0, 1).reshape(B, L, G * R * d)


def mlstm_scan(q, k, v, log_i, log_f, state):
    B, H, L, _ = q.shape
    nc = L // CHUNK
    tri = jnp.tril(jnp.ones((CHUNK, CHUNK), bool))

    def chunks(a):
        return jnp.moveaxis(a.reshape((B, H, nc, CHUNK) + a.shape[3:]), 2, 0)

    def step(carry, inp):
        C, n, m = carry
        qc, kc, vc, ic, fc = inp
        b = jnp.cumsum(fc, axis=-1)
        logw = jnp.where(tri, b[..., :, None] - b[..., None, :] + ic[..., None, :], -jnp.inf)
        inter = b + m[..., None]
        m_t = jnp.maximum(inter, jnp.max(logw, axis=-1))
        w_prev = jnp.exp(inter - m_t)
        scores = jnp.einsum('bhtd,bhsd->bhts', qc, kc) * jnp.exp(logw - m_t[..., None])
        num = (w_prev[..., None] * jnp.einsum('bhtd,bhde->bhte', qc, C)
               + jnp.einsum('bhts,bhse->bhte', scores, vc))
        den = w_prev * jnp.einsum('bhtd,bhd->bht', qc, n) + jnp.sum(scores, axis=-1)
        h = num / jnp.maximum(jnp.abs(den), jnp.exp(-m_t))[..., None]
        bl = b[..., -1]
        logu = bl[..., None] - b + ic
        m_new = jnp.maximum(bl + m, jnp.max(logu, axis=-1))
        a_prev = jnp.exp(bl + m - m_new)
        u = jnp.exp(logu - m_new[..., None])
        C = a_prev[..., None, None] * C + jnp.einsum('bhs,bhsd,bhse->bhde', u, kc, vc)
        n = a_prev[..., None] * n + jnp.einsum('bhs,bhsd->bhd', u, kc)
        return (C, n, m_new), h

    state, hs = lax.scan(step, state, (chunks(q), chunks(k), chunks(v), chunks(log_i), chunks(log_f)))
    return jnp.moveaxis(hs, 0, 2).reshape(B, H, L, -1), state


def hgrn2_scan(q, k, v, log_f, S0):
    B, H, L, _ = q.shape
    nc = L // CHUNK
    tri = jnp.tril(jnp.ones((CHUNK, CHUNK), bool))

    def chunks(a):
        return jnp.moveaxis(a.reshape(B, H, nc, CHUNK, a.shape[-1]), 2, 0)

    def step(S, inp):
        qc, kc, vc, fc = inp
        b = jnp.cumsum(fc, axis=2)
        o = jnp.einsum('bhtd,bhde->bhte', qc * jnp.exp(b), S)
        diff = b[:, :, :, None, :] - b[:, :, None, :, :]
        decay = jnp.exp(jnp.where(tri[:, :, None], diff, -jnp.inf))
        A = jnp.einsum('bhtd,bhsd,bhtsd->bhts', qc, kc, decay)
        o = o + jnp.einsum('bhts,bhse->bhte', A, vc)
        bl = b[:, :, -1:, :]
        S = (jnp.exp(bl[:, :, 0, :, None]) * S
             + jnp.einsum('bhsd,bhse->bhde', kc * jnp.exp(bl - b), vc))
        return S, o

    S, o = lax.scan(step, S0, (chunks(q), chunks(k), chunks(v), chunks(log_f)))
    return jnp.moveaxis(o, 0, 2).reshape(B, H, L, -1), S


def _identity(a):
    return a


def _reverse(a):
    return jnp.flip(a, axis=2)


def project_stream(p, ml_conv, ml_f_bias, lb, ang):
    B, L, _ = p.shape
    f32 = jnp.float32
    (a_q, a_k, a_v, m_q, m_k, m_v, m_o, m_i, m_f,
     h_q, h_f_fwd, h_f_bwd, h_i, h_g, br_g) = split_columns(p)

    def heads(a, h):
        return jnp.swapaxes(a.reshape(B, L, h, -1), 1, 2).astype(f32)

    q = a_q.reshape(B, L, ATT_KV_HEADS, ATT_GROUP, ATT_HEAD_DIM)
    k = a_k.reshape(B, L, ATT_KV_HEADS, ATT_HEAD_DIM)
    v = a_v.reshape(B, L, ATT_KV_HEADS, ATT_HEAD_DIM)
    if ang is not None:
        q = rope_2d(q, ang[0], ang[1])
        k = rope_2d(k, ang[0], ang[1])
    q = q * ATT_HEAD_DIM ** -0.5

    mq = jax.nn.silu(short_conv(m_q, ml_conv[:, :ML_QK]))
    mk = jax.nn.silu(short_conv(m_k, ml_conv[:, ML_QK:])) * ML_DK ** -0.5
    ml_i = jnp.transpose(m_i.reshape(B, L, 2, ML_HEADS).astype(f32), (2, 0, 3, 1))
    ml_f = jnp.transpose(jax.nn.log_sigmoid(m_f.reshape(B, L, 2, ML_HEADS).astype(f32) + ml_f_bias),
                         (2, 0, 3, 1))

    lb_h = lb.reshape(HG_HEADS, HG_DK)
    hq = jax.nn.silu(h_q) * HG_DK ** -0.5
    hg_k, hg_f = [], []
    for z in (h_f_fwd, h_f_bwd):
        z = z.reshape(B, L, HG_HEADS, HG_DK).astype(f32)
        log_f = jnp.logaddexp(jnp.log(lb_h), jnp.log1p(-lb_h) + jax.nn.log_sigmoid(z))
        hg_f.append(jnp.swapaxes(log_f, 1, 2))
        hg_k.append(jnp.swapaxes((1.0 - lb_h) * jax.nn.sigmoid(-z), 1, 2))

    return dict(att_q=q, att_k=k, att_v=v,
                ml_q=heads(mq, ML_HEADS), ml_k=heads(mk, ML_HEADS), ml_v=heads(m_v, ML_HEADS),
                ml_o=m_o, ml_i=ml_i, ml_f=ml_f,
                hg_q=heads(hq, HG_HEADS), hg_k=hg_k, hg_f=hg_f, hg_v=heads(h_i, HG_HEADS), hg_g=h_g,
                br_g=br_g)


def mlstm_bidir(sx, sc):
    B = sc['ml_q'].shape[0]
    f32 = jnp.float32
    init = (jnp.zeros((B, ML_HEADS, ML_DK, ML_DV), f32), jnp.zeros((B, ML_HEADS, ML_DK), f32),
            jnp.zeros((B, ML_HEADS), f32))
    out_x, out_c = [], []
    for d in range(2):
        fl = _reverse if d == 1 else _identity
        hc, st = mlstm_scan(fl(sc['ml_q']), fl(sc['ml_k']), fl(sc['ml_v']),
                            fl(sc['ml_i'][d]), fl(sc['ml_f'][d]), init)
        hx, _ = mlstm_scan(fl(sx['ml_q']), fl(sx['ml_k']), fl(sx['ml_v']),
                           fl(sx['ml_i'][d]), fl(sx['ml_f'][d]), st)
        out_x.append(fl(hx))
        out_c.append(fl(hc))
    return out_x[0] + out_x[1], out_c[0] + out_c[1]


def hgrn2_bidir(sx, sc):
    B = sc['hg_q'].shape[0]
    init = jnp.zeros((B, HG_HEADS, HG_DK, HG_DV), jnp.float32)
    out_x, out_c = [], []
    for d in range(2):
        fl = _reverse if d == 1 else _identity
        oc, st = hgrn2_scan(fl(sc['hg_q']), fl(sc['hg_k'][d]), fl(sc['hg_v']), fl(sc['hg_f'][d]), init)
        ox, _ = hgrn2_scan(fl(sx['hg_q']), fl(sx['hg_k'][d]), fl(sx['hg_v']), fl(sx['hg_f'][d]), st)
        out_x.append(fl(ox))
        out_c.append(fl(oc))
    return out_x[0] + out_x[1], out_c[0] + out_c[1]


def mlstm_readout(h, o_gate, g):
    B, _, L, _ = h.shape
    y = rms_norm(jnp.swapaxes(h, 1, 2), g.reshape(ML_HEADS, ML_DV)).reshape(B, L, ML_V)
    return (y * jax.nn.sigmoid(o_gate.astype(jnp.float32))).astype(o_gate.dtype)


def hgrn2_readout(o, gate, g):
    B, _, L, _ = o.shape
    y = rms_norm(jnp.swapaxes(o, 1, 2), g.reshape(HG_HEADS, HG_DV)).reshape(B, L, HG_V)
    return (y * jax.nn.silu(gate.astype(jnp.float32))).astype(gate.dtype)


def merge_branches(br_g, att, ml, hg, wb_att, wb_ml, wb_hg, w_out):
    B, L, _ = att.shape
    g = jax.nn.sigmoid(br_g).reshape(B, L, N_BRANCH, D_MODEL)
    y = (g[..., 0, :] * (att @ wb_att) + g[..., 1, :] * (ml @ wb_ml) + g[..., 2, :] * (hg @ wb_hg))
    return y @ w_out


def token_mix(hx, hc, ang, w_in, sink, ml_conv, ml_f_bias, ml_norm, lb, hg_norm,
              wb_att, wb_ml, wb_hg, w_out, need_ctx_out):
    sx = project_stream(hx @ w_in, ml_conv, ml_f_bias, lb, ang)
    sc = project_stream(hc @ w_in, ml_conv, ml_f_bias, lb, None)
    att_x = banded_attention(sx['att_q'], sx['att_k'], sx['att_v'], sc['att_k'], sc['att_v'], sink)
    ml_x, ml_c = mlstm_bidir(sx, sc)
    hg_x, hg_c = hgrn2_bidir(sx, sc)
    yx = merge_branches(sx['br_g'], att_x.astype(hx.dtype),
                        mlstm_readout(ml_x, sx['ml_o'], ml_norm),
                        hgrn2_readout(hg_x, sx['hg_g'], hg_norm), wb_att, wb_ml, wb_hg, w_out)
    yc = None
    if need_ctx_out:
        att_c = context_attention(sc['att_q'], sc['att_k'], sc['att_v'], sink)
        yc = merge_branches(sc['br_g'], att_c.astype(hc.dtype),
                            mlstm_readout(ml_c, sc['ml_o'], ml_norm),
                            hgrn2_readout(hg_c, sc['hg_g'], hg_norm), wb_att, wb_ml, wb_hg, w_out)
    return yx, yc


def setup_inputs(seed: int = 0) -> dict:
    key = jax.random.key(seed)
    ks = jax.random.split(key, 24)
    f32 = jnp.float32

    def nrm(k, shape, scale):
        return jax.random.normal(k, shape, f32) * scale

    return {
        "x": nrm(ks[0], (BATCH, SEQ, D_MODEL), 1.0),
        "c": nrm(ks[1], (BATCH, D_MODEL), 1.0),
        "ctx": nrm(ks[2], (BATCH, CTX_LEN, D_MODEL), 1.0),
        "c_ctx": nrm(ks[3], (D_MODEL,), 1.0),
        "w_ada": nrm(ks[4], (DEPTH, D_MODEL, N_MOD * D_MODEL), 0.5 * D_MODEL ** -0.5),
        "b_ada": nrm(ks[5], (DEPTH, N_MOD * D_MODEL), 0.02),
        "norm_pre": 1.0 + nrm(ks[6], (DEPTH, N_SUB, D_MODEL), 0.02),
        "norm_post": 1.0 + nrm(ks[7], (DEPTH, N_SUB, D_MODEL), 0.02),
        "ffn_w1": nrm(ks[8], (DEPTH, 2, D_MODEL, D_FF), D_MODEL ** -0.5),
        "ffn_w3": nrm(ks[9], (DEPTH, 2, D_MODEL, D_FF), D_MODEL ** -0.5),
        "ffn_w2": nrm(ks[10], (DEPTH, 2, D_FF, D_MODEL), D_FF ** -0.5),
        "w_in": nrm(ks[11], (DEPTH, D_MODEL, D_IN), D_MODEL ** -0.5),
        "att_sink": nrm(ks[12], (DEPTH, ATT_HEADS), 0.5),
        "ml_conv": nrm(ks[13], (DEPTH, ML_CONV_W, 2 * ML_QK), ML_CONV_W ** -0.5),
        "ml_f_bias": jnp.linspace(3.0, 6.0, ML_HEADS, dtype=f32) + nrm(ks[14], (DEPTH, 2, ML_HEADS), 0.1),
        "ml_norm": 1.0 + nrm(ks[15], (DEPTH, ML_V), 0.02),
        "hg_lb_logits": nrm(ks[16], (DEPTH, HG_K), 0.1),
        "hg_norm": 1.0 + nrm(ks[17], (DEPTH, HG_V), 0.02),
        "w_branch_att": nrm(ks[18], (DEPTH, ATT_Q, D_MODEL), ATT_Q ** -0.5),
        "w_branch_ml": nrm(ks[19], (DEPTH, ML_V, D_MODEL), ML_V ** -0.5),
        "w_branch_hg": nrm(ks[20], (DEPTH, HG_V, D_MODEL), HG_V ** -0.5),
        "w_out": nrm(ks[21], (DEPTH, D_MODEL, D_MODEL), D_MODEL ** -0.5),
    }


def reference(x, c, ctx, c_ctx, w_ada, b_ada, norm_pre, norm_post, ffn_w1, ffn_w3, ffn_w2,
              w_in, att_sink, ml_conv, ml_f_bias, ml_norm, hg_lb_logits, hg_norm,
              w_branch_att, w_branch_ml, w_branch_hg, w_out):
    B, L, D = x.shape
    ang = axial_angles(L)
    lb_all = jnp.cumsum(jax.nn.softmax(hg_lb_logits.astype(jnp.float32), axis=0), axis=0)
    lb_all = lb_all - lb_all[0:1]
    cx = ctx
    for l in range(DEPTH):
        last = l == DEPTH - 1
        mod_x = (jax.nn.silu(c) @ w_ada[l] + b_ada[l]).reshape(B, 1, N_MOD, D)
        mod_c = (jax.nn.silu(c_ctx) @ w_ada[l] + b_ada[l]).reshape(1, 1, N_MOD, D)
        x = macaron_ffn(x, mod_x, 0, norm_pre[l, 0], norm_post[l, 0], ffn_w1[l, 0], ffn_w3[l, 0], ffn_w2[l, 0])
        cx = macaron_ffn(cx, mod_c, 0, norm_pre[l, 0], norm_post[l, 0], ffn_w1[l, 0], ffn_w3[l, 0], ffn_w2[l, 0])
        hx = ada_pre(x, mod_x, 1, norm_pre[l, 1])
        hc = ada_pre(cx, mod_c, 1, norm_pre[l, 1])
        yx, yc = token_mix(hx, hc, ang, w_in[l], att_sink[l], ml_conv[l], ml_f_bias[l], ml_norm[l],
                           lb_all[l], hg_norm[l], w_branch_att[l], w_branch_ml[l], w_branch_hg[l],
                           w_out[l], not last)
        x = ada_post(x, yx, mod_x, 1, norm_post[l, 1], 1.0)
        x = macaron_ffn(x, mod_x, 2, norm_pre[l, 2], norm_post[l, 2], ffn_w1[l, 1], ffn_w3[l, 1], ffn_w2[l, 1])
        if not last:
            cx = ada_post(cx, yc, mod_c, 1, norm_post[l, 1], 1.0)
            cx = macaron_ffn(cx, mod_c, 2, norm_pre[l, 2], norm_post[l, 2], ffn_w1[l, 1], ffn_w3[l, 1], ffn_w2[l, 1])
    return x
```

```python
import numpy as np
import concourse.bass as bass
import concourse.mybir as mybir
from concourse.bass_utils import run_bass_kernel_spmd
from contextlib import ExitStack

F32 = mybir.dt.float32
BF16 = mybir.dt.bfloat16
ALU = mybir.AluOpType
AF = mybir.ActivationFunctionType
AX = mybir.AxisListType


class Reg:
    __slots__ = ("name", "lw", "lr", "dsem", "dcnt", "dkind", "excl")

    def __init__(self, name):
        self.name = name
        self.lw = []
        self.lr = []
        self.dsem = None
        self.dcnt = 0
        self.dkind = None
        self.excl = False


class Eng:
    def __init__(self, name, eng, sem):
        self.name = name
        self.eng = eng
        self.sem = sem
        self.cnt = 0
        self.seen = {}


class _Op:
    __slots__ = ("id", "eng", "fns", "r", "w", "piece", "dma", "preds", "dur", "lat")


class _FakeInst:
    def then_inc(self, *a, **k):
        return self


class _FakeEng:
    def __init__(self):
        self.calls = []

    def __getattr__(self, name):
        def f(*a, **k):
            self.calls.append((name, a, k))
            return _FakeInst()
        return f


def _nfree(ap):
    sh = ap.shape
    n = 1
    for d_ in sh[1:]:
        n *= int(d_)
    return n


def _estimate(ename, calls):
    t = 0.0
    for name, a, k in calls:
        try:
            if name == "matmul":
                rhs = a[2] if len(a) > 2 else k["rhs"]
                n = _nfree(rhs)
                mult = 4 if rhs.dtype == F32 else 1
                t += (max(n, 64) * mult + 40) / 2400.0
            elif name == "transpose":
                t += 0.16
            else:
                out = a[0] if a else k.get("out")
                n = _nfree(out)
                if name == "reciprocal":
                    t += 0.1 + n * 0.0064
                elif ename == "scalar":
                    t += 0.17 + n / 1200.0
                elif ename == "gpsimd":
                    t += 0.2 + n / 480.0
                else:
                    t += 0.15 + n / 960.0
        except Exception:
            t += 0.3
    return t


class FW:
    SAME_ENGINE_SYNC = True
    SBUF_LIMIT = 206 * 1024
    WINDOW = 40
    LAT = 0.25

    def __init__(self, nc, es):
        self.nc = nc
        self.es = es
        self.es0 = es
        self.sem_pool = {}
        self.sems = {}
        self.E = {}
        for name in ("tensor", "vector", "scalar", "gpsimd", "sync"):
            sem = es.enter_context(nc.semaphore("s_" + name))
            self.sems[id(sem)] = sem
            self.E[name] = Eng(name, getattr(nc, name), sem)
        self.nwait = 0
        self.ninst = 0
        self.dregs = []
        self.ops = []
        self.group = {}
        self.events = {}
        self.next_id = 0
        self.semcnt = {}

    def reg(self, name):
        return Reg(name)

    def sbuf(self, name, shape, dt):
        self.uid = getattr(self, "uid", 0) + 1
        name = "%s_%d" % (name, self.uid)
        t = self.es.enter_context(self.nc.sbuf_tensor(name, list(shape), dt))
        nb = int(np.prod(shape[1:])) * (2 if dt == BF16 else 4)
        nb = (nb + 31) // 32 * 32
        self.sb_used = getattr(self, "sb_used", 0) + nb
        self.sb_peak = max(getattr(self, "sb_peak", 0), self.sb_used)
        assert self.sb_used <= self.SBUF_LIMIT, ("SBUF over budget", name, self.sb_used)
        return t, Reg(name)

    def psum(self, name, shape, dt):
        t = self.es.enter_context(self.nc.psum_tensor(name, list(shape), dt))
        rg = Reg(name)
        rg.excl = True
        return t, rg

    def _dsem(self, reg, kind):
        if reg.dsem is not None:
            assert reg.dkind == kind, (reg.name, reg.dkind, kind)
        if reg.dsem is None:
            reg.dkind = kind
            pool = self.sem_pool.setdefault(kind, [])
            if pool:
                reg.dsem, reg.dcnt = pool.pop()
            else:
                reg.dsem = self.es0.enter_context(self.nc.semaphore("d%d" % len(self.sems)))
                self.sems[id(reg.dsem)] = reg.dsem
                reg.dcnt = 0
            self.semcnt[id(reg.dsem)] = reg.dcnt
            self.dregs.append(reg)
        return reg.dsem

    def release_dregs(self, keep=()):
        keep_ids = {id(k) for k in keep}
        rest = []
        for rg in self.dregs:
            if id(rg) in keep_ids:
                rest.append(rg)
            else:
                self.sem_pool[rg.dkind].append((rg.dsem, self.semcnt[id(rg.dsem)]))
                rg.dsem = None
        self.dregs = rest

    def _record(self, o):
        preds = set()
        for b in o.r:
            preds.update(b.lw)
            if b.excl:
                for p in b.lr:
                    preds.add(p)
        for b in o.w:
            if not o.piece:
                preds.update(b.lw)
            preds.update(b.lr)
        preds.discard(o.id)
        o.preds = preds
        for b in o.r:
            b.lr.append(o.id)
        for b in o.w:
            if o.piece:
                b.lw.append(o.id)
            else:
                b.lw = [o.id]
                b.lr = []
        self.ops.append(o)

    def op(self, ename, fn, r=(), w=(), sig=True, piece=False):
        fe = _FakeEng()
        fn(fe)
        dur = _estimate(ename, fe.calls)
        g = self.group.get(ename)
        if g is None:
            g = _Op()
            g.id = self.next_id
            self.next_id += 1
            g.eng = ename
            g.fns, g.r, g.w = [], [], []
            g.piece = piece
            g.dma = None
            g.dur = 0.0
        g.fns.extend(fe.calls)
        g.r.extend(r)
        g.w.extend(w)
        g.dur += dur
        if not sig:
            self.group[ename] = g
            return
        self.group.pop(ename, None)
        self._record(g)

    def dma(self, qname, out, in_, sb, r=(), w=(), piece=False, **kw):
        assert qname not in self.group
        self._dsem(sb, "sw" if qname == "gpsimd" else "hw")
        o = _Op()
        o.id = self.next_id
        self.next_id += 1
        o.eng = qname
        o.fns = []
        o.r, o.w = list(r), list(w)
        o.piece = piece
        o.dma = (out, in_, sb, kw)
        try:
            nbytes = _nfree(out) * int(out.shape[0]) * (2 if out.dtype == BF16 else 4)
        except Exception:
            nbytes = 1 << 18
        o.dur = 0.4
        o.lat = 2.0 + nbytes / 1.5e5
        self._record(o)

    def flush(self):
        assert not self.group, "open accumulation group at flush"
        ops = self.ops
        self.ops = []
        n = len(ops)
        if n == 0:
            return
        idx = {o.id: i for i, o in enumerate(ops)}
        lpreds = [[idx[p] for p in o.preds if p in idx] for o in ops]
        succs = [[] for _ in range(n)]
        indeg = [0] * n
        for i, ps in enumerate(lpreds):
            indeg[i] = len(ps)
            for p in ps:
                succs[p].append(i)
        engs = list(self.E.keys())
        seq = {e: [] for e in engs}
        for i, o in enumerate(ops):
            seq[o.eng].append(i)
        pos = {e: 0 for e in engs}
        sched = [False] * n
        done = [0.0] * n
        ready_t = [0.0] * n
        etime = {e: 0.0 for e in engs}
        order = []
        import os as _os2
        W = int(_os2.environ.get('SCHED_W', self.WINDOW))
        LAT = float(_os2.environ.get('SCHED_LAT', self.LAT))
        remaining = n
        while remaining:
            best = None
            for e in engs:
                sq = seq[e]
                p0 = pos[e]
                while p0 < len(sq) and sched[sq[p0]]:
                    p0 += 1
                pos[e] = p0
                te = etime[e]
                cand = None
                for q in range(p0, min(len(sq), p0 + W)):
                    i = sq[q]
                    if sched[i] or indeg[i]:
                        continue
                    st = ready_t[i] if ready_t[i] > te else te
                    if cand is None or st < cand[0] - 1e-9:
                        cand = (st, i)
                        if st <= te:
                            break
                if cand is not None and (best is None or cand[0] < best[0] - 1e-9 or (abs(cand[0] - best[0]) <= 1e-9 and cand[1] < best[1])):
                    best = (cand[0], cand[1], e)
            assert best is not None, "scheduler deadlock"
            st, i, e = best
            o = ops[i]
            sched[i] = True
            remaining -= 1
            etime[e] = st + o.dur
            fin = st + o.dur + (o.lat if o.dma is not None else 0.0)
            done[i] = fin
            order.append(i)
            for s_ in succs[i]:
                indeg[s_] -= 1
                lat = LAT if ops[s_].eng != e else 0.08
                if fin + lat > ready_t[s_]:
                    ready_t[s_] = fin + lat
        self.est_time = getattr(self, "est_time", 0.0) + max(done) if done else 0.0
        for i in order:
            self._emit(ops[i])

    def _emit(self, o):
        E = self.E[o.eng]
        need = {}
        for p in o.preds:
            semkey, val, is_dma, ename = self.events[p]
            if is_dma:
                val = self.semcnt[semkey]
            elif ename == E.name and (E.name == "tensor" or not self.SAME_ENGINE_SYNC):
                continue
            if need.get(semkey, 0) < val:
                need[semkey] = val
        for semkey, val in need.items():
            if E.seen.get(semkey, 0) >= val:
                continue
            E.eng.wait_ge(self.sems[semkey], val)
            E.seen[semkey] = val
            self.nwait += 1
        if o.dma is not None:
            out, in_, sb, kw = o.dma
            inst = E.eng.dma_start(out=out, in_=in_, **kw)
            self.ninst += 1
            key = id(sb.dsem)
            self.semcnt[key] += 16
            sb.dcnt = self.semcnt[key]
            inst.then_inc(sb.dsem, 16)
            self.events[o.id] = (key, self.semcnt[key], True, E.name)
        else:
            inst = None
            for (cname, ca, ck) in o.fns:
                inst = getattr(E.eng, cname)(*ca, **ck)
                self.ninst += 1
            E.cnt += 1
            inst.then_inc(E.sem, 1)
            self.events[o.id] = (id(E.sem), E.cnt, False, E.name)

    def barrier(self):
        self.flush()
        for name, E in self.E.items():
            for oname, O in self.E.items():
                if O.cnt == 0:
                    continue
                if E.seen.get(id(O.sem), 0) < O.cnt:
                    E.eng.wait_ge(O.sem, O.cnt)
                    E.seen[id(O.sem)] = O.cnt
            for rg in self.dregs:
                key = id(rg.dsem)
                if self.semcnt[key] > 0 and E.seen.get(key, 0) < self.semcnt[key]:
                    E.eng.wait_ge(rg.dsem, self.semcnt[key])
                    E.seen[key] = self.semcnt[key]

    def finish(self, regs=()):
        self.flush()
        E = self.E["sync"]
        for name, e in self.E.items():
            if name == "sync":
                continue
            if e.cnt > 0 and E.seen.get(id(e.sem), 0) < e.cnt:
                E.eng.wait_ge(e.sem, e.cnt)
        for rg in self.dregs:
            key = id(rg.dsem)
            if self.semcnt[key] > 0:
                E.eng.wait_ge(rg.dsem, self.semcnt[key])


D = 1024
KC = 8
L = 4096
LC = 256
T = L + LC
DFF = 2816
FC = DFF // 128
DIN = 7184
DEPTH = 2
EPS = 1e-6
C_AQ, C_AK, C_AV = 0, 512, 640
C_MQ, C_MK, C_MV, C_MO, C_MI, C_MF = 768, 1024, 1280, 1792, 2304, 2312
C_HQ, C_HF, C_HV, C_HG, C_BR = 2320, 2576, 3088, 3600, 4112
NT = 256
NP = 512


def _partner(j):
    j = np.asarray(j)
    return np.where((j % 32) < 16, j + 16, j - 16)


def host_constants():
    c = {}
    c["ident"] = np.eye(128, dtype=np.float32)
    c["ones"] = np.ones((128, 128), np.float32)
    s = np.arange(128)[:, None]
    t = np.arange(128)[None, :]
    c["masku"] = (s <= t).astype(np.float32)
    c["maskl"] = (s >= t).astype(np.float32)
    rows = L // 64
    row = np.repeat(np.arange(rows, dtype=np.float32), 64)
    col = np.tile(np.arange(64, dtype=np.float32), rows)
    inv = (np.float32(10000.0) ** (-np.arange(16, dtype=np.float32) / np.float32(16))).astype(np.float32)
    ar = (row[None, :] * inv[:, None]).astype(np.float32)
    ac = (col[None, :] * inv[:, None]).astype(np.float32)
    cos = np.concatenate([np.cos(ar), np.cos(ar), np.cos(ac), np.cos(ac)], 0).astype(np.float32)
    sin = np.concatenate([-np.sin(ar), np.sin(ar), -np.sin(ac), np.sin(ac)], 0).astype(np.float32)
    c["rcos"] = cos
    c["rsin"] = sin
    rm = np.ones((64, 512), np.float32)
    rm[:, ::32] = 0.0
    c["rmask"] = rm
    return c


class K:
    pass


def build_nc(stop=None, debug=False):
    nc = bass.Bass("TRN2", target_bir_lowering=False)
    k = K()

    def din(name, shape, dt=F32):
        return nc.dram_tensor(name, list(shape), dt, kind="ExternalInput").ap()

    x_in = din("x", [L, D])
    ctx_in = din("ctx", [LC, D])
    cvec = din("cvec", [128, KC, 2])
    w_ada = din("w_ada", [DEPTH, D, 9 * D])
    b_ada = din("b_ada", [DEPTH, 128, 72])
    g_pre = din("g_pre", [DEPTH, 128, 3, 8])
    g_post = din("g_post", [DEPTH, 128, 3, 8])
    ffn_w1 = din("ffn_w1", [DEPTH, 2, D, DFF])
    ffn_w3 = din("ffn_w3", [DEPTH, 2, D, DFF])
    ffn_w2 = din("ffn_w2", [DEPTH, 2, DFF, D])
    w_in = din("w_in", [DEPTH, D, DIN])
    w_sw = din("w_sw", [DEPTH, D, 640])
    att_sink = din("att_sink", [DEPTH, 1, 8])
    ml_conv = din("ml_conv", [DEPTH, 512, 3])
    ml_fb = din("ml_fb", [DEPTH, 1, 8])
    ml_norm = din("ml_norm", [DEPTH, 1, 512])
    hg_lbl = din("hg_lbl", [64, 2, 4])
    hg_norm = din("hg_norm", [DEPTH, 1, 512])
    wb_att = din("wb_att", [DEPTH, 512, D])
    wb_ml = din("wb_ml", [DEPTH, 512, D])
    wb_hg = din("wb_hg", [DEPTH, 512, D])
    w_out = din("w_out", [DEPTH, D, D])
    c_ident = din("c_ident", [128, 128])
    c_ones = din("c_ones", [128, 128])
    c_masku = din("c_masku", [128, 128])
    c_maskl = din("c_maskl", [128, 128])
    c_rcos = din("c_rcos", [64, L])
    c_rsin = din("c_rsin", [64, L])
    c_rmask = din("c_rmask", [64, 512])

    out_d = nc.dram_tensor("out", [L, D], F32, kind="ExternalOutput").ap()
    skind = "ExternalOutput" if debug else "Internal"
    xTd = nc.dram_tensor("xTd", [KC, 128, T], F32, kind=skind).ap()
    h2d = nc.dram_tensor("h2d", [KC, 128, T], BF16, kind="Internal").ap()
    attTd = nc.dram_tensor("attTd", [512, T], BF16, kind=skind).ap()
    mlTd = nc.dram_tensor("mlTd", [512, T], BF16, kind=skind).ap()
    hgTd = nc.dram_tensor("hgTd", [512, T], BF16, kind=skind).ap()
    dbg = nc.dram_tensor("dbg", [128, 1024], F32, kind="ExternalOutput").ap() if debug else None
    w_inb = nc.dram_tensor("w_inb", [DEPTH, D, DIN], BF16, kind="Internal").ap()
    w_swb = nc.dram_tensor("w_swb", [DEPTH, D, 640], BF16, kind="Internal").ap()
    wb_attb = nc.dram_tensor("wb_attb", [DEPTH, 512, D], BF16, kind="Internal").ap()
    wb_mlb = nc.dram_tensor("wb_mlb", [DEPTH, 512, D], BF16, kind="Internal").ap()
    wb_hgb = nc.dram_tensor("wb_hgb", [DEPTH, 512, D], BF16, kind="Internal").ap()
    w_outb = nc.dram_tensor("w_outb", [DEPTH, D, D], BF16, kind="Internal").ap()

    with ExitStack() as es:
        fw = FW(nc, es)
        op, dma = fw.op, fw.dma

        ident_f, ident_fr = fw.sbuf("ident_f", [128, 128], F32)
        ident_b, ident_br = fw.sbuf("ident_b", [128, 128], BF16)
        ones_f, ones_fr = fw.sbuf("ones_f", [128, 128], F32)
        ones_b, ones_br = fw.sbuf("ones_b", [128, 128], BF16)
        masku_f, masku_fr = fw.sbuf("masku_f", [128, 128], F32)
        maskl_f, maskl_fr = fw.sbuf("maskl_f", [128, 128], F32)
        masku_b, masku_br = fw.sbuf("masku_b", [128, 128], BF16)
        maskl_b, maskl_br = fw.sbuf("maskl_b", [128, 128], BF16)
        dma("sync", ident_f[:], c_ident, sb=ident_fr, w=[ident_fr])
        op("vector", lambda e: e.tensor_copy(ident_b[:], ident_f[:]), r=[ident_fr], w=[ident_br])
        dma("sync", ones_f[:], c_ones, sb=ones_fr, w=[ones_fr])
        op("vector", lambda e: e.tensor_copy(ones_b[:], ones_f[:]), r=[ones_fr], w=[ones_br])
        dma("sync", masku_f[:], c_masku, sb=masku_fr, w=[masku_fr])
        dma("sync", maskl_f[:], c_maskl, sb=maskl_fr, w=[maskl_fr])
        op("vector", lambda e: e.tensor_copy(masku_b[:], masku_f[:]), r=[masku_fr], w=[masku_br])
        op("vector", lambda e: e.tensor_copy(maskl_b[:], maskl_f[:]), r=[maskl_fr], w=[maskl_br])
        cstg = []
        cst_state = {"i": 0}

        def set_staging(tiles):
            cstg[:] = tiles

        wcast_reg = [fw.reg("wcast%d" % i) for i in range(DEPTH)]

        def precast_units(l):
            u = []
            for kk in range(KC):
                rows = slice(kk * 128, (kk + 1) * 128)
                for c0 in range(0, DIN, 1024):
                    n = min(1024, DIN - c0)
                    u.append((w_inb[l, rows, c0:c0 + n], w_in[l, rows, c0:c0 + n], n))
                u.append((w_swb[l, rows, :], w_sw[l, rows, :], 640))
                u.append((w_outb[l, rows, :], w_out[l, rows, :], D))
            for kk in range(4):
                rows = slice(kk * 128, (kk + 1) * 128)
                u.append((wb_attb[l, rows, :], wb_att[l, rows, :], D))
                u.append((wb_mlb[l, rows, :], wb_ml[l, rows, :], D))
                u.append((wb_hgb[l, rows, :], wb_hg[l, rows, :], D))
            return u

        pc_state = {"i": 0}

        def precast_emit(l, units, fslots, bslots):
            for (dst, src, n) in units:
                i = pc_state["i"]
                pc_state["i"] += 1
                fs, fsr = fslots[i % len(fslots)]
                bs, bsr = bslots[i % len(bslots)]
                dma("sync", fs[:, 0:n], src, sb=fsr, w=[fsr])
                eng = ("vector", "gpsimd", "scalar")[i % 3]
                if eng == "scalar":
                    op("scalar", lambda e: e.copy(bs[:, 0:n], fs[:, 0:n]), r=[fsr], w=[bsr])
                else:
                    op(eng, lambda e: e.tensor_copy(bs[:, 0:n], fs[:, 0:n]), r=[fsr], w=[bsr])
                dma("sync", dst, bs[:, 0:n], sb=bsr, r=[bsr], w=[wcast_reg[l]], piece=True)

        def alloc_hps(ntok):
            hps_ = [fw.sbuf("hp%d" % i, [128, KC, ntok], BF16) for i in range(2)]
            cstg[:] = [(t[:].rearrange("p a b -> p (a b)").bitcast(F32)[:, 0:1024], r_) for (t, r_) in hps_]
            return hps_

        def alloc_staging(n=3):
            cstg[:] = [(t[:, :], r_) for (t, r_) in [fw.sbuf("cstg%d" % i, [128, 1024], F32) for i in range(n)]]

        def castload(dst_ap, src_ap, dstr, ncols):
            i = cst_state["i"]
            cst_state["i"] += 1
            st, str_ = cstg[i % len(cstg)]
            dma("sync", st[:, 0:ncols], src_ap, sb=str_, w=[str_])
            eng = ("vector", "gpsimd", "scalar")[i % 3]
            if eng == "scalar":
                op("scalar", lambda e: e.copy(dst_ap, st[:, 0:ncols]), r=[str_], w=[dstr], piece=True)
            else:
                op(eng, lambda e: e.tensor_copy(dst_ap, st[:, 0:ncols]), r=[str_], w=[dstr], piece=True)
        CONST = [ident_fr, ident_br, ones_fr, ones_br, masku_fr, maskl_fr, masku_br, maskl_br]

        sc, scr = fw.sbuf("sc", [128, KC, 2], F32)
        dma("sync", sc[:], cvec, sb=scr, w=[scr])
        op("scalar", lambda e: e.activation(sc[:], sc[:], AF.Silu), r=[scr], w=[scr])
        modT, modr = fw.sbuf("modT", [128, 72, 2], F32)
        gpre, gprer = fw.sbuf("gpre", [128, 3, 8], F32)
        gpost, gpostr = fw.sbuf("gpost", [128, 3, 8], F32)
        Asc, Ascr = fw.sbuf("Asc", [128, 3, 8, 2], F32)
        Gsc, Gscr = fw.sbuf("Gsc", [128, 3, 8, 2], F32)
        mtmp, mtmpr = fw.sbuf("mtmp", [128, 8, 2], F32)
        bada, badar = fw.sbuf("bada", [128, 72], F32)
        modv = modT[:].rearrange("p (j c) w -> p j c w", c=8)

        PS = [fw.psum("ps%d" % i, [128, 512], F32) for i in range(7)]
        psB, psBr = fw.psum("psB", [128, 1024], BF16)

        state = {"phase": 0}

        class Phase:
            def __enter__(self_p):
                self_p.sb0 = getattr(fw, "sb_used", 0)
                self_p.es2 = ExitStack()
                self_p.es2.__enter__()
                self_p.old = fw.es
                fw.es = self_p.es2
                return self_p

            def __exit__(self_p, *a):
                fw.barrier()
                fw.release_dregs()
                fw.es = self_p.old
                fw.sb_used = self_p.sb0
                self_p.es2.__exit__(None, None, None)
                return False

        def mod_phase(l):
            with Phase():
                wA = [fw.sbuf("wA%d" % i, [128, KC, 1152], F32) for i in range(2)]
                dma("sync", bada[:], b_ada[l], sb=badar, w=[badar])
                dma("sync", gpre[:], g_pre[l], sb=gprer, w=[gprer])
                dma("sync", gpost[:], g_post[l], sb=gpostr, w=[gpostr])
                pm, pmr = PS[0]
                for grp in range(8):
                    wt, wr = wA[grp % 2]
                    for kk in range(KC):
                        dma("sync", wt[:, kk, :], w_ada[l, kk * 128:(kk + 1) * 128, grp * 1152:(grp + 1) * 1152],
                            sb=wr, w=[wr], piece=True)
                    for mm in range(9):
                        m = grp * 9 + mm
                        for kk in range(KC):
                            op("tensor", lambda e: e.matmul(pm[:, 2 * m:2 * m + 2], wt[:, kk, mm * 128:(mm + 1) * 128],
                                                            sc[:, kk, :], start=(kk == 0), stop=(kk == KC - 1)),
                               r=[wr, scr], w=[pmr], sig=(kk == KC - 1))
                op("vector", lambda e: e.tensor_tensor(modT[:], pm[:, 0:144].rearrange("p (m w) -> p m w", w=2),
                                                       bada[:].unsqueeze(2).to_broadcast([128, 72, 2]), ALU.add),
                   r=[pmr, badar], w=[modr])
                for j in range(3):
                    wgt = 1.0 if j == 1 else 0.5
                    op("vector", lambda e: e.tensor_scalar(mtmp[:], modv[:, 3 * j + 1], 1.0, None, ALU.add), r=[modr], w=[mtmpr])
                    op("vector", lambda e: e.tensor_tensor(Asc[:, j], mtmp[:], gpre[:, j, :].unsqueeze(2).to_broadcast([128, 8, 2]), ALU.mult),
                       r=[mtmpr, gprer], w=[Ascr])
                    op("vector", lambda e: e.scalar_tensor_tensor(Gsc[:, j], modv[:, 3 * j + 2], wgt,
                                                                  gpost[:, j, :].unsqueeze(2).to_broadcast([128, 8, 2]), ALU.mult, ALU.mult),
                       r=[modr, gpostr], w=[Gscr])
                if l == 0:
                    fsl = [fw.sbuf("pcf%d" % i, [128, 1024], F32) for i in range(2)]
                    bsl = [fw.sbuf("pcb%d" % i, [128, 1024], BF16) for i in range(2)]
                    precast_emit(0, precast_units(0), [(t_[:, :], r_) for t_, r_ in fsl], [(t_[:, :], r_) for t_, r_ in bsl])

        def rstd_from_sq(sq, sqr, rs, rsr, N, pst):
            ps, psr = pst
            for cc in range(KC):
                op("tensor", lambda e: e.matmul(ps[:, :N], ones_b[:], sq[:, cc, :N], start=(cc == 0), stop=(cc == KC - 1)),
                   r=[sqr, ones_br], w=[psr], sig=(cc == KC - 1))
            op("scalar", lambda e: e.activation(rs[:, :N], ps[:, :N], AF.Sqrt, bias=epsb[:, 0:1], scale=1.0 / D), r=[psr, epsr], w=[rsr])
            op("vector", lambda e: e.reciprocal(rs[:, :N], rs[:, :N]), r=[rsr], w=[rsr])

        epsb, epsr = fw.sbuf("epsb", [128, 1], F32)
        op("vector", lambda e: e.memset(epsb[:], EPS), w=[epsr])

        def load_tile_tokmajor(src, tok0, N, xT, xTr, tm, tmr):
            for blk in range(N // 128):
                dma("sync", tm, src[tok0 + blk * 128: tok0 + (blk + 1) * 128, :], sb=tmr, w=[tmr])
                for half in range(2):
                    ps, psr = PS[4 + half]
                    for q in range(4):
                        cc = half * 4 + q
                        op("tensor", lambda e: e.transpose(ps[:, q * 128:(q + 1) * 128], tm[:, cc * 128:(cc + 1) * 128], ident_f[:]),
                           r=[tmr, ident_fr], w=[psr], sig=(q == 3))
                    eng = "scalar" if half == 0 else "vector"
                    src_v = ps[:, :].rearrange("p (q t) -> p q t", q=4)
                    dst_v = xT[:, half * 4:(half + 1) * 4, blk * 128:(blk + 1) * 128]
                    if eng == "scalar":
                        op("scalar", lambda e: e.copy(dst_v, src_v), r=[psr], w=[xTr])
                    else:
                        op("vector", lambda e: e.tensor_copy(dst_v, src_v), r=[psr], w=[xTr])

        def store_tile_tokmajor(dst, tok0, N, xT, xTr, tm, tmr):
            for blk in range(N // 128):
                for half in range(2):
                    ps, psr = PS[4 + half]
                    for q in range(4):
                        cc = half * 4 + q
                        op("tensor", lambda e: e.transpose(ps[:, q * 128:(q + 1) * 128], xT[:, cc, blk * 128:(blk + 1) * 128], ident_f[:]),
                           r=[xTr, ident_fr], w=[psr], sig=(q == 3))
                    if half == 0:
                        op("scalar", lambda e: e.copy(tm[:, 0:512], ps[:, :]), r=[psr], w=[tmr])
                    else:
                        op("vector", lambda e: e.tensor_copy(tm[:, 512:1024], ps[:, :]), r=[psr], w=[tmr])
                dma("sync", dst[tok0 + blk * 128: tok0 + (blk + 1) * 128, :], tm, sb=tmr, r=[tmr])

        def tile_info(ti):
            if ti < L // NT:
                return ti * NT, 0
            return L, 1

        xTd_v = xTd.rearrange("c p t -> p c t")
        h2d_v = h2d.rearrange("c p t -> p c t")

        def pre_norm(j, wi, xT, xTr, sq, sqr, rs, rsr, tmp, tmpr, h, hr, N):
            op("scalar", lambda e: e.activation(sq[:, :, :N], xT[:, :, :N], AF.Square), r=[xTr], w=[sqr])
            rstd_from_sq(sq, sqr, rs, rsr, N, PS[6])
            for cc in range(KC):
                tt, ttr = tmp[cc % 2], tmpr[cc % 2]
                op("vector", lambda e: e.scalar_tensor_tensor(tt[:, :N], xT[:, cc, :N], Asc[:, j, cc, wi:wi + 1], rs[:, :N], ALU.mult, ALU.mult),
                   r=[xTr, Ascr, rsr], w=[ttr])
                op("scalar", lambda e: e.activation(h[:, cc, :N], tt[:, :N], AF.Identity, bias=modv[:, 3 * j, cc, wi:wi + 1], scale=1.0),
                   r=[ttr, modr], w=[hr])

        def post_norm_add(j, wi, ysb, ysbr, sq, sqr, rs, rsr, tmp, tmpr, xT, xTr, N):
            rstd_from_sq(sq, sqr, rs, rsr, N, PS[6])
            for m in range(KC):
                tt, ttr = tmp[m % 2], tmpr[m % 2]
                op("vector", lambda e: e.scalar_tensor_tensor(tt[:, :N], ysb[:, m, :N], Gsc[:, j, m, wi:wi + 1], rs[:, :N], ALU.mult, ALU.mult),
                   r=[ysbr, Gscr, rsr], w=[ttr])
                op("gpsimd", lambda e: e.tensor_tensor(xT[:, m, :N], tt[:, :N], xT[:, m, :N], ALU.add), r=[ttr, xTr], w=[xTr])

        def ffn_phase(l, which):
            j = 0 if which == 0 else 2
            first = (l == 0 and which == 0)
            final = (l == DEPTH - 1 and which == 1)
            with Phase():
                w1, w1r = fw.sbuf("w1", [128, KC, DFF], BF16)
                w3, w3r = fw.sbuf("w3", [128, KC, DFF], BF16)
                w2, w2r = fw.sbuf("w2", [128, FC, D], BF16)
                ysb, ysbr = fw.sbuf("ysb", [128, KC, NT], F32)
                xTs = [fw.sbuf("xT%d" % i, [128, KC, NT], F32) for i in range(2)]
                fuse_h2 = (which == 0)
                tm, tmr = (None, None)
                if fuse_h2:
                    h2t, h2tr = fw.sbuf("h2t", [128, KC, NT], BF16)
                    tm_t, tmr = fw.sbuf("tm", [128, D], F32)
                    tm = tm_t[:, :]
                    set_staging([(tm, tmr), (h2t[:].rearrange("p c t -> p (c t)").bitcast(F32)[:, 0:1024], h2tr)])
                else:
                    stg3 = [fw.sbuf("wstg%d" % i, [128, 1024], F32) for i in range(3)]
                    set_staging([(t_[:, :], r_) for (t_, r_) in stg3])
                    if final:
                        tm, tmr = stg3[0][0][:, :], stg3[0][1]
                w1rs = [fw.reg("w1b%d" % i) for i in range(3)]
                w3rs = [fw.reg("w3b%d" % i) for i in range(3)]
                w2rs = [fw.reg("w2b%d" % i) for i in range(3)]
                for bi, c0 in enumerate(range(0, DFF, 1024)):
                    c1 = min(DFF, c0 + 1024)
                    for kk in range(KC):
                        castload(w1[:, kk, c0:c1], ffn_w1[l, which, kk * 128:(kk + 1) * 128, c0:c1], w1rs[bi], c1 - c0)
                        castload(w3[:, kk, c0:c1], ffn_w3[l, which, kk * 128:(kk + 1) * 128, c0:c1], w3rs[bi], c1 - c0)
                for f in range(FC):
                    castload(w2[:, f, :], ffn_w2[l, which, f * 128:(f + 1) * 128, :], w2rs[f // 8], D)
                h, hr = fw.sbuf("h", [128, KC, NT], BF16)
                a, ar = fw.sbuf("a", [128, FC, NT], BF16)
                sqa = [fw.sbuf("sqa%d" % i, [128, NT], BF16) for i in range(2)]
                sqb = [fw.sbuf("sqb%d" % i, [128, NT], BF16) for i in range(2)]
                if fuse_h2:
                    sqc = [fw.sbuf("sqc%d" % i, [128, NT], BF16) for i in range(2)]
                    tmpc = [fw.sbuf("tmpc%d" % i, [128, NT], F32) for i in range(2)]
                    rs3, rs3r = fw.sbuf("rs3", [128, NT], F32)
                tmpa = [fw.sbuf("tmpa%d" % i, [128, NT], F32) for i in range(2)]
                tmpb = [fw.sbuf("tmpb%d" % i, [128, NT], F32) for i in range(2)]
                ss_ = [fw.sbuf("ss%d" % i, [128, NT], F32) for i in range(2)]
                rs1, rs1r = fw.sbuf("rs1", [128, NT], F32)
                rs2, rs2r = fw.sbuf("rs2", [128, NT], F32)
                tiles = []
                for ti in range(L // NT + 1):
                    tok0, wi = tile_info(ti)
                    if final and wi == 1:
                        continue
                    tiles.append((ti, tok0, wi))
                N = NT

                def issue_load(idx):
                    ti, tok0, wi = tiles[idx]
                    xT, xTr = xTs[idx % 2]
                    if first:
                        pass
                    else:
                        dma("sync", xT[:, :, :N], xTd_v[:, :, tok0:tok0 + N], sb=xTr, r=[xTd_reg[ti]], w=[xTr])

                def prep(idx):
                    ti, tok0, wi = tiles[idx]
                    xT, xTr = xTs[idx % 2]
                    if first:
                        load_tile_tokmajor(x_in if wi == 0 else ctx_in, tok0 if wi == 0 else 0, N, xT, xTr, tm, tmr)
                    ps, psr = PS[6]
                    for cc in range(KC):
                        sq_, sq_r = sqa[cc % 2]
                        op("scalar", lambda e: e.activation(sq_[:, :N], xT[:, cc, :N], AF.Square), r=[xTr], w=[sq_r])
                        op("tensor", lambda e: e.matmul(ps[:, :N], ones_b[:], sq_[:, :N], start=(cc == 0), stop=(cc == KC - 1)),
                           r=[sq_r, ones_br], w=[psr], sig=True)
                    op("scalar", lambda e: e.activation(rs1[:, :N], ps[:, :N], AF.Sqrt, bias=epsb[:, 0:1], scale=1.0 / D), r=[psr, epsr], w=[rs1r])
                    op("vector", lambda e: e.reciprocal(rs1[:, :N], rs1[:, :N]), r=[rs1r], w=[rs1r])
                    for cc in range(KC):
                        tt, ttr = tmpa[cc % 2]
                        op("vector", lambda e: e.scalar_tensor_tensor(tt[:, :N], xT[:, cc, :N], Asc[:, j, cc, wi:wi + 1], rs1[:, :N], ALU.mult, ALU.mult),
                           r=[xTr, Ascr, rs1r], w=[ttr])
                        op("scalar", lambda e: e.activation(h[:, cc, :N], tt[:, :N], AF.Identity, bias=modv[:, 3 * j, cc, wi:wi + 1], scale=1.0),
                           r=[ttr, modr], w=[hr])

                issue_load(0)
                if len(tiles) > 1:
                    issue_load(1)
                prep(0)
                for idx, (ti, tok0, wi) in enumerate(tiles):
                    xT, xTr = xTs[idx % 2]
                    for f in range(FC):
                        pu, pur = PS[f % 2]
                        pv, pvr = PS[2 + f % 2]
                        s_, s_r = ss_[f % 2]
                        for kk in range(KC):
                            op("tensor", lambda e: e.matmul(pu[:, :N], w1[:, kk, f * 128:(f + 1) * 128], h[:, kk, :N], start=(kk == 0), stop=(kk == KC - 1)),
                               r=[w1rs[f // 8], hr], w=[pur], sig=(kk == KC - 1))
                        for kk in range(KC):
                            op("tensor", lambda e: e.matmul(pv[:, :N], w3[:, kk, f * 128:(f + 1) * 128], h[:, kk, :N], start=(kk == 0), stop=(kk == KC - 1)),
                               r=[w3rs[f // 8], hr], w=[pvr], sig=(kk == KC - 1))
                        op("scalar", lambda e: e.activation(s_[:, :N], pu[:, :N], AF.Silu), r=[pur], w=[s_r])
                        op("vector", lambda e: e.tensor_tensor(a[:, f, :N], s_[:, :N], pv[:, :N], ALU.mult), r=[s_r, pvr], w=[ar])
                    if idx + 1 < len(tiles):
                        prep(idx + 1)
                    pst, pstr = PS[6]
                    for m in range(KC):
                        py, pyr = PS[4 + m % 2]
                        for f in range(FC):
                            op("tensor", lambda e: e.matmul(py[:, :N], w2[:, f, m * 128:(m + 1) * 128], a[:, f, :N], start=(f == 0), stop=(f == FC - 1)),
                               r=[w2rs[f // 8], ar], w=[pyr], sig=(f == FC - 1))
                        op("scalar", lambda e: e.copy(ysb[:, m, :N], py[:, :N]), r=[pyr], w=[ysbr])
                        sq_, sq_r = sqb[m % 2]
                        op("gpsimd", lambda e: e.tensor_tensor(sq_[:, :N], ysb[:, m, :N], ysb[:, m, :N], ALU.mult), r=[ysbr], w=[sq_r])
                        op("tensor", lambda e: e.matmul(pst[:, :N], ones_b[:], sq_[:, :N], start=(m == 0), stop=(m == KC - 1)),
                           r=[sq_r, ones_br], w=[pstr], sig=True)
                    op("scalar", lambda e: e.activation(rs2[:, :N], pst[:, :N], AF.Sqrt, bias=epsb[:, 0:1], scale=1.0 / D), r=[pstr, epsr], w=[rs2r])
                    op("vector", lambda e: e.reciprocal(rs2[:, :N], rs2[:, :N]), r=[rs2r], w=[rs2r])
                    for m in range(KC):
                        tt, ttr = tmpb[m % 2]
                        op("vector", lambda e: e.scalar_tensor_tensor(tt[:, :N], ysb[:, m, :N], Gsc[:, j, m, wi:wi + 1], rs2[:, :N], ALU.mult, ALU.mult),
                           r=[ysbr, Gscr, rs2r], w=[ttr])
                        op("gpsimd", lambda e: e.tensor_tensor(xT[:, m, :N], tt[:, :N], xT[:, m, :N], ALU.add), r=[ttr, xTr], w=[xTr])
                    if final:
                        store_tile_tokmajor(out_d, tok0, N, xT, xTr, tm, tmr)
                    else:
                        dma("sync", xTd_v[:, :, tok0:tok0 + N], xT[:, :, :N], sb=xTr, r=[xTr], w=[xTd_reg[ti]])
                    if fuse_h2:
                        ps3, ps3r = PS[6]
                        for cc in range(KC):
                            sq_, sq_r = sqc[cc % 2]
                            op("scalar", lambda e: e.activation(sq_[:, :N], xT[:, cc, :N], AF.Square), r=[xTr], w=[sq_r])
                            op("tensor", lambda e: e.matmul(ps3[:, :N], ones_b[:], sq_[:, :N], start=(cc == 0), stop=(cc == KC - 1)),
                               r=[sq_r, ones_br], w=[ps3r], sig=True)
                        op("scalar", lambda e: e.activation(rs3[:, :N], ps3[:, :N], AF.Sqrt, bias=epsb[:, 0:1], scale=1.0 / D), r=[ps3r, epsr], w=[rs3r])
                        op("vector", lambda e: e.reciprocal(rs3[:, :N], rs3[:, :N]), r=[rs3r], w=[rs3r])
                        for cc in range(KC):
                            tt, ttr = tmpc[cc % 2]
                            op("vector", lambda e: e.scalar_tensor_tensor(tt[:, :N], xT[:, cc, :N], Asc[:, 1, cc, wi:wi + 1], rs3[:, :N], ALU.mult, ALU.mult),
                               r=[xTr, Ascr, rs3r], w=[ttr])
                            op("scalar", lambda e: e.activation(h2t[:, cc, :N], tt[:, :N], AF.Identity, bias=modv[:, 3, cc, wi:wi + 1], scale=1.0),
                               r=[ttr, modr], w=[h2tr])
                        dma("sync", h2d_v[:, :, tok0:tok0 + N], h2t[:, :, :N], sb=h2tr, r=[h2tr])
                    if idx + 2 < len(tiles):
                        issue_load(idx + 2)
                    if which == 1 and l + 1 < DEPTH:
                        if idx == 0:
                            pc_units = precast_units(l + 1)
                            b2 = stg3[2][0][:, :].bitcast(BF16)
                            pc_f = [(stg3[0][0][:, :], stg3[0][1]), (stg3[1][0][:, :], stg3[1][1])]
                            pc_b = [(b2[:, 0:1024], stg3[2][1])]
                        per = (len(pc_units) + len(tiles) - 1) // len(tiles)
                        precast_emit(l + 1, pc_units[idx * per:(idx + 1) * per], pc_f, pc_b)

        xTd_reg = [fw.reg("xTd%d" % i) for i in range(L // NT + 1)]

        def h2_phase(l):
            with Phase():
                xTs = [fw.sbuf("xT%d" % i, [128, KC, NP], F32) for i in range(2)]
                hs = [fw.sbuf("h%d" % i, [128, KC, NP], BF16) for i in range(2)]
                sq, sqr = fw.sbuf("sq", [128, KC, NP], BF16)
                tmp_ = [fw.sbuf("tmp%d" % i, [128, NP], F32) for i in range(2)]
                tmp, tmpr = [t[0] for t in tmp_], [t[1] for t in tmp_]
                rs, rsr = fw.sbuf("rs", [128, NP], F32)
                for pi in range(9):
                    tok0, N, wi = (pi * NP, NP, 0) if pi < 8 else (L, LC, 1)
                    xT, xTr = xTs[pi % 2]
                    h, hr = hs[pi % 2]
                    dma("sync", xT[:, :, :N], xTd_v[:, :, tok0:tok0 + N], sb=xTr, w=[xTr])
                    pre_norm(1, wi, xT, xTr, sq, sqr, rs, rsr, tmp, tmpr, h, hr, N)
                    dma("sync", h2d_v[:, :, tok0:tok0 + N], h[:, :, :N], sb=hr, r=[hr])

        def pieces():
            for pi in range(9):
                yield (pi, pi * NP, NP) if pi < 8 else (pi, L, LC)

        def load_w(dst, dstr, src_cols_ap_fn, ncols):
            for kk in range(KC):
                dma("sync", dst[:, kk, :ncols], src_cols_ap_fn(kk), sb=dstr, r=wcast_reg, w=[dstr], piece=True)

        def load_w2(dst, dstr, src2d, ncols):
            dma("sync", dst[:, :, :ncols], src2d.rearrange("(k p) c -> p k c", p=128), sb=dstr, r=wcast_reg, w=[dstr])

        def fm(ps, psr, wt, wtr, c0, M, hp, hpr, N):
            for kk in range(KC):
                op("tensor", lambda e: e.matmul(ps[0:M, :N], wt[:, kk, c0:c0 + M], hp[:, kk, :N], start=(kk == 0), stop=(kk == KC - 1)),
                   r=[wtr, hpr], w=[psr], sig=(kk == KC - 1))

        def tmj(pso, psr, wt, wtr, c0, ncol, hp, hpr, t0, nt):
            for kk in range(KC):
                op("tensor", lambda e: e.matmul(pso, hp[:, kk, t0:t0 + nt], wt[:, kk, c0:c0 + ncol], start=(kk == 0), stop=(kk == KC - 1)),
                   r=[wtr, hpr], w=[psr], sig=(kk == KC - 1))

        def att_phase(l, g):
            with Phase():
                hps = alloc_hps(NP)
                wq, wqr = fw.sbuf("wq", [128, KC, 256], BF16)
                wqs, wqsr = fw.sbuf("wqs", [128, KC, 256], BF16)
                wk, wkr = fw.sbuf("wk", [128, KC, 64], BF16)
                wks, wksr = fw.sbuf("wks", [128, KC, 64], BF16)
                wv, wvr = fw.sbuf("wv", [128, KC, 64], BF16)
                load_w2(wq, wqr, w_inb[l][:, C_AQ + g * 256: C_AQ + (g + 1) * 256], 256)
                load_w2(wqs, wqsr, w_swb[l][:, g * 256:(g + 1) * 256], 256)
                load_w2(wk, wkr, w_inb[l][:, C_AK + g * 64: C_AK + (g + 1) * 64], 64)
                load_w2(wks, wksr, w_swb[l][:, 512 + g * 64: 512 + (g + 1) * 64], 64)
                load_w2(wv, wvr, w_inb[l][:, C_AV + g * 64: C_AV + (g + 1) * 64], 64)
                rcos, rcosr = fw.sbuf("rcos", [64, L], F32)
                rsin, rsinr = fw.sbuf("rsin", [64, L], F32)
                dma("sync", rcos[:], c_rcos, sb=rcosr, w=[rcosr])
                dma("sync", rsin[:], c_rsin, sb=rsinr, w=[rsinr])
                esk, eskr = fw.sbuf("esk", [64, 8], F32)
                dma("sync", esk[:], att_sink[l].partition_broadcast(64), sb=eskr, w=[eskr])
                op("scalar", lambda e: e.activation(esk[:], esk[:], AF.Exp), r=[eskr], w=[eskr])
                QT, _ = fw.sbuf("QT", [64, 4, T], BF16)
                KT, _ = fw.sbuf("KT", [64, T], BF16)
                Vt, _ = fw.sbuf("Vt", [128, T // 128, 64], BF16)
                QTr = [fw.reg("QT%d" % i) for i in range(9)]
                KTr = [fw.reg("KT%d" % i) for i in range(9)]
                Vtr = [fw.reg("Vt%d" % i) for i in range(9)]

                def pc_of(blk):
                    return 8 if blk >= 32 else blk // 4
                t1s = [fw.sbuf("t1_%d" % i, [64, NP], F32) for i in range(2)]
                t2s = [fw.sbuf("t2_%d" % i, [64, NP], F32) for i in range(2)]
                cnt = 0
                plist = list(pieces())
                plist = [plist[8]] + plist[:8]
                for pidx, (pi, tok0, N) in enumerate(plist):
                    hp, hpr = hps[pidx % 2]
                    dma("sync", hp[:, :, :N], h2d_v[:, :, tok0:tok0 + N], sb=hpr, w=[hpr])
                    for r_ in range(5):
                        wa, war, wb_, wbr_, c0 = (wq, wqr, wqs, wqsr, r_ * 64) if r_ < 4 else (wk, wkr, wks, wksr, 0)
                        dst = QT[:, r_, tok0:tok0 + N] if r_ < 4 else KT[:, tok0:tok0 + N]
                        dstr = QTr[pi] if r_ < 4 else KTr[pi]
                        p1, p1r = PS[cnt % 2]
                        p2, p2r = PS[2 + cnt % 2]
                        t1, t1r = t1s[cnt % 2]
                        t2, t2r = t2s[cnt % 2]
                        cnt += 1
                        fm(p1, p1r, wa, war, c0, 64, hp, hpr, N)
                        if tok0 < L:
                            fm(p2, p2r, wb_, wbr_, c0, 64, hp, hpr, N)
                            op("vector", lambda e: e.tensor_tensor(t1[:, :N], p1[0:64, :N], rcos[:, tok0:tok0 + N], ALU.mult), r=[p1r, rcosr], w=[t1r])
                            op("vector", lambda e: e.tensor_tensor(t2[:, :N], p2[0:64, :N], rsin[:, tok0:tok0 + N], ALU.mult), r=[p2r, rsinr], w=[t2r])
                            op("gpsimd", lambda e: e.tensor_tensor(dst, t1[:, :N], t2[:, :N], ALU.add), r=[t1r, t2r], w=[dstr])
                        else:
                            op("scalar", lambda e: e.copy(dst, p1[0:64, :N]), r=[p1r], w=[dstr])
                    for b_ in range(N // 128):
                        blk = tok0 // 128 + b_
                        pv, pvr = PS[4 + blk % 2]
                        tmj(pv[:, 0:64], pvr, wv, wvr, 0, 64, hp, hpr, b_ * 128, 128)
                        op("scalar", lambda e: e.copy(Vt[:, blk, :], pv[:, 0:64]), r=[pvr], w=[Vtr[pi]])
                Es = [fw.sbuf("E%d" % i, [128, 512], BF16) for i in range(3)]
                dtmp, dtmpr = fw.sbuf("dtmp", [64, 512], F32)
                osts = [fw.sbuf("ost%d" % i, [64, 4, 512], BF16) for i in range(2)]
                qblocks = list(range(32)) + ([32, 33] if l < DEPTH - 1 else [])
                ecnt = 0
                for qi, i in enumerate(qblocks):
                    if i < 32:
                        keys = [j for j in (i - 1, i, i + 1) if 0 <= j < 32] + [32, 33]
                    else:
                        keys = [32, 33]
                    pn, pnr = PS[3 + qi % 2]
                    pd, pdr = PS[5 + qi % 2]
                    ost, ostr = osts[(qi // 4) % 2]
                    for idx, j in enumerate(keys):
                        ps, psr = PS[ecnt % 3]
                        E, Er = Es[ecnt % 3]
                        ecnt += 1
                        op("tensor", lambda e: e.matmul(ps[:, :].rearrange("p (r q) -> p r q", r=4), KT[:, j * 128:(j + 1) * 128],
                                                        QT[:, :, i * 128:(i + 1) * 128], start=True, stop=True),
                           r=[KTr[pc_of(j)], QTr[pc_of(i)]], w=[psr])
                        op("scalar", lambda e: e.activation(E[:], ps[:], AF.Exp, scale=0.125), r=[psr], w=[Er])
                        if i < 32 and j == i - 1:
                            op("vector", lambda e: e.tensor_tensor(E[:].rearrange("p (r q) -> p r q", r=4), E[:].rearrange("p (r q) -> p r q", r=4),
                                                                   maskl_b[:].unsqueeze(1).to_broadcast([128, 4, 128]), ALU.mult), r=[Er, maskl_br], w=[Er])
                        if i < 32 and j == i + 1 and j < 32:
                            op("vector", lambda e: e.tensor_tensor(E[:].rearrange("p (r q) -> p r q", r=4), E[:].rearrange("p (r q) -> p r q", r=4),
                                                                   masku_b[:].unsqueeze(1).to_broadcast([128, 4, 128]), ALU.mult), r=[Er, masku_br], w=[Er])
                        last = (idx == len(keys) - 1)
                        op("tensor", lambda e: e.matmul(pn[0:64, :], Vt[:, j, :], E[:], start=(idx == 0), stop=last), r=[Vtr[pc_of(j)], Er], w=[pnr], sig=last)
                        op("tensor", lambda e: e.matmul(pd[0:64, :], ones_b[:, 0:64], E[:], start=(idx == 0), stop=last), r=[ones_br, Er], w=[pdr], sig=last)
                    op("vector", lambda e: e.tensor_tensor(dtmp[:].rearrange("p (r q) -> p r q", r=4), pd[0:64, :].rearrange("p (r q) -> p r q", r=4),
                                                           esk[:, g * 4:(g + 1) * 4].unsqueeze(2).to_broadcast([64, 4, 128]), ALU.add), r=[pdr, eskr], w=[dtmpr])
                    op("scalar", lambda e: e.activation(dtmp[:], dtmp[:], AF.Ln), r=[dtmpr], w=[dtmpr])
                    op("scalar", lambda e: e.activation(dtmp[:], dtmp[:], AF.Exp, scale=-1.0), r=[dtmpr], w=[dtmpr])
                    sl = (qi % 4) * 128
                    op("vector", lambda e: e.tensor_tensor(ost[:, :, sl:sl + 128], pn[0:64, :].rearrange("p (r q) -> p r q", r=4),
                                                           dtmp[:].rearrange("p (r q) -> p r q", r=4), ALU.mult), r=[pnr, dtmpr], w=[ostr])
                    endgrp = (qi % 4 == 3) or (qi == len(qblocks) - 1)
                    if endgrp:
                        nb = qi % 4 + 1
                        t0 = (i - nb + 1) * 128
                        dma("sync", attTd[g * 256:(g + 1) * 256, t0:t0 + nb * 128].rearrange("(r d) t -> d r t", d=64),
                            ost[:, :, 0:nb * 128], sb=ostr, r=[ostr])

        NB = T // 128
        gA, gAr = fw.sbuf("gA", [128, 2, NB, 4], F32)
        gEb, gEbr = fw.sbuf("gEb", [128, 2, NB, 4], F32)
        gEbL, gEbLr = fw.sbuf("gEbL", [128, 2, NB, 4], F32)
        gW, gWr = fw.sbuf("gW", [128, 2, NB, 4], F32)
        ln8, ln8r = fw.sbuf("ln8", [128, 1], F32)
        op("vector", lambda e: e.memset(ln8[:], float(np.log(0.125))), w=[ln8r])
        oneb, onebr = fw.sbuf("oneb", [128, 1], F32)
        op("vector", lambda e: e.memset(oneb[:], 1.0), w=[onebr])

        def gates_phase(l):
            with Phase():
                hps = alloc_hps(NP)
                wg, wgr = fw.sbuf("wg", [128, KC, 16], BF16)
                load_w2(wg, wgr, w_inb[l][:, C_MI:C_MI + 16], 16)
                fb, fbr = fw.sbuf("fb", [128, 8], F32)
                dma("sync", fb[:], ml_fb[l].partition_broadcast(128), sb=fbr, w=[fbr])
                Gt, Gtr = fw.sbuf("Gt", [128, NB, 16], F32)
                for pi, tok0, N in pieces():
                    hp, hpr = hps[pi % 2]
                    dma("sync", hp[:, :, :N], h2d_v[:, :, tok0:tok0 + N], sb=hpr, w=[hpr])
                    pg, pgr = PS[pi % 2]
                    nb_ = N // 128
                    for b_ in range(nb_):
                        tmj(pg[:, b_ * 16:(b_ + 1) * 16], pgr, wg, wgr, 0, 16, hp, hpr, b_ * 128, 128)
                    blk0 = tok0 // 128
                    op("scalar", lambda e: e.copy(Gt[:, blk0:blk0 + nb_, :], pg[:, 0:nb_ * 16].rearrange("p (b c) -> p b c", c=16)), r=[pgr], w=[Gtr])
                zf, zfr = fw.sbuf("zf", [128, NB, 8], F32)
                spd = [fw.sbuf("spd%d" % i, [128, NB, 4], F32) for i in range(2)]
                tg, tgr = fw.sbuf("tg", [128, NB, 4], F32)
                op("vector", lambda e: e.tensor_tensor(zf[:], Gt[:, :, 8:16], fb[:].unsqueeze(1).to_broadcast([128, NB, 8]), ALU.add), r=[Gtr, fbr], w=[zfr])
                op("scalar", lambda e: e.activation(zf[:], zf[:], AF.Exp, scale=-1.0), r=[zfr], w=[zfr])
                op("scalar", lambda e: e.activation(zf[:], zf[:], AF.Ln, bias=oneb[:, 0:1], scale=1.0), r=[zfr, onebr], w=[zfr])
                for dd in range(2):
                    sp_, spr = spd[dd]
                    op("vector", lambda e: e.tensor_copy(sp_[:], zf[:, :, dd * 4:(dd + 1) * 4]), r=[zfr], w=[spr])
                    pc, pcr = PS[2 + dd]
                    msk, mskr = (masku_f, masku_fr) if dd == 0 else (maskl_f, maskl_fr)
                    spf = sp_[:].rearrange("p b c -> p (b c)")
                    W4 = NB * 4
                    op("tensor", lambda e: e.matmul(pc[:, 0:W4], msk[:], spf, start=True, stop=True), r=[mskr, spr], w=[pcr])
                    op("tensor", lambda e: e.matmul(pc[:, W4:2 * W4], ones_f[:], spf, start=True, stop=True), r=[ones_fr, spr], w=[pcr])
                    nbv = pc[:, 0:W4].rearrange("p (b c) -> p b c", c=4)
                    nbLv = pc[:, W4:2 * W4].rearrange("p (b c) -> p b c", c=4)
                    op("vector", lambda e: e.tensor_tensor(tg[:], Gt[:, :, dd * 4:(dd + 1) * 4], nbv, ALU.add), r=[Gtr, pcr], w=[tgr])
                    op("scalar", lambda e: e.activation(gA[:, dd], tg[:], AF.Exp, bias=ln8[:, 0:1], scale=1.0), r=[tgr, ln8r], w=[gAr])
                    op("scalar", lambda e: e.activation(gEb[:, dd], nbv, AF.Exp, scale=-1.0), r=[pcr], w=[gEbr])
                    op("scalar", lambda e: e.activation(gEbL[:, dd], nbLv, AF.Exp, scale=-1.0), r=[pcr], w=[gEbLr])
                    op("vector", lambda e: e.tensor_tensor(gW[:, dd], gA[:, dd], gEbL[:, dd], ALU.mult), r=[gAr, gEbLr], w=[gWr])


        def rsqrt_small(buf, bufr, n, scale):
            op("scalar", lambda e: e.activation(buf, buf, AF.Sqrt, bias=epsb[0:n, 0:1], scale=scale), r=[bufr, epsr], w=[bufr])
            op("vector", lambda e: e.reciprocal(buf, buf), r=[bufr], w=[bufr])

        def mlstm_phase(l, hh):
            with Phase():
                hps = alloc_hps(NP)
                wmq, wmqr = fw.sbuf("wmq", [128, KC, 64], BF16)
                wmk, wmkr = fw.sbuf("wmk", [128, KC, 64], BF16)
                wmv, wmvr = fw.sbuf("wmv", [128, KC, 128], BF16)
                wmo, wmor = fw.sbuf("wmo", [128, KC, 128], BF16)
                load_w2(wmq, wmqr, w_inb[l][:, C_MQ + hh * 64:C_MQ + (hh + 1) * 64], 64)
                load_w2(wmk, wmkr, w_inb[l][:, C_MK + hh * 64:C_MK + (hh + 1) * 64], 64)
                load_w2(wmv, wmvr, w_inb[l][:, C_MV + hh * 128:C_MV + (hh + 1) * 128], 128)
                load_w2(wmo, wmor, w_inb[l][:, C_MO + hh * 128:C_MO + (hh + 1) * 128], 128)
                gml, gmlr = fw.sbuf("gml", [128, 128], F32)
                dma("sync", gml[:], ml_norm[l, :, hh * 128:(hh + 1) * 128].partition_broadcast(128), sb=gmlr, w=[gmlr])
                cw, cwr = fw.sbuf("cw", [64, 2, 3], F32)
                dma("sync", cw[:, 0, :], ml_conv[l, hh * 64:(hh + 1) * 64, :], sb=cwr, w=[cwr], piece=True)
                dma("sync", cw[:, 1, :], ml_conv[l, 256 + hh * 64:256 + (hh + 1) * 64, :], sb=cwr, w=[cwr], piece=True)
                PW = T + 4
                pre = [fw.sbuf("pre%d" % i, [64, PW], F32) for i in range(2)]
                QK = [fw.sbuf("QK%d" % i, [64, T], BF16) for i in range(2)]
                Vx, _ = fw.sbuf("Vx", [128, NB, 129], BF16)
                Vxp = [fw.reg("Vx%d" % i) for i in range(9)]
                Vx1r = fw.reg("Vx1")
                og, ogr = fw.sbuf("og", [128, NB, 128], F32)
                prr = [[fw.reg("pre%d_%d" % (z, i)) for i in range(9)] for z in range(2)]
                padr = [fw.reg("pad%d" % z) for z in range(2)]
                QKr = [[fw.reg("QK%d_%d" % (z, i)) for i in range(5)] for z in range(2)]
                accs = [fw.sbuf("acc%d" % i, [64, 1024], F32) for i in range(2)]
                for z in range(2):
                    pz, _pz = pre[z]
                    for c_ in (0, L + 1, L + 2, PW - 1):
                        op("vector", lambda e: e.memset(pz[:, c_:c_ + 1], 0.0), w=[padr[z]], piece=True)
                op("vector", lambda e: e.memset(Vx[:, :, 128:129], 1.0), w=[Vx1r])

                def pcb(blk):
                    return 8 if blk >= 32 else blk // 4

                def ckb(blk):
                    return 4 if blk >= 32 else blk // 8

                def conv_chunk(c):
                    s0, d0, n = [(0, 0, 1024), (1024, 1024, 1024), (2048, 2048, 1024), (3072, 3072, 1024), (L + 2, L, LC)][c]
                    for z in range(2):
                        pz, _pz = pre[z]
                        qk, _qk = QK[z]
                        acc, accr = accs[z]
                        if c < 3:
                            rr = [prr[z][2 * c], prr[z][2 * c + 1], prr[z][2 * c + 2], padr[z]]
                        elif c == 3:
                            rr = [prr[z][6], prr[z][7], padr[z]]
                        else:
                            rr = [prr[z][8], padr[z]]
                        op("vector", lambda e: e.tensor_scalar(acc[:, :n], pz[:, s0:s0 + n], cw[:, z, 0:1], None, ALU.mult), r=rr + [cwr], w=[accr])
                        op("vector", lambda e: e.scalar_tensor_tensor(acc[:, :n], pz[:, s0 + 1:s0 + 1 + n], cw[:, z, 1:2], acc[:, :n], ALU.mult, ALU.add), r=rr + [cwr, accr], w=[accr])
                        op("vector", lambda e: e.scalar_tensor_tensor(acc[:, :n], pz[:, s0 + 2:s0 + 2 + n], cw[:, z, 2:3], acc[:, :n], ALU.mult, ALU.add), r=rr + [cwr, accr], w=[accr])
                        op("scalar", lambda e: e.activation(qk[:, d0:d0 + n], acc[:, :n], AF.Silu), r=[accr], w=[QKr[z][c]])

                def poff(tok0):
                    return 1 + tok0 if tok0 < L else L + 3 + (tok0 - L)

                for pi, tok0, N in pieces():
                    hp, hpr = hps[pi % 2]
                    dma("sync", hp[:, :, :N], h2d_v[:, :, tok0:tok0 + N], sb=hpr, w=[hpr])
                    for z, (wt, wtr) in enumerate(((wmq, wmqr), (wmk, wmkr))):
                        pp, ppr = PS[z]
                        fm(pp, ppr, wt, wtr, 0, 64, hp, hpr, N)
                        pz, _pz = pre[z]
                        o_ = poff(tok0)
                        op("scalar", lambda e: e.copy(pz[:, o_:o_ + N], pp[0:64, :N]), r=[ppr], w=[prr[z][pi]])
                    for b_ in range(N // 128):
                        blk = tok0 // 128 + b_
                        pv, pvr = PS[2 + blk % 2]
                        tmj(pv[:, 0:128], pvr, wmv, wmvr, 0, 128, hp, hpr, b_ * 128, 128)
                        op("vector", lambda e: e.tensor_copy(Vx[:, blk, 0:128], pv[:, 0:128]), r=[pvr], w=[Vxp[pi]], piece=True)
                        po, por = PS[4 + blk % 2]
                        tmj(po[:, 0:128], por, wmo, wmor, 0, 128, hp, hpr, b_ * 128, 128)
                        op("scalar", lambda e: e.activation(og[:, blk, :], po[:, 0:128], AF.Sigmoid), r=[por], w=[ogr])
                    if pi in (2, 4, 6):
                        conv_chunk(pi // 2 - 1)
                    elif pi == 7:
                        conv_chunk(3)
                    elif pi == 8:
                        conv_chunk(4)
                QT, _qt = QK[0]
                KT, _kt = QK[1]
                QTr, KTr = QKr[0], QKr[1]
                Kt, Ktr = fw.sbuf("Kt", [128, NB, 64], BF16)
                for b0 in range(0, NB, 16):
                    nb_ = min(16, NB - b0)
                    for b_ in range(nb_):
                        blk = b0 + b_
                        op("tensor", lambda e: e.transpose(psB[:, b_ * 64:(b_ + 1) * 64], KT[:, blk * 128:(blk + 1) * 128], ident_b[0:64, 0:64]),
                           r=[KTr[ckb(blk)], ident_br], w=[psBr], sig=(b_ == nb_ - 1))
                    op("scalar", lambda e: e.copy(Kt[:, b0:b0 + nb_, :], psB[:, 0:nb_ * 64].rearrange("p (b c) -> p b c", c=64)), r=[psBr], w=[Ktr])
                Kw, Kwr = fw.sbuf("Kw", [128, 2, NB, 64], BF16)
                for dd in range(2):
                    op("vector", lambda e: e.tensor_tensor(Kw[:, dd], Kt[:], gW[:, dd, :, hh:hh + 1].to_broadcast([128, NB, 64]), ALU.mult), r=[Ktr, gWr], w=[Kwr])
                hraw, hrawr = fw.sbuf("hraw", [128, 2, NB, 129], F32)
                hregs = [fw.reg("hraw%d" % i) for i in range(2)]
                SpA = [fw.sbuf("SpA%d" % i, [128, NB, 128], BF16) for i in range(2)]
                stmp = [fw.sbuf("stmp%d" % i, [128, 4, 128], F32) for i in range(2)]
                bk = 0
                for b0 in range(0, NB, 4):
                    nb_ = min(4, NB - b0)
                    pst, pstr = PS[bk % 2]
                    bk += 1
                    for b_ in range(nb_):
                        blk = b0 + b_
                        op("tensor", lambda e: e.matmul(pst[:, b_ * 128:(b_ + 1) * 128], KT[:, blk * 128:(blk + 1) * 128], QT[:, blk * 128:(blk + 1) * 128], start=True, stop=True),
                           r=[KTr[ckb(blk)], QTr[ckb(blk)]], w=[pstr], sig=(b_ == nb_ - 1))
                    for dd in range(2):
                        msk, mskr = (masku_f, masku_fr) if dd == 0 else (maskl_f, maskl_fr)
                        st_, st_r = stmp[dd]
                        op("vector", lambda e: e.tensor_tensor(st_[:, 0:nb_, :], pst[:, 0:nb_ * 128].rearrange("p (b t) -> p b t", t=128),
                                                               gA[:, dd, b0:b0 + nb_, hh:hh + 1].to_broadcast([128, nb_, 128]), ALU.mult), r=[pstr, gAr], w=[st_r])
                        op("gpsimd", lambda e: e.tensor_tensor(SpA[dd][0][:, b0:b0 + nb_, :], st_[:, 0:nb_, :], msk[:].unsqueeze(1).to_broadcast([128, nb_, 128]), ALU.mult),
                           r=[st_r, mskr], w=[SpA[dd][1]])
                for dd in range(2):
                    for b0 in range(0, NB, 3):
                        nb_ = min(3, NB - b0)
                        pso, psor = PS[2 + bk % 2]
                        bk += 1
                        for b_ in range(nb_):
                            blk = b0 + b_
                            op("tensor", lambda e: e.matmul(pso[:, b_ * 129:(b_ + 1) * 129], SpA[dd][0][:, blk, :], Vx[:, blk, :], start=True, stop=True),
                               r=[SpA[dd][1], Vxp[pcb(blk)], Vx1r], w=[psor], sig=(b_ == nb_ - 1))
                        op("scalar", lambda e: e.copy(hraw[:, dd, b0:b0 + nb_, :], pso[:, 0:nb_ * 129].rearrange("p (b v) -> p b v", v=129)), r=[psor], w=[hregs[dd]])
                Csts = [fw.sbuf("Cst%d" % i, [64, 129], F32) for i in range(2)]
                Cbf = [[fw.sbuf("Cbf%d_%d" % (i, j), [64, 129], BF16) for j in range(2)] for i in range(2)]
                for dd in range(2):
                    op("vector", lambda e: e.memset(Csts[dd][0][:], 0.0), w=[Csts[dd][1]])
                    for j in range(2):
                        op("gpsimd", lambda e: e.memset(Cbf[dd][j][0][:], 0.0), w=[Cbf[dd][j][1]])
                lat_f = [list(range(b, min(b + 3, 32))) for b in range(0, 32, 3)]
                groups = [[[32, 33]] + lat_f, [[33, 32]] + [list(reversed(g_)) for g_ in reversed(lat_f)]]
                PS7 = (psB[:, :].bitcast(F32), psBr)
                pibanks = [[PS[4], PS[6]], [PS[5], PS7]]
                for dd in range(2):
                    assert sum(len(g_) for g_ in groups[dd]) == NB
                j = [0, 0]
                for gi in range(len(groups[0])):
                    for dd in range(2):
                        grp = groups[dd][gi]
                        bmin = min(grp)
                        pin, pinr = pibanks[dd][gi % 2]
                        Cst, Cstr = Csts[dd]
                        for blk in grp:
                            jj = j[dd]
                            j[dd] += 1
                            psc, pscr = PS[(jj % 2) * 2 + dd]
                            op("tensor", lambda e: e.matmul(psc[0:64, 0:129], Kw[:, dd, blk, :], Vx[:, blk, :], start=True, stop=True), r=[Kwr, Vxp[pcb(blk)], Vx1r], w=[pscr])
                            cprev, cprevr = Cbf[dd][(jj + 1) % 2]
                            sl_ = (blk - bmin) * 129
                            op("tensor", lambda e: e.matmul(pin[:, sl_:sl_ + 129], QT[:, blk * 128:(blk + 1) * 128], cprev[:], start=True, stop=True), r=[QTr[ckb(blk)], cprevr], w=[pinr])
                            op("vector", lambda e: e.scalar_tensor_tensor(Cst[:], Cst[:], gEbL[0:64, dd, blk, hh:hh + 1], psc[0:64, 0:129], ALU.mult, ALU.add),
                               r=[Cstr, gEbLr, pscr], w=[Cstr])
                            cnext, cnextr = Cbf[dd][jj % 2]
                            op("scalar", lambda e: e.copy(cnext[:], Cst[:]), r=[Cstr], w=[cnextr])
                        n_ = len(grp)
                        op("vector", lambda e: e.tensor_tensor(hraw[:, dd, bmin:bmin + n_, :], hraw[:, dd, bmin:bmin + n_, :],
                                                               pin[:, 0:n_ * 129].rearrange("p (b v) -> p b v", v=129), ALU.add), r=[pinr, hregs[dd]], w=[hregs[dd]])
                op("vector", lambda e: e.tensor_copy(hraw[:, 0, 0, 0:1], hraw[:, 0, 0, 0:1]), r=[hregs[0], hregs[1]], w=[hrawr])
                dn, dnr = fw.sbuf("dn", [128, 2, NB], F32)
                for dd in range(2):
                    op("vector", lambda e: e.tensor_tensor(dn[:, dd, :], hraw[:, dd, :, 128], gEb[:, dd, :, hh], ALU.mult), r=[hrawr, gEbr], w=[dnr])
                op("scalar", lambda e: e.activation(dn[:], dn[:], AF.Abs), r=[dnr], w=[dnr])
                op("vector", lambda e: e.tensor_scalar(dn[:], dn[:], 1.0, None, ALU.max), r=[dnr], w=[dnr])
                op("vector", lambda e: e.reciprocal(dn[:], dn[:]), r=[dnr], w=[dnr])
                for dd in range(2):
                    op("vector", lambda e: e.tensor_tensor(dn[:, dd, :], dn[:, dd, :], gEb[:, dd, :, hh], ALU.mult), r=[dnr, gEbr], w=[dnr])
                    op("vector", lambda e: e.tensor_tensor(hraw[:, dd, :, 0:128], hraw[:, dd, :, 0:128], dn[:, dd, :].unsqueeze(2).to_broadcast([128, NB, 128]), ALU.mult),
                       r=[hrawr, dnr], w=[hrawr])
                hsum = hraw[:, 0, :, 0:128]
                hsumr = hrawr
                op("gpsimd", lambda e: e.tensor_tensor(hsum, hraw[:, 0, :, 0:128], hraw[:, 1, :, 0:128], ALU.add), r=[hrawr], w=[hrawr])
                ssum, ssumr = fw.sbuf("ssum", [128, NB], F32)
                readout(hsum, hsumr, ssum, ssumr, hraw[:, 1, :, 0:128], hrawr, og, ogr, gml[:, :], gmlr, 128, NB)
                yb, ybr = fw.sbuf("yb", [128, NB, 128], BF16)
                op("vector", lambda e: e.tensor_copy(yb[:], hsum), r=[hsumr], w=[ybr])
                stg = [(hps[i][0][:, 0:2, :].rearrange("p a b -> p (a b)"), hps[i][1]) for i in range(2)]
                for gi, b0 in enumerate(range(0, NB, 8)):
                    nb_ = min(8, NB - b0)
                    for b_ in range(nb_):
                        op("tensor", lambda e: e.transpose(psB[:, b_ * 128:(b_ + 1) * 128], yb[:, b0 + b_, :], ident_b[:]), r=[ybr, ident_br], w=[psBr], sig=(b_ == nb_ - 1))
                    st, str_ = stg[gi % 2]
                    op("scalar", lambda e: e.copy(st[:, 0:nb_ * 128], psB[:, 0:nb_ * 128]), r=[psBr], w=[str_])
                    dma("sync", mlTd[hh * 128:(hh + 1) * 128, b0 * 128:(b0 + nb_) * 128], st[:, 0:nb_ * 128], sb=str_, r=[str_])

        def readout(h, hr, ssum, ssumr, scratch, scratchr, gate, gater, gvec, gvecr, P, NBK):
            op("scalar", lambda e: e.activation(scratch, h, AF.Square), r=[hr], w=[scratchr])
            op("vector", lambda e: e.tensor_reduce(ssum[:], scratch, AX.X, ALU.add), r=[scratchr], w=[ssumr])
            rsqrt_small(ssum[:], ssumr, P, 1.0 / 128)
            op("vector", lambda e: e.tensor_tensor(h, h, ssum[:].unsqueeze(2).to_broadcast([P, NBK, 128]), ALU.mult), r=[hr, ssumr], w=[hr])
            op("vector", lambda e: e.tensor_tensor(h, h, gvec[0:P].unsqueeze(1).to_broadcast([P, NBK, 128]), ALU.mult), r=[hr, gvecr], w=[hr])
            op("vector", lambda e: e.tensor_tensor(h, h, gate[:], ALU.mult), r=[hr, gater], w=[hr])

        NCH = T // 64
        NC32 = T // 32
        lbt, lbtr = fw.sbuf("lbt", [64, 2, 4], F32)
        lb1, lb1r = fw.sbuf("lb1", [64, DEPTH, 4], F32)
        oml, omlr = fw.sbuf("oml", [64, DEPTH, 4], F32)
        rmask, rmaskr = fw.sbuf("rmask", [64, 512], F32)

        def lb_setup():
            dma("sync", lbt[:], hg_lbl, sb=lbtr, w=[lbtr])
            dma("sync", rmask[:], c_rmask, sb=rmaskr, w=[rmaskr])
            op("scalar", lambda e: e.activation(lbt[:], lbt[:], AF.Exp), r=[lbtr], w=[lbtr])
            op("vector", lambda e: e.memset(lb1[:], 0.0), w=[lb1r])
            op("vector", lambda e: e.tensor_tensor(lb1[:, 1, :], lbt[:, 0, :], lbt[:, 1, :], ALU.add), r=[lbtr], w=[lb1r])
            op("vector", lambda e: e.reciprocal(lb1[:, 1, :], lb1[:, 1, :]), r=[lb1r], w=[lb1r])
            op("vector", lambda e: e.tensor_tensor(lb1[:, 1, :], lb1[:, 1, :], lbt[:, 1, :], ALU.mult), r=[lb1r, lbtr], w=[lb1r])
            op("vector", lambda e: e.tensor_scalar(oml[:], lb1[:], -1.0, 1.0, ALU.mult, ALU.add), r=[lb1r], w=[omlr])

        def hgrn_phase(l, hh):
            with Phase():
                hps = alloc_hps(NP)
                whq, whqr = fw.sbuf("whq", [128, KC, 64], BF16)
                whf = [fw.sbuf("whf%d" % i, [128, KC, 64], BF16) for i in range(2)]
                whv, whvr = fw.sbuf("whv", [128, KC, 128], BF16)
                whg, whgr = fw.sbuf("whg", [128, KC, 128], BF16)
                load_w2(whq, whqr, w_inb[l][:, C_HQ + hh * 64:C_HQ + (hh + 1) * 64], 64)
                for dd in range(2):
                    load_w2(whf[dd][0], whf[dd][1], w_inb[l][:, C_HF + dd * 256 + hh * 64:C_HF + dd * 256 + (hh + 1) * 64], 64)
                load_w2(whv, whvr, w_inb[l][:, C_HV + hh * 128:C_HV + (hh + 1) * 128], 128)
                load_w2(whg, whgr, w_inb[l][:, C_HG + hh * 128:C_HG + (hh + 1) * 128], 128)
                ghg, ghgr = fw.sbuf("ghg", [64, 128], F32)
                dma("sync", ghg[:], hg_norm[l, :, hh * 128:(hh + 1) * 128].partition_broadcast(64), sb=ghgr, w=[ghgr])
                qt_ = [fw.sbuf("qt%d" % i, [64, T], BF16) for i in range(2)]
                kt_ = [fw.sbuf("kt%d" % i, [64, T], BF16) for i in range(2)]
                PT, PTr = fw.sbuf("PT", [64, 2, NC32, 2], F32)
                Vt, Vtr = fw.sbuf("Vt", [64, NCH, 128], BF16)
                gt, gtr = fw.sbuf("gt", [64, NCH, 128], BF16)
                osum, osumr = fw.sbuf("osum", [64, NCH, 128], F32)
                graw, grawr = fw.sbuf("graw", [64, 8, 128], F32)
                gtmp, gtmpr = fw.sbuf("gtmp", [64, 8, 128], F32)
                tblk, tblkr = fw.sbuf("tblk", [64, 11 * NP], F32)
                def carve(i, nm):
                    return tblk[:, i * NP:(i + 1) * NP], fw.reg(nm)
                qs, qsr = carve(0, "qs")
                tA = [carve(1 + i, "tA%d" % i) for i in range(2)]
                tB = [carve(3 + i, "tB%d" % i) for i in range(2)]
                tC = [carve(5 + i, "tC%d" % i) for i in range(2)]
                tD = [carve(7 + i, "tD%d" % i) for i in range(2)]
                tE = [carve(9 + i, "tE%d" % i) for i in range(2)]
                qpr = [[fw.reg("qp%d_%d" % (d_, i)) for i in range(9)] for d_ in range(2)]
                kpr = [[fw.reg("kp%d_%d" % (d_, i)) for i in range(9)] for d_ in range(2)]
                Vpr = [fw.reg("Vp%d" % i) for i in range(9)]
                Apr = [[fw.reg("Ap%d_%d" % (d_, i)) for i in range(9)] for d_ in range(2)]
                ogr_ = [fw.reg("og%d" % i) for i in range(NCH // 4)]
                mt = [fw.sbuf("mt%d" % i, [64, 32], F32) for i in range(2)]
                for dd in range(2):
                    msk, mskr = (masku_f, masku_fr) if dd == 0 else (maskl_f, maskl_fr)
                    op("vector", lambda e: e.tensor_copy(mt[dd][0][0:32, :], msk[0:32, 0:32]), r=[mskr], w=[mt[dd][1]])
                    op("vector", lambda e: e.tensor_copy(mt[dd][0][32:64, :], msk[32:64, 32:64]), r=[mskr], w=[mt[dd][1]])
                AmA = [fw.sbuf("AmA%d" % i, [64, NCH, 32], BF16) for i in range(2)]
                PS7 = (psB[:, :].bitcast(F32), psBr)
                bk = 0
                for pi, tok0, N in pieces():
                    hp, hpr = hps[pi % 2]
                    dma("sync", hp[:, :, :N], h2d_v[:, :, tok0:tok0 + N], sb=hpr, w=[hpr])
                    nch = N // 64
                    ch0 = tok0 // 64
                    n32 = N // 32
                    c32 = tok0 // 32
                    pq, pqr = PS[0]
                    fm(pq, pqr, whq, whqr, 0, 64, hp, hpr, N)
                    op("scalar", lambda e: e.activation(qs[:, :N], pq[0:64, :N], AF.Exp, scale=-1.0), r=[pqr], w=[qsr])
                    op("scalar", lambda e: e.activation(qs[:, :N], qs[:, :N], AF.Ln, bias=oneb[0:64, 0:1], scale=1.0), r=[qsr, onebr], w=[qsr])
                    op("scalar", lambda e: e.activation(qs[:, :N], qs[:, :N], AF.Exp, scale=-1.0), r=[qsr], w=[qsr])
                    op("vector", lambda e: e.tensor_tensor(qs[:, :N], qs[:, :N], pq[0:64, :N], ALU.mult), r=[qsr, pqr], w=[qsr])
                    for dd in range(2):
                        pz, pzr = PS[1 + dd]
                        fm(pz, pzr, whf[dd][0], whf[dd][1], 0, 64, hp, hpr, N)
                        f_, f_r = tA[dd]
                        lf, lfr = tB[dd]
                        p_, p_r = tC[dd]
                        ud, udr = tD[dd]
                        ee, eer = tE[dd]
                        op("scalar", lambda e: e.activation(f_[:, :N], pz[0:64, :N], AF.Exp, scale=-1.0), r=[pzr], w=[f_r])
                        op("scalar", lambda e: e.activation(f_[:, :N], f_[:, :N], AF.Ln, bias=oneb[0:64, 0:1], scale=1.0), r=[f_r, onebr], w=[f_r])
                        op("scalar", lambda e: e.activation(f_[:, :N], f_[:, :N], AF.Exp, scale=-1.0), r=[f_r], w=[f_r])
                        op("vector", lambda e: e.tensor_scalar(f_[:, :N], f_[:, :N], oml[:, l, hh:hh + 1], lb1[:, l, hh:hh + 1], ALU.mult, ALU.add),
                           r=[f_r, omlr, lb1r], w=[f_r])
                        op("scalar", lambda e: e.activation(lf[:, :N], f_[:, :N], AF.Ln), r=[f_r], w=[lfr])
                        op("vector", lambda e: e.tensor_tensor_scan(p_[:, :N], rmask[:, :N], lf[:, :N], 0.0, ALU.mult, ALU.add), r=[rmaskr, lfr], w=[p_r])
                        p3 = p_[:, :N].rearrange("p (c t) -> p c t", t=64)
                        op("gpsimd", lambda e: e.tensor_copy(PT[:, dd, c32:c32 + n32, 0], p_[:, :N].rearrange("p (c t) -> p c t", t=32)[:, :, 15]), r=[p_r], w=[PTr])
                        op("gpsimd", lambda e: e.tensor_copy(PT[:, dd, c32:c32 + n32, 1], p_[:, :N].rearrange("p (c t) -> p c t", t=32)[:, :, 31]), r=[p_r], w=[PTr])
                        if dd == 1:
                            op("vector", lambda e: e.tensor_tensor(ud[:, :N], p_[:, :N], lf[:, :N], ALU.subtract), r=[p_r, lfr], w=[udr])
                            usrc = ud
                            usrcr = udr
                        else:
                            usrc = p_
                            usrcr = p_r
                        op("vector", lambda e: e.tensor_tensor(ud[:, :N].rearrange("p (c t) -> p c t", t=32), usrc[:, :N].rearrange("p (c t) -> p c t", t=32),
                                                               PT[:, dd, c32:c32 + n32, 0:1].to_broadcast([64, n32, 32]), ALU.subtract),
                           r=[usrcr, PTr], w=[udr])
                        sq_, sk_ = (1.0, -1.0) if dd == 0 else (-1.0, 1.0)
                        op("vector", lambda e: e.tensor_scalar(f_[:, :N], f_[:, :N], -1.0, 1.0, ALU.mult, ALU.add), r=[f_r], w=[f_r])
                        op("scalar", lambda e: e.activation(ee[:, :N], ud[:, :N], AF.Exp, scale=sq_), r=[udr], w=[eer])
                        op("vector", lambda e: e.scalar_tensor_tensor(qt_[dd][0][:, tok0:tok0 + N], qs[:, :N], 0.125, ee[:, :N], ALU.mult, ALU.mult),
                           r=[qsr, eer], w=[qpr[dd][pi]])
                        op("scalar", lambda e: e.activation(ee[:, :N], ud[:, :N], AF.Exp, scale=sk_), r=[udr], w=[eer])
                        op("vector", lambda e: e.tensor_tensor(kt_[dd][0][:, tok0:tok0 + N], f_[:, :N], ee[:, :N], ALU.mult), r=[f_r, eer], w=[kpr[dd][pi]])
                    for c_ in range(nch):
                        cc = ch0 + c_
                        pv, pvr = PS[3 + cc % 2]
                        tmj(pv[0:64, 0:128], pvr, whv, whvr, 0, 128, hp, hpr, c_ * 64, 64)
                        op("vector", lambda e: e.tensor_copy(Vt[:, cc, :], pv[0:64, 0:128]), r=[pvr], w=[Vpr[pi]], piece=True)
                        pg, pgr = PS[5 + cc % 2]
                        tmj(pg[0:64, 0:128], pgr, whg, whgr, 0, 128, hp, hpr, c_ * 64, 64)
                        op("vector", lambda e: e.tensor_copy(graw[:, c_, :], pg[0:64, 0:128]), r=[pgr], w=[grawr])
                    op("scalar", lambda e: e.activation(gtmp[:, 0:nch, :], graw[:, 0:nch, :], AF.Exp, scale=-1.0), r=[grawr], w=[gtmpr])
                    op("scalar", lambda e: e.activation(gtmp[:, 0:nch, :], gtmp[:, 0:nch, :], AF.Ln, bias=oneb[0:64, 0:1], scale=1.0), r=[gtmpr, onebr], w=[gtmpr])
                    op("scalar", lambda e: e.activation(gtmp[:, 0:nch, :], gtmp[:, 0:nch, :], AF.Exp, scale=-1.0), r=[gtmpr], w=[gtmpr])
                    op("vector", lambda e: e.tensor_tensor(gt[:, ch0:ch0 + nch, :], gtmp[:, 0:nch, :], graw[:, 0:nch, :], ALU.mult), r=[gtmpr, grawr], w=[gtr])
                    for dd in range(2):
                        qd, _q = qt_[dd]
                        kd, _k = kt_[dd]
                        Aall, _a = AmA[dd]
                        pa, par = PS7
                        for j_ in range(nch):
                            for a_ in range(2):
                                cc = (ch0 + j_) * 2 + a_
                                P_ = slice(a_ * 32, (a_ + 1) * 32)
                                sl = slice(cc * 32, (cc + 1) * 32)
                                op("tensor", lambda e: e.matmul(pa[P_, j_ * 32:(j_ + 1) * 32], kd[:, sl], qd[:, sl], start=True, stop=True),
                                   r=[kpr[dd][pi], qpr[dd][pi]], w=[par], sig=(j_ == nch - 1 and a_ == 1))
                        op("vector", lambda e: e.tensor_tensor(Aall[:, ch0:ch0 + nch, :], pa[0:64, 0:nch * 32].rearrange("p (c t) -> p c t", t=32),
                                                               mt[dd][0][:].unsqueeze(1).to_broadcast([64, nch, 32]), ALU.mult), r=[par, mt[dd][1]], w=[Apr[dd][pi]])
                    for dd in range(2):
                        Aall, _a = AmA[dd]
                        for g0 in range(ch0, ch0 + nch, 4):
                            po, por = PS[bk % 3]
                            bk += 1
                            for j_ in range(4):
                                c64 = g0 + j_
                                for a_ in range(2):
                                    P_ = slice(a_ * 32, (a_ + 1) * 32)
                                    op("tensor", lambda e: e.matmul(po[P_, j_ * 128:(j_ + 1) * 128], Aall[P_, c64, :], Vt[P_, c64, :], start=True, stop=True),
                                       r=[Apr[dd][pi], Vpr[pi]], w=[por], sig=(j_ == 3 and a_ == 1))
                            if dd == 0:
                                op("scalar", lambda e: e.copy(osum[:, g0:g0 + 4, :], po[0:64, :].rearrange("p (c v) -> p c v", v=128)), r=[por], w=[ogr_[g0 // 4]])
                            else:
                                op("vector", lambda e: e.tensor_tensor(osum[:, g0:g0 + 4, :], osum[:, g0:g0 + 4, :], po[0:64, :].rearrange("p (c v) -> p c v", v=128), ALU.add),
                                   r=[por, ogr_[g0 // 4]], w=[ogr_[g0 // 4]])
                X1, X1r = fw.sbuf("X1", [64, 2, NC32], F32)
                X2, X2r = fw.sbuf("X2", [64, 2, NC32], F32)
                X3, X3r = fw.sbuf("X3", [64, 2, NC32], F32)
                op("vector", lambda e: e.tensor_tensor(X2[:], PT[:, :, :, 1], PT[:, :, :, 0], ALU.subtract), r=[PTr], w=[X2r])
                op("scalar", lambda e: e.activation(X2[:], X2[:], AF.Exp), r=[X2r], w=[X2r])
                op("scalar", lambda e: e.activation(X1[:], PT[:, :, :, 0], AF.Exp), r=[PTr], w=[X1r])
                op("scalar", lambda e: e.activation(X3[:], PT[:, :, :, 1], AF.Exp), r=[PTr], w=[X3r])
                KY, KYr = fw.sbuf("KY", [64, 2 * NCH * 64], BF16)
                ktok = KY[:].rearrange("p (d c k) -> p d c k", d=2, k=64)
                fw.barrier()
                qt_ = [(qt_[d_][0], fw.reg("qtall%d" % d_)) for d_ in range(2)]
                kt_ = [(kt_[d_][0], fw.reg("ktall%d" % d_)) for d_ in range(2)]
                Vtr = fw.reg("Vtall")
                osumr = fw.reg("osumall")
                ksv = tblk[:, :].bitcast(BF16)
                ks_ = [(ksv[:, dd * T:(dd + 1) * T], fw.reg("ks%d" % dd)) for dd in range(2)]
                for dd in range(2):
                    Xd, Xdr = (X2, X2r) if dd == 0 else (X1, X1r)
                    op("vector", lambda e: e.tensor_tensor(ks_[dd][0].rearrange("p (c t) -> p c t", t=32), kt_[dd][0][:, :].rearrange("p (c t) -> p c t", t=32),
                                                           Xd[:, dd, :].unsqueeze(2).to_broadcast([64, NC32, 32]), ALU.mult), r=[kt_[dd][1], Xdr], w=[ks_[dd][1]])
                for dd in range(2):
                    for c0 in range(0, NCH, 16):
                        n_ = min(16, NCH - c0)
                        for c_ in range(n_):
                            cc = c0 + c_
                            op("tensor", lambda e: e.transpose(psB[0:64, c_ * 64:(c_ + 1) * 64], ks_[dd][0][:, cc * 64:(cc + 1) * 64], ident_b[0:64, 0:64]),
                               r=[ks_[dd][1], ident_br], w=[psBr], sig=(c_ == n_ - 1))
                        op("scalar", lambda e: e.copy(ktok[:, dd, c0:c0 + n_, :], psB[0:64, 0:n_ * 64].rearrange("p (c k) -> p c k", k=64)), r=[psBr], w=[KYr])
                for dd in range(2):
                    Xin, Xinr = (X1, X1r) if dd == 0 else (X2, X2r)
                    op("vector", lambda e: e.tensor_tensor(qt_[dd][0][:, :].rearrange("p (c t) -> p c t", t=32), qt_[dd][0][:, :].rearrange("p (c t) -> p c t", t=32),
                                                           Xin[:, dd, :].unsqueeze(2).to_broadcast([64, NC32, 32]), ALU.mult), r=[qt_[dd][1], Xinr], w=[qt_[dd][1]])
                Sst = [fw.sbuf("Sst%d" % i, [64, 128], F32) for i in range(2)]
                Sbf = [[fw.sbuf("Sbf%d_%d" % (i, j), [64, 128], BF16) for j in range(2)] for i in range(2)]
                for dd in range(2):
                    op("vector", lambda e: e.memset(Sst[dd][0][:], 0.0), w=[Sst[dd][1]])
                    for j in range(2):
                        op("gpsimd", lambda e: e.memset(Sbf[dd][j][0][:], 0.0), w=[Sbf[dd][j][1]])
                c0c = L // 32
                orders = [list(range(c0c, NC32)) + list(range(c0c)), list(range(NC32 - 1, c0c - 1, -1)) + list(range(c0c - 1, -1, -1))]
                PS7 = (psB[:, :].bitcast(F32), psBr)
                pobanks = [[PS[4], PS[6]], [PS[5], PS7]]
                for j in range(NC32):
                    for dd in range(2):
                        qd, qdr = qt_[dd]
                        cc = orders[dd][j]
                        c64, a_ = cc // 2, cc % 2
                        P_ = slice(a_ * 32, (a_ + 1) * 32)
                        sl = slice(cc * 32, (cc + 1) * 32)
                        grp, slot = c64 // 4, c64 % 4
                        st_, st_r = Sst[dd]
                        pdl, pdlr = PS[(j % 2) * 2 + dd]
                        op("tensor", lambda e: e.matmul(pdl[0:64, 0:128], ktok[P_, dd, c64, :], Vt[P_, c64, :], start=True, stop=True), r=[KYr, Vtr], w=[pdlr])
                        sprev, sprevr = Sbf[dd][(j + 1) % 2]
                        po, por = pobanks[dd][(j // 8) % 2]
                        op("tensor", lambda e: e.matmul(po[P_, slot * 128:(slot + 1) * 128], qd[:, sl], sprev[:], start=True, stop=True), r=[qdr, sprevr], w=[por])
                        op("vector", lambda e: e.scalar_tensor_tensor(st_[:], st_[:], X3[:, dd, cc:cc + 1], pdl[0:64, 0:128], ALU.mult, ALU.add), r=[st_r, X3r, pdlr], w=[st_r])
                        if j + 1 < NC32:
                            snext, snextr = Sbf[dd][j % 2]
                            op("scalar", lambda e: e.copy(snext[:], st_[:]), r=[st_r], w=[snextr])
                        if j % 8 == 7:
                            op("vector", lambda e: e.tensor_tensor(osum[:, grp * 4:(grp + 1) * 4, :], osum[:, grp * 4:(grp + 1) * 4, :],
                                                                   po[0:64, :].rearrange("p (c v) -> p c v", v=128), ALU.add), r=[por, osumr], w=[osumr])
                ssum, ssumr = fw.sbuf("ssum", [64, NCH], F32)
                fw.barrier()
                sqt, sqtr = tblk[:, 0:17 * 128].rearrange("p (c v) -> p c v", v=128), tblkr
                for c0 in range(0, NCH, 17):
                    op("scalar", lambda e: e.activation(sqt, osum[:, c0:c0 + 17, :], AF.Square), r=[osumr], w=[sqtr])
                    op("vector", lambda e: e.tensor_reduce(ssum[:, c0:c0 + 17], sqt, AX.X, ALU.add), r=[sqtr], w=[ssumr])
                rsqrt_small(ssum[:], ssumr, 64, 1.0 / 128)
                op("vector", lambda e: e.tensor_tensor(osum[:], osum[:], ssum[:].unsqueeze(2).to_broadcast([64, NCH, 128]), ALU.mult), r=[osumr, ssumr], w=[osumr])
                op("vector", lambda e: e.tensor_tensor(osum[:], osum[:], ghg[:, :].unsqueeze(1).to_broadcast([64, NCH, 128]), ALU.mult),
                   r=[osumr, ghgr], w=[osumr])
                yh = KY[:].rearrange("p (c v) -> p c v", v=128)
                op("vector", lambda e: e.tensor_tensor(yh, osum[:], gt[:], ALU.mult), r=[osumr, gtr], w=[KYr])
                stg = [(hps[i][0][:, 0:2, :].rearrange("p a b -> p (a b)"), hps[i][1]) for i in range(2)]
                for gi, c0 in enumerate(range(0, NCH, 16)):
                    n_ = min(16, NCH - c0)
                    for c_ in range(n_):
                        op("tensor", lambda e: e.transpose(psB[:, c_ * 64:(c_ + 1) * 64], yh[:, c0 + c_, :], ident_b[0:64, 0:64]), r=[KYr, ident_br], w=[psBr], sig=(c_ == n_ - 1))
                    st, str_ = stg[gi % 2]
                    op("scalar", lambda e: e.copy(st[:, 0:n_ * 64], psB[:, 0:n_ * 64]), r=[psBr], w=[str_])
                    dma("sync", hgTd[hh * 128:(hh + 1) * 128, c0 * 64:(c0 + n_) * 64], st[:, 0:n_ * 64], sb=str_, r=[str_])

        def merge_phase(l):
            last = (l == DEPTH - 1)
            with Phase():
                hps = alloc_hps(NT)
                wbs = []
                for nm, src in (("wba", wb_attb), ("wbm", wb_mlb), ("wbh", wb_hgb)):
                    wt, wtr = fw.sbuf(nm, [128, 4, D], BF16)
                    for kk in range(4):
                        dma("sync", wt[:, kk, :], src[l, kk * 128:(kk + 1) * 128, :], sb=wtr, r=wcast_reg, w=[wtr], piece=True)
                    wbs.append((wt, wtr))
                wbg, wbgr = fw.sbuf("wbg", [128, KC, 3 * D], BF16)
                for kk in range(KC):
                    for c0 in range(0, 3 * D, 1024):
                        dma("sync", wbg[:, kk, c0:c0 + 1024], w_inb[l, kk * 128:(kk + 1) * 128, C_BR + c0:C_BR + c0 + 1024], sb=wbgr, r=wcast_reg, w=[wbgr], piece=True)
                wo, wor = fw.sbuf("wo", [128, KC, D], BF16)
                for kk in range(KC):
                    dma("sync", wo[:, kk, :], w_outb[l, kk * 128:(kk + 1) * 128, :], sb=wor, r=wcast_reg, w=[wor], piece=True)
                brs = [[fw.sbuf("br%d_%d" % (b_, i), [128, 4, NT], BF16) for i in range(2)] for b_ in range(3)]
                xTs = [fw.sbuf("xT%d" % i, [128, KC, NT], F32) for i in range(2)]
                ym, ymr = fw.sbuf("ym", [128, KC, NT], BF16)
                sq, sqr = fw.sbuf("sq", [128, KC, NT], BF16)
                ysb, ysbr = fw.sbuf("ysb", [128, KC, NT], F32)
                gsb = [fw.sbuf("gsb%d" % i, [128, NT], F32) for i in range(3)]
                yacc, yaccr = fw.sbuf("yacc", [128, NT], F32)
                tmp_ = [fw.sbuf("tmp%d" % i, [128, NT], F32) for i in range(2)]
                tmp, tmpr = [t[0] for t in tmp_], [t[1] for t in tmp_]
                rs, rsr = fw.sbuf("rs", [128, NT], F32)
                srcs = (attTd, mlTd, hgTd)
                ntiles = L // NT + (0 if last else 1)
                for ti in range(ntiles):
                    tok0, wi = tile_info(ti)
                    N = NT
                    hp, hpr = hps[ti % 2]
                    xT, xTr = xTs[ti % 2]
                    dma("sync", hp[:, :, :N], h2d_v[:, :, tok0:tok0 + N], sb=hpr, w=[hpr])
                    dma("sync", xT[:, :, :N], xTd_v[:, :, tok0:tok0 + N], sb=xTr, w=[xTr])
                    for b_ in range(3):
                        bt, btr = brs[b_][ti % 2]
                        dma("sync", bt[:, :, :N], srcs[b_][:, tok0:tok0 + N].rearrange("(c p) t -> p c t", p=128), sb=btr, w=[btr])
                    for m in range(KC):
                        for b_ in range(3):
                            bt, btr = brs[b_][ti % 2]
                            wt, wtr = wbs[b_]
                            pz, pzr = PS[b_ % 3]
                            pg, pgr = PS[3 + b_ % 3]
                            for kk in range(4):
                                op("tensor", lambda e: e.matmul(pz[:, :N], wt[:, kk, m * 128:(m + 1) * 128], bt[:, kk, :N], start=(kk == 0), stop=(kk == 3)),
                                   r=[wtr, btr], w=[pzr], sig=(kk == 3))
                            fm(pg, pgr, wbg, wbgr, b_ * D + m * 128, 128, hp, hpr, N)
                            gs, gsr = gsb[b_]
                            op("scalar", lambda e: e.activation(gs[:, :N], pg[:, :N], AF.Sigmoid), r=[pgr], w=[gsr])
                            if b_ == 0:
                                op("vector", lambda e: e.tensor_tensor(yacc[:, :N], gs[:, :N], pz[:, :N], ALU.mult), r=[gsr, pzr], w=[yaccr])
                            else:
                                op("vector", lambda e: e.tensor_tensor(gs[:, :N], gs[:, :N], pz[:, :N], ALU.mult), r=[gsr, pzr], w=[gsr])
                                if b_ == 1:
                                    op("gpsimd", lambda e: e.tensor_tensor(yacc[:, :N], yacc[:, :N], gs[:, :N], ALU.add), r=[gsr, yaccr], w=[yaccr])
                                else:
                                    op("gpsimd", lambda e: e.tensor_tensor(ym[:, m, :N], yacc[:, :N], gs[:, :N], ALU.add), r=[gsr, yaccr], w=[ymr])
                    for m2 in range(KC):
                        py, pyr = PS[m2 % 2]
                        for m in range(KC):
                            op("tensor", lambda e: e.matmul(py[:, :N], wo[:, m, m2 * 128:(m2 + 1) * 128], ym[:, m, :N], start=(m == 0), stop=(m == KC - 1)),
                               r=[wor, ymr], w=[pyr], sig=(m == KC - 1))
                        op("scalar", lambda e: e.activation(sq[:, m2, :N], py[:, :N], AF.Square), r=[pyr], w=[sqr])
                        op("vector", lambda e: e.tensor_copy(ysb[:, m2, :N], py[:, :N]), r=[pyr], w=[ysbr])
                    post_norm_add(1, wi, ysb, ysbr, sq, sqr, rs, rsr, tmp, tmpr, xT, xTr, N)
                    dma("sync", xTd_v[:, :, tok0:tok0 + N], xT[:, :, :N], sb=xTr, r=[xTr], w=[xTd_reg[ti]])

        lb_setup()
        stages = []
        for l in range(DEPTH):
            stages.append(("mod%d" % l, lambda l=l: mod_phase(l)))
            stages.append(("ffn1_%d" % l, lambda l=l: ffn_phase(l, 0)))
            for g in range(2):
                stages.append(("att%d_%d" % (l, g), lambda l=l, g=g: att_phase(l, g)))
            stages.append(("gates%d" % l, lambda l=l: gates_phase(l)))
            for hh in range(4):
                stages.append(("ml%d_%d" % (l, hh), lambda l=l, hh=hh: mlstm_phase(l, hh)))
            for hh in range(4):
                stages.append(("hg%d_%d" % (l, hh), lambda l=l, hh=hh: hgrn_phase(l, hh)))
            stages.append(("merge%d" % l, lambda l=l: merge_phase(l)))
            stages.append(("ffn2_%d" % l, lambda l=l: ffn_phase(l, 1)))
        k.stage_names = [s[0] for s in stages]
        for name, fn in stages:
            if name.startswith("gates") or name.startswith("ml"):
                pass
            fn()
            if debug and name.startswith("mod"):
                dma("sync", dbg[:, 0:144], modT[:].rearrange("p m w -> p (m w)"), sb=modr, r=[modr])
            if stop is not None and name == stop:
                break
        fw.barrier()
        fw.finish()
        k.ninst = fw.ninst
        k.nwait = fw.nwait
    k.nc = nc
    return k


def host_inputs(inputs):
    f32 = np.float32
    g = {k_: np.asarray(v) for k_, v in inputs.items()}
    cst = host_constants()
    shared = {}
    shared["w_ada"] = np.ascontiguousarray(g["w_ada"], f32)
    shared["b_ada"] = np.ascontiguousarray(g["b_ada"].reshape(DEPTH, 72, 128).transpose(0, 2, 1), f32)
    shared["g_pre"] = np.ascontiguousarray(g["norm_pre"].reshape(DEPTH, 3, 8, 128).transpose(0, 3, 1, 2), f32)
    shared["g_post"] = np.ascontiguousarray(g["norm_post"].reshape(DEPTH, 3, 8, 128).transpose(0, 3, 1, 2), f32)
    shared["ffn_w1"] = np.ascontiguousarray(g["ffn_w1"], f32)
    shared["ffn_w3"] = np.ascontiguousarray(g["ffn_w3"], f32)
    shared["ffn_w2"] = np.ascontiguousarray(g["ffn_w2"], f32)
    shared["w_in"] = np.ascontiguousarray(g["w_in"], f32)
    qcols = np.concatenate([h * 64 + _partner(np.arange(64)) for h in range(8)])
    kcols = np.concatenate([C_AK + h * 64 + _partner(np.arange(64)) for h in range(2)])
    shared["w_sw"] = np.ascontiguousarray(g["w_in"][:, :, np.concatenate([qcols, kcols])], f32)
    shared["att_sink"] = np.ascontiguousarray(g["att_sink"].reshape(DEPTH, 1, 8), f32)
    shared["ml_conv"] = np.ascontiguousarray(g["ml_conv"].transpose(0, 2, 1), f32)
    shared["ml_fb"] = np.ascontiguousarray(g["ml_f_bias"].reshape(DEPTH, 1, 8), f32)
    shared["ml_norm"] = np.ascontiguousarray(g["ml_norm"].reshape(DEPTH, 1, 512), f32)
    shared["hg_lbl"] = np.ascontiguousarray(g["hg_lb_logits"].reshape(DEPTH, 4, 64).transpose(2, 0, 1), f32)
    shared["hg_norm"] = np.ascontiguousarray(g["hg_norm"].reshape(DEPTH, 1, 512), f32)
    shared["wb_att"] = np.ascontiguousarray(g["w_branch_att"], f32)
    shared["wb_ml"] = np.ascontiguousarray(g["w_branch_ml"], f32)
    shared["wb_hg"] = np.ascontiguousarray(g["w_branch_hg"], f32)
    shared["w_out"] = np.ascontiguousarray(g["w_out"], f32)
    for nm in ("ident", "ones", "masku", "maskl", "rcos", "rsin", "rmask"):
        shared["c_" + nm] = cst[nm]
    maps = []
    cc = np.asarray(g["c_ctx"], f32).reshape(8, 128).T
    for b in range(g["x"].shape[0]):
        m = dict(shared)
        m["x"] = np.ascontiguousarray(g["x"][b], f32)
        m["ctx"] = np.ascontiguousarray(g["ctx"][b], f32)
        cb = np.asarray(g["c"][b], f32).reshape(8, 128).T
        m["cvec"] = np.ascontiguousarray(np.stack([cb, cc], axis=-1), f32)
        maps.append(m)
    return maps


_CACHE = {}


def kernel(**inputs):
    maps = host_inputs(inputs)
    if "k" not in _CACHE:
        _CACHE["k"] = build_nc()
    k = _CACHE["k"]
    res = run_bass_kernel_spmd(k.nc, maps, core_ids=list(range(len(maps))))
    return np.stack([np.asarray(r["out"], np.float32) for r in res.results], axis=0)
```

```python
import numpy as np
import concourse.bass as bass
import concourse.mybir as mybir
from concourse.bass_utils import run_bass_kernel_spmd
from contextlib import ExitStack

F32 = mybir.dt.float32
BF16 = mybir.dt.bfloat16
ALU = mybir.AluOpType
AF = mybir.ActivationFunctionType
AX = mybir.AxisListType


class Reg:
    __slots__ = ("name", "lw", "lr", "dsem", "dcnt", "dkind", "excl")

    def __init__(self, name):
        self.name = name
        self.lw = []
        self.lr = []
        self.dsem = None
        self.dcnt = 0
        self.dkind = None
        self.excl = False


class Eng:
    def __init__(self, name, eng, sem):
        self.name = name
        self.eng = eng
        self.sem = sem
        self.cnt = 0
        self.seen = {}


class _Op:
    __slots__ = ("id", "eng", "fns", "r", "w", "piece", "dma", "preds", "dur", "lat")


class _FakeInst:
    def then_inc(self, *a, **k):
        return self


class _FakeEng:
    def __init__(self):
        self.calls = []

    def __getattr__(self, name):
        def f(*a, **k):
            self.calls.append((name, a, k))
            return _FakeInst()
        return f


def _nfree(ap):
    sh = ap.shape
    n = 1
    for d_ in sh[1:]:
        n *= int(d_)
    return n


def _estimate(ename, calls):
    t = 0.0
    for name, a, k in calls:
        try:
            if name == "matmul":
                rhs = a[2] if len(a) > 2 else k["rhs"]
                n = _nfree(rhs)
                mult = 4 if rhs.dtype == F32 else 1
                t += (max(n, 64) * mult + 40) / 2400.0
            elif name == "transpose":
                t += 0.16
            else:
                out = a[0] if a else k.get("out")
                n = _nfree(out)
                if name == "reciprocal":
                    t += 0.1 + n * 0.0064
                elif ename == "scalar":
                    t += 0.17 + n / 1200.0
                elif ename == "gpsimd":
                    t += 0.2 + n / 480.0
                else:
                    t += 0.15 + n / 960.0
        except Exception:
            t += 0.3
    return t


class FW:
    SAME_ENGINE_SYNC = True
    SBUF_LIMIT = 206 * 1024
    WINDOW = 40
    LAT = 0.25

    def __init__(self, nc, es):
        self.nc = nc
        self.es = es
        self.es0 = es
        self.sem_pool = {}
        self.sems = {}
        self.E = {}
        for name in ("tensor", "vector", "scalar", "gpsimd", "sync"):
            sem = es.enter_context(nc.semaphore("s_" + name))
            self.sems[id(sem)] = sem
            self.E[name] = Eng(name, getattr(nc, name), sem)
        self.nwait = 0
        self.ninst = 0
        self.dregs = []
        self.ops = []
        self.group = {}
        self.events = {}
        self.next_id = 0
        self.semcnt = {}

    def reg(self, name):
        return Reg(name)

    def sbuf(self, name, shape, dt):
        self.uid = getattr(self, "uid", 0) + 1
        name = "%s_%d" % (name, self.uid)
        t = self.es.enter_context(self.nc.sbuf_tensor(name, list(shape), dt))
        nb = int(np.prod(shape[1:])) * (2 if dt == BF16 else 4)
        nb = (nb + 31) // 32 * 32
        self.sb_used = getattr(self, "sb_used", 0) + nb
        self.sb_peak = max(getattr(self, "sb_peak", 0), self.sb_used)
        assert self.sb_used <= self.SBUF_LIMIT, ("SBUF over budget", name, self.sb_used)
        return t, Reg(name)

    def psum(self, name, shape, dt):
        t = self.es.enter_context(self.nc.psum_tensor(name, list(shape), dt))
        rg = Reg(name)
        rg.excl = True
        return t, rg

    def _dsem(self, reg, kind):
        if reg.dsem is not None:
            assert reg.dkind == kind, (reg.name, reg.dkind, kind)
        if reg.dsem is None:
            reg.dkind = kind
            pool = self.sem_pool.setdefault(kind, [])
            if pool:
                reg.dsem, reg.dcnt = pool.pop()
            else:
                reg.dsem = self.es0.enter_context(self.nc.semaphore("d%d" % len(self.sems)))
                self.sems[id(reg.dsem)] = reg.dsem
                reg.dcnt = 0
            self.semcnt[id(reg.dsem)] = reg.dcnt
            self.dregs.append(reg)
        return reg.dsem

    def release_dregs(self, keep=()):
        keep_ids = {id(k) for k in keep}
        rest = []
        for rg in self.dregs:
            if id(rg) in keep_ids:
                rest.append(rg)
            else:
                self.sem_pool[rg.dkind].append((rg.dsem, self.semcnt[id(rg.dsem)]))
                rg.dsem = None
        self.dregs = rest

    def _record(self, o):
        preds = set()
        for b in o.r:
            preds.update(b.lw)
            if b.excl:
                for p in b.lr:
                    preds.add(p)
        for b in o.w:
            if not o.piece:
                preds.update(b.lw)
            preds.update(b.lr)
        preds.discard(o.id)
        o.preds = preds
        for b in o.r:
            b.lr.append(o.id)
        for b in o.w:
            if o.piece:
                b.lw.append(o.id)
            else:
                b.lw = [o.id]
                b.lr = []
        self.ops.append(o)

    def op(self, ename, fn, r=(), w=(), sig=True, piece=False):
        fe = _FakeEng()
        fn(fe)
        dur = _estimate(ename, fe.calls)
        g = self.group.get(ename)
        if g is None:
            g = _Op()
            g.id = self.next_id
            self.next_id += 1
            g.eng = ename
            g.fns, g.r, g.w = [], [], []
            g.piece = piece
            g.dma = None
            g.dur = 0.0
        g.fns.extend(fe.calls)
        g.r.extend(r)
        g.w.extend(w)
        g.dur += dur
        if not sig:
            self.group[ename] = g
            return
        self.group.pop(ename, None)
        self._record(g)

    def dma(self, qname, out, in_, sb, r=(), w=(), piece=False, **kw):
        assert qname not in self.group
        self._dsem(sb, "sw" if qname == "gpsimd" else "hw")
        o = _Op()
        o.id = self.next_id
        self.next_id += 1
        o.eng = qname
        o.fns = []
        o.r, o.w = list(r), list(w)
        o.piece = piece
        o.dma = (out, in_, sb, kw)
        try:
            nbytes = _nfree(out) * int(out.shape[0]) * (2 if out.dtype == BF16 else 4)
        except Exception:
            nbytes = 1 << 18
        o.dur = 0.4
        o.lat = 2.0 + nbytes / 1.5e5
        self._record(o)

    def flush(self):
        assert not self.group, "open accumulation group at flush"
        ops = self.ops
        self.ops = []
        n = len(ops)
        if n == 0:
            return
        idx = {o.id: i for i, o in enumerate(ops)}
        lpreds = [[idx[p] for p in o.preds if p in idx] for o in ops]
        succs = [[] for _ in range(n)]
        indeg = [0] * n
        for i, ps in enumerate(lpreds):
            indeg[i] = len(ps)
            for p in ps:
                succs[p].append(i)
        engs = list(self.E.keys())
        seq = {e: [] for e in engs}
        for i, o in enumerate(ops):
            seq[o.eng].append(i)
        pos = {e: 0 for e in engs}
        sched = [False] * n
        done = [0.0] * n
        ready_t = [0.0] * n
        etime = {e: 0.0 for e in engs}
        order = []
        import os as _os2
        W = int(_os2.environ.get('SCHED_W', self.WINDOW))
        LAT = float(_os2.environ.get('SCHED_LAT', self.LAT))
        remaining = n
        while remaining:
            best = None
            for e in engs:
                sq = seq[e]
                p0 = pos[e]
                while p0 < len(sq) and sched[sq[p0]]:
                    p0 += 1
                pos[e] = p0
                te = etime[e]
                cand = None
                for q in range(p0, min(len(sq), p0 + W)):
                    i = sq[q]
                    if sched[i] or indeg[i]:
                        continue
                    st = ready_t[i] if ready_t[i] > te else te
                    if cand is None or st < cand[0] - 1e-9:
                        cand = (st, i)
                        if st <= te:
                            break
                if cand is not None and (best is None or cand[0] < best[0] - 1e-9 or (abs(cand[0] - best[0]) <= 1e-9 and cand[1] < best[1])):
                    best = (cand[0], cand[1], e)
            assert best is not None, "scheduler deadlock"
            st, i, e = best
            o = ops[i]
            sched[i] = True
            remaining -= 1
            etime[e] = st + o.dur
            fin = st + o.dur + (o.lat if o.dma is not None else 0.0)
            done[i] = fin
            order.append(i)
            for s_ in succs[i]:
                indeg[s_] -= 1
                lat = LAT if ops[s_].eng != e else 0.08
                if fin + lat > ready_t[s_]:
                    ready_t[s_] = fin + lat
        self.est_time = getattr(self, "est_time", 0.0) + max(done) if done else 0.0
        for i in order:
            self._emit(ops[i])

    def _emit(self, o):
        E = self.E[o.eng]
        need = {}
        for p in o.preds:
            semkey, val, is_dma, ename = self.events[p]
            if is_dma:
                val = self.semcnt[semkey]
            elif ename == E.name and (E.name == "tensor" or not self.SAME_ENGINE_SYNC):
                continue
            if need.get(semkey, 0) < val:
                need[semkey] = val
        for semkey, val in need.items():
            if E.seen.get(semkey, 0) >= val:
                continue
            E.eng.wait_ge(self.sems[semkey], val)
            E.seen[semkey] = val
            self.nwait += 1
        if o.dma is not None:
            out, in_, sb, kw = o.dma
            inst = E.eng.dma_start(out=out, in_=in_, **kw)
            self.ninst += 1
            key = id(sb.dsem)
            self.semcnt[key] += 16
            sb.dcnt = self.semcnt[key]
            inst.then_inc(sb.dsem, 16)
            self.events[o.id] = (key, self.semcnt[key], True, E.name)
        else:
            inst = None
            for (cname, ca, ck) in o.fns:
                inst = getattr(E.eng, cname)(*ca, **ck)
                self.ninst += 1
            E.cnt += 1
            inst.then_inc(E.sem, 1)
            self.events[o.id] = (id(E.sem), E.cnt, False, E.name)

    def barrier(self):
        self.flush()
        for name, E in self.E.items():
            for oname, O in self.E.items():
                if O.cnt == 0:
                    continue
                if E.seen.get(id(O.sem), 0) < O.cnt:
                    E.eng.wait_ge(O.sem, O.cnt)
                    E.seen[id(O.sem)] = O.cnt
            for rg in self.dregs:
                key = id(rg.dsem)
                if self.semcnt[key] > 0 and E.seen.get(key, 0) < self.semcnt[key]:
                    E.eng.wait_ge(rg.dsem, self.semcnt[key])
                    E.seen[key] = self.semcnt[key]

    def finish(self, regs=()):
        self.flush()
        E = self.E["sync"]
        for name, e in self.E.items():
            if name == "sync":
                continue
            if e.cnt > 0 and E.seen.get(id(e.sem), 0) < e.cnt:
                E.eng.wait_ge(e.sem, e.cnt)
        for rg in self.dregs:
            key = id(rg.dsem)
            if self.semcnt[key] > 0:
                E.eng.wait_ge(rg.dsem, self.semcnt[key])


D = 1024
KC = 8
L = 4096
LC = 256
T = L + LC
DFF = 2816
FC = DFF // 128
DIN = 7184
DEPTH = 2
EPS = 1e-6
C_AQ, C_AK, C_AV = 0, 512, 640
C_MQ, C_MK, C_MV, C_MO, C_MI, C_MF = 768, 1024, 1280, 1792, 2304, 2312
C_HQ, C_HF, C_HV, C_HG, C_BR = 2320, 2576, 3088, 3600, 4112
NT = 256
NP = 512


def _partner(j):
    j = np.asarray(j)
    return np.where((j % 32) < 16, j + 16, j - 16)


def host_constants():
    c = {}
    c["ident"] = np.eye(128, dtype=np.float32)
    c["ones"] = np.ones((128, 128), np.float32)
    s = np.arange(128)[:, None]
    t = np.arange(128)[None, :]
    c["masku"] = (s <= t).astype(np.float32)
    c["maskl"] = (s >= t).astype(np.float32)
    rows = L // 64
    row = np.repeat(np.arange(rows, dtype=np.float32), 64)
    col = np.tile(np.arange(64, dtype=np.float32), rows)
    inv = (np.float32(10000.0) ** (-np.arange(16, dtype=np.float32) / np.float32(16))).astype(np.float32)
    ar = (row[None, :] * inv[:, None]).astype(np.float32)
    ac = (col[None, :] * inv[:, None]).astype(np.float32)
    cos = np.concatenate([np.cos(ar), np.cos(ar), np.cos(ac), np.cos(ac)], 0).astype(np.float32)
    sin = np.concatenate([-np.sin(ar), np.sin(ar), -np.sin(ac), np.sin(ac)], 0).astype(np.float32)
    c["rcos"] = cos
    c["rsin"] = sin
    rm = np.ones((64, 512), np.float32)
    rm[:, ::32] = 0.0
    c["rmask"] = rm
    return c


class K:
    pass


def build_nc(stop=None, debug=False):
    nc = bass.Bass("TRN2", target_bir_lowering=False)
    k = K()

    def din(name, shape, dt=F32):
        return nc.dram_tensor(name, list(shape), dt, kind="ExternalInput").ap()

    x_in = din("x", [L, D])
    ctx_in = din("ctx", [LC, D])
    cvec = din("cvec", [128, KC, 2])
    w_ada = din("w_ada", [DEPTH, D, 9 * D])
    b_ada = din("b_ada", [DEPTH, 128, 72])
    g_pre = din("g_pre", [DEPTH, 128, 3, 8])
    g_post = din("g_post", [DEPTH, 128, 3, 8])
    ffn_w1 = din("ffn_w1", [DEPTH, 2, D, DFF])
    ffn_w3 = din("ffn_w3", [DEPTH, 2, D, DFF])
    ffn_w2 = din("ffn_w2", [DEPTH, 2, DFF, D])
    w_in = din("w_in", [DEPTH, D, DIN])
    w_sw = din("w_sw", [DEPTH, D, 640])
    att_sink = din("att_sink", [DEPTH, 1, 8])
    ml_conv = din("ml_conv", [DEPTH, 512, 3])
    ml_fb = din("ml_fb", [DEPTH, 1, 8])
    ml_norm = din("ml_norm", [DEPTH, 1, 512])
    hg_lbl = din("hg_lbl", [64, 2, 4])
    hg_norm = din("hg_norm", [DEPTH, 1, 512])
    wb_att = din("wb_att", [DEPTH, 512, D])
    wb_ml = din("wb_ml", [DEPTH, 512, D])
    wb_hg = din("wb_hg", [DEPTH, 512, D])
    w_out = din("w_out", [DEPTH, D, D])
    c_ident = din("c_ident", [128, 128])
    c_ones = din("c_ones", [128, 128])
    c_masku = din("c_masku", [128, 128])
    c_maskl = din("c_maskl", [128, 128])
    c_rcos = din("c_rcos", [64, L])
    c_rsin = din("c_rsin", [64, L])
    c_rmask = din("c_rmask", [64, 512])

    out_d = nc.dram_tensor("out", [L, D], F32, kind="ExternalOutput").ap()
    skind = "ExternalOutput" if debug else "Internal"
    xTd = nc.dram_tensor("xTd", [KC, 128, T], F32, kind=skind).ap()
    h2d = nc.dram_tensor("h2d", [KC, 128, T], BF16, kind="Internal").ap()
    attTd = nc.dram_tensor("attTd", [512, T], BF16, kind=skind).ap()
    mlTd = nc.dram_tensor("mlTd", [512, T], BF16, kind=skind).ap()
    hgTd = nc.dram_tensor("hgTd", [512, T], BF16, kind=skind).ap()
    dbg = nc.dram_tensor("dbg", [128, 1024], F32, kind="ExternalOutput").ap() if debug else None
    w_inb = nc.dram_tensor("w_inb", [DEPTH, D, DIN], BF16, kind="Internal").ap()
    w_swb = nc.dram_tensor("w_swb", [DEPTH, D, 640], BF16, kind="Internal").ap()
    wb_attb = nc.dram_tensor("wb_attb", [DEPTH, 512, D], BF16, kind="Internal").ap()
    wb_mlb = nc.dram_tensor("wb_mlb", [DEPTH, 512, D], BF16, kind="Internal").ap()
    wb_hgb = nc.dram_tensor("wb_hgb", [DEPTH, 512, D], BF16, kind="Internal").ap()
    w_outb = nc.dram_tensor("w_outb", [DEPTH, D, D], BF16, kind="Internal").ap()

    with ExitStack() as es:
        fw = FW(nc, es)
        op, dma = fw.op, fw.dma

        ident_f, ident_fr = fw.sbuf("ident_f", [128, 128], F32)
        ident_b, ident_br = fw.sbuf("ident_b", [128, 128], BF16)
        ones_f, ones_fr = fw.sbuf("ones_f", [128, 128], F32)
        ones_b, ones_br = fw.sbuf("ones_b", [128, 128], BF16)
        masku_f, masku_fr = fw.sbuf("masku_f", [128, 128], F32)
        maskl_f, maskl_fr = fw.sbuf("maskl_f", [128, 128], F32)
        masku_b, masku_br = fw.sbuf("masku_b", [128, 128], BF16)
        maskl_b, maskl_br = fw.sbuf("maskl_b", [128, 128], BF16)
        dma("sync", ident_f[:], c_ident, sb=ident_fr, w=[ident_fr])
        op("vector", lambda e: e.tensor_copy(ident_b[:], ident_f[:]), r=[ident_fr], w=[ident_br])
        dma("sync", ones_f[:], c_ones, sb=ones_fr, w=[ones_fr])
        op("vector", lambda e: e.tensor_copy(ones_b[:], ones_f[:]), r=[ones_fr], w=[ones_br])
        dma("sync", masku_f[:], c_masku, sb=masku_fr, w=[masku_fr])
        dma("sync", maskl_f[:], c_maskl, sb=maskl_fr, w=[maskl_fr])
        op("vector", lambda e: e.tensor_copy(masku_b[:], masku_f[:]), r=[masku_fr], w=[masku_br])
        op("vector", lambda e: e.tensor_copy(maskl_b[:], maskl_f[:]), r=[maskl_fr], w=[maskl_br])
        cstg = []
        cst_state = {"i": 0}

        def set_staging(tiles):
            cstg[:] = tiles

        wcast_reg = [fw.reg("wcast%d" % i) for i in range(DEPTH)]

        def precast_units(l):
            u = []
            for kk in range(KC):
                rows = slice(kk * 128, (kk + 1) * 128)
                for c0 in range(0, DIN, 1024):
                    n = min(1024, DIN - c0)
                    u.append((w_inb[l, rows, c0:c0 + n], w_in[l, rows, c0:c0 + n], n))
                u.append((w_swb[l, rows, :], w_sw[l, rows, :], 640))
                u.append((w_outb[l, rows, :], w_out[l, rows, :], D))
            for kk in range(4):
                rows = slice(kk * 128, (kk + 1) * 128)
                u.append((wb_attb[l, rows, :], wb_att[l, rows, :], D))
                u.append((wb_mlb[l, rows, :], wb_ml[l, rows, :], D))
                u.append((wb_hgb[l, rows, :], wb_hg[l, rows, :], D))
            return u

        pc_state = {"i": 0}

        def precast_emit(l, units, fslots, bslots):
            for (dst, src, n) in units:
                i = pc_state["i"]
                pc_state["i"] += 1
                fs, fsr = fslots[i % len(fslots)]
                bs, bsr = bslots[i % len(bslots)]
                dma("sync", fs[:, 0:n], src, sb=fsr, w=[fsr])
                eng = ("vector", "gpsimd", "scalar")[i % 3]
                if eng == "scalar":
                    op("scalar", lambda e: e.copy(bs[:, 0:n], fs[:, 0:n]), r=[fsr], w=[bsr])
                else:
                    op(eng, lambda e: e.tensor_copy(bs[:, 0:n], fs[:, 0:n]), r=[fsr], w=[bsr])
                dma("sync", dst, bs[:, 0:n], sb=bsr, r=[bsr], w=[wcast_reg[l]], piece=True)

        def alloc_hps(ntok):
            hps_ = [fw.sbuf("hp%d" % i, [128, KC, ntok], BF16) for i in range(2)]
            cstg[:] = [(t[:].rearrange("p a b -> p (a b)").bitcast(F32)[:, 0:1024], r_) for (t, r_) in hps_]
            return hps_

        def alloc_staging(n=3):
            cstg[:] = [(t[:, :], r_) for (t, r_) in [fw.sbuf("cstg%d" % i, [128, 1024], F32) for i in range(n)]]

        def castload(dst_ap, src_ap, dstr, ncols):
            i = cst_state["i"]
            cst_state["i"] += 1
            st, str_ = cstg[i % len(cstg)]
            dma("sync", st[:, 0:ncols], src_ap, sb=str_, w=[str_])
            eng = ("vector", "scalar", "vector", "scalar", "gpsimd")[i % 5]
            if eng == "scalar":
                op("scalar", lambda e: e.copy(dst_ap, st[:, 0:ncols]), r=[str_], w=[dstr], piece=True)
            else:
                op(eng, lambda e: e.tensor_copy(dst_ap, st[:, 0:ncols]), r=[str_], w=[dstr], piece=True)
        CONST = [ident_fr, ident_br, ones_fr, ones_br, masku_fr, maskl_fr, masku_br, maskl_br]

        sc, scr = fw.sbuf("sc", [128, KC, 2], F32)
        dma("sync", sc[:], cvec, sb=scr, w=[scr])
        op("scalar", lambda e: e.activation(sc[:], sc[:], AF.Silu), r=[scr], w=[scr])
        modT, modr = fw.sbuf("modT", [128, 72, 2], F32)
        gpre, gprer = fw.sbuf("gpre", [128, 3, 8], F32)
        gpost, gpostr = fw.sbuf("gpost", [128, 3, 8], F32)
        Asc, Ascr = fw.sbuf("Asc", [128, 3, 8, 2], F32)
        Gsc, Gscr = fw.sbuf("Gsc", [128, 3, 8, 2], F32)
        mtmp, mtmpr = fw.sbuf("mtmp", [128, 8, 2], F32)
        bada, badar = fw.sbuf("bada", [128, 72], F32)
        modv = modT[:].rearrange("p (j c) w -> p j c w", c=8)

        PS = [fw.psum("ps%d" % i, [128, 512], F32) for i in range(7)]
        psB, psBr = fw.psum("psB", [128, 1024], BF16)

        state = {"phase": 0}

        class Phase:
            def __enter__(self_p):
                self_p.sb0 = getattr(fw, "sb_used", 0)
                self_p.es2 = ExitStack()
                self_p.es2.__enter__()
                self_p.old = fw.es
                fw.es = self_p.es2
                return self_p

            def __exit__(self_p, *a):
                fw.barrier()
                fw.release_dregs()
                fw.es = self_p.old
                fw.sb_used = self_p.sb0
                self_p.es2.__exit__(None, None, None)
                return False

        def mod_phase(l):
            with Phase():
                wA = [fw.sbuf("wA%d" % i, [128, KC, 1152], F32) for i in range(2)]
                dma("sync", bada[:], b_ada[l], sb=badar, w=[badar])
                dma("sync", gpre[:], g_pre[l], sb=gprer, w=[gprer])
                dma("sync", gpost[:], g_post[l], sb=gpostr, w=[gpostr])
                pm, pmr = PS[0]
                for grp in range(8):
                    wt, wr = wA[grp % 2]
                    for kk in range(KC):
                        dma("sync", wt[:, kk, :], w_ada[l, kk * 128:(kk + 1) * 128, grp * 1152:(grp + 1) * 1152],
                            sb=wr, w=[wr], piece=True)
                    for mm in range(9):
                        m = grp * 9 + mm
                        for kk in range(KC):
                            op("tensor", lambda e: e.matmul(pm[:, 2 * m:2 * m + 2], wt[:, kk, mm * 128:(mm + 1) * 128],
                                                            sc[:, kk, :], start=(kk == 0), stop=(kk == KC - 1)),
                               r=[wr, scr], w=[pmr], sig=(kk == KC - 1))
                op("vector", lambda e: e.tensor_tensor(modT[:], pm[:, 0:144].rearrange("p (m w) -> p m w", w=2),
                                                       bada[:].unsqueeze(2).to_broadcast([128, 72, 2]), ALU.add),
                   r=[pmr, badar], w=[modr])
                for j in range(3):
                    wgt = 1.0 if j == 1 else 0.5
                    op("vector", lambda e: e.tensor_scalar(mtmp[:], modv[:, 3 * j + 1], 1.0, None, ALU.add), r=[modr], w=[mtmpr])
                    op("vector", lambda e: e.tensor_tensor(Asc[:, j], mtmp[:], gpre[:, j, :].unsqueeze(2).to_broadcast([128, 8, 2]), ALU.mult),
                       r=[mtmpr, gprer], w=[Ascr])
                    op("vector", lambda e: e.scalar_tensor_tensor(Gsc[:, j], modv[:, 3 * j + 2], wgt,
                                                                  gpost[:, j, :].unsqueeze(2).to_broadcast([128, 8, 2]), ALU.mult, ALU.mult),
                       r=[modr, gpostr], w=[Gscr])
                if l == 0:
                    fsl = [fw.sbuf("pcf%d" % i, [128, 1024], F32) for i in range(2)]
                    bsl = [fw.sbuf("pcb%d" % i, [128, 1024], BF16) for i in range(2)]
                    precast_emit(0, precast_units(0), [(t_[:, :], r_) for t_, r_ in fsl], [(t_[:, :], r_) for t_, r_ in bsl])

        def rstd_from_sq(sq, sqr, rs, rsr, N, pst):
            ps, psr = pst
            for cc in range(KC):
                op("tensor", lambda e: e.matmul(ps[:, :N], ones_b[:], sq[:, cc, :N], start=(cc == 0), stop=(cc == KC - 1)),
                   r=[sqr, ones_br], w=[psr], sig=(cc == KC - 1))
            op("scalar", lambda e: e.activation(rs[:, :N], ps[:, :N], AF.Sqrt, bias=epsb[:, 0:1], scale=1.0 / D), r=[psr, epsr], w=[rsr])
            op("vector", lambda e: e.reciprocal(rs[:, :N], rs[:, :N]), r=[rsr], w=[rsr])

        epsb, epsr = fw.sbuf("epsb", [128, 1], F32)
        op("vector", lambda e: e.memset(epsb[:], EPS), w=[epsr])

        def load_tile_tokmajor(src, tok0, N, xT, xTr, tm, tmr):
            for blk in range(N // 128):
                dma("sync", tm, src[tok0 + blk * 128: tok0 + (blk + 1) * 128, :], sb=tmr, w=[tmr])
                for half in range(2):
                    ps, psr = PS[4 + half]
                    for q in range(4):
                        cc = half * 4 + q
                        op("tensor", lambda e: e.transpose(ps[:, q * 128:(q + 1) * 128], tm[:, cc * 128:(cc + 1) * 128], ident_f[:]),
                           r=[tmr, ident_fr], w=[psr], sig=(q == 3))
                    eng = "scalar" if half == 0 else "vector"
                    src_v = ps[:, :].rearrange("p (q t) -> p q t", q=4)
                    dst_v = xT[:, half * 4:(half + 1) * 4, blk * 128:(blk + 1) * 128]
                    if eng == "scalar":
                        op("scalar", lambda e: e.copy(dst_v, src_v), r=[psr], w=[xTr])
                    else:
                        op("vector", lambda e: e.tensor_copy(dst_v, src_v), r=[psr], w=[xTr])

        def store_tile_tokmajor(dst, tok0, N, xT, xTr, tm, tmr):
            for blk in range(N // 128):
                for half in range(2):
                    ps, psr = PS[4 + half]
                    for q in range(4):
                        cc = half * 4 + q
                        op("tensor", lambda e: e.transpose(ps[:, q * 128:(q + 1) * 128], xT[:, cc, blk * 128:(blk + 1) * 128], ident_f[:]),
                           r=[xTr, ident_fr], w=[psr], sig=(q == 3))
                    if half == 0:
                        op("scalar", lambda e: e.copy(tm[:, 0:512], ps[:, :]), r=[psr], w=[tmr])
                    else:
                        op("vector", lambda e: e.tensor_copy(tm[:, 512:1024], ps[:, :]), r=[psr], w=[tmr])
                dma("sync", dst[tok0 + blk * 128: tok0 + (blk + 1) * 128, :], tm, sb=tmr, r=[tmr])

        def tile_info(ti):
            if ti < L // NT:
                return ti * NT, 0
            return L, 1

        xTd_v = xTd.rearrange("c p t -> p c t")
        h2d_v = h2d.rearrange("c p t -> p c t")

        def pre_norm(j, wi, xT, xTr, sq, sqr, rs, rsr, tmp, tmpr, h, hr, N):
            op("scalar", lambda e: e.activation(sq[:, :, :N], xT[:, :, :N], AF.Square), r=[xTr], w=[sqr])
            rstd_from_sq(sq, sqr, rs, rsr, N, PS[6])
            for cc in range(KC):
                tt, ttr = tmp[cc % 2], tmpr[cc % 2]
                op("vector", lambda e: e.scalar_tensor_tensor(tt[:, :N], xT[:, cc, :N], Asc[:, j, cc, wi:wi + 1], rs[:, :N], ALU.mult, ALU.mult),
                   r=[xTr, Ascr, rsr], w=[ttr])
                op("scalar", lambda e: e.activation(h[:, cc, :N], tt[:, :N], AF.Identity, bias=modv[:, 3 * j, cc, wi:wi + 1], scale=1.0),
                   r=[ttr, modr], w=[hr])

        def post_norm_add(j, wi, ysb, ysbr, sq, sqr, rs, rsr, tmp, tmpr, xT, xTr, N):
            rstd_from_sq(sq, sqr, rs, rsr, N, PS[6])
            for m in range(KC):
                tt, ttr = tmp[m % 2], tmpr[m % 2]
                op("vector", lambda e: e.scalar_tensor_tensor(tt[:, :N], ysb[:, m, :N], Gsc[:, j, m, wi:wi + 1], rs[:, :N], ALU.mult, ALU.mult),
                   r=[ysbr, Gscr, rsr], w=[ttr])
                op("gpsimd", lambda e: e.tensor_tensor(xT[:, m, :N], tt[:, :N], xT[:, m, :N], ALU.add), r=[ttr, xTr], w=[xTr])

        def ffn_phase(l, which):
            j = 0 if which == 0 else 2
            first = (l == 0 and which == 0)
            final = (l == DEPTH - 1 and which == 1)
            with Phase():
                w1, w1r = fw.sbuf("w1", [128, KC, DFF], BF16)
                w3, w3r = fw.sbuf("w3", [128, KC, DFF], BF16)
                w2, w2r = fw.sbuf("w2", [128, FC, D], BF16)
                ysb, ysbr = fw.sbuf("ysb", [128, KC, NT], F32)
                xTs = [fw.sbuf("xT%d" % i, [128, KC, NT], F32) for i in range(2)]
                fuse_h2 = (which == 0)
                tm, tmr = (None, None)
                if fuse_h2:
                    h2t, h2tr = fw.sbuf("h2t", [128, KC, NT], BF16)
                    tm_t, tmr = fw.sbuf("tm", [128, D], F32)
                    tm = tm_t[:, :]
                    set_staging([(tm, tmr), (h2t[:].rearrange("p c t -> p (c t)").bitcast(F32)[:, 0:1024], h2tr)])
                else:
                    stg3 = [fw.sbuf("wstg%d" % i, [128, 1024], F32) for i in range(3)]
                    set_staging([(t_[:, :], r_) for (t_, r_) in stg3])
                    if final:
                        tm, tmr = stg3[0][0][:, :], stg3[0][1]
                w1rs = [fw.reg("w1b%d" % i) for i in range(3)]
                w3rs = [fw.reg("w3b%d" % i) for i in range(3)]
                w2rs = [fw.reg("w2b%d" % i) for i in range(3)]
                for bi, c0 in enumerate(range(0, DFF, 1024)):
                    c1 = min(DFF, c0 + 1024)
                    for kk in range(KC):
                        castload(w1[:, kk, c0:c1], ffn_w1[l, which, kk * 128:(kk + 1) * 128, c0:c1], w1rs[bi], c1 - c0)
                        castload(w3[:, kk, c0:c1], ffn_w3[l, which, kk * 128:(kk + 1) * 128, c0:c1], w3rs[bi], c1 - c0)
                for f in range(FC):
                    castload(w2[:, f, :], ffn_w2[l, which, f * 128:(f + 1) * 128, :], w2rs[f // 8], D)
                h, hr = fw.sbuf("h", [128, KC, NT], BF16)
                a, ar = fw.sbuf("a", [128, FC, NT], BF16)
                sqa = [fw.sbuf("sqa%d" % i, [128, NT], BF16) for i in range(2)]
                sqb = [fw.sbuf("sqb%d" % i, [128, NT], BF16) for i in range(2)]
                if fuse_h2:
                    sqc = [fw.sbuf("sqc%d" % i, [128, NT], BF16) for i in range(2)]
                    tmpc = [fw.sbuf("tmpc%d" % i, [128, NT], F32) for i in range(2)]
                    rs3, rs3r = fw.sbuf("rs3", [128, NT], F32)
                tmpa = [fw.sbuf("tmpa%d" % i, [128, NT], F32) for i in range(2)]
                tmpb = [fw.sbuf("tmpb%d" % i, [128, NT], F32) for i in range(2)]
                ss_ = [fw.sbuf("ss%d" % i, [128, NT], F32) for i in range(2)]
                rs1, rs1r = fw.sbuf("rs1", [128, NT], F32)
                rs2, rs2r = fw.sbuf("rs2", [128, NT], F32)
                tiles = []
                for ti in range(L // NT + 1):
                    tok0, wi = tile_info(ti)
                    if final and wi == 1:
                        continue
                    tiles.append((ti, tok0, wi))
                N = NT

                def issue_load(idx):
                    ti, tok0, wi = tiles[idx]
                    xT, xTr = xTs[idx % 2]
                    if first:
                        pass
                    else:
                        dma("sync", xT[:, :, :N], xTd_v[:, :, tok0:tok0 + N], sb=xTr, r=[xTd_reg[ti]], w=[xTr])

                def prep(idx):
                    ti, tok0, wi = tiles[idx]
                    xT, xTr = xTs[idx % 2]
                    if first:
                        load_tile_tokmajor(x_in if wi == 0 else ctx_in, tok0 if wi == 0 else 0, N, xT, xTr, tm, tmr)
                    ps, psr = PS[6]
                    for cc in range(KC):
                        sq_, sq_r = sqa[cc % 2]
                        op("scalar", lambda e: e.activation(sq_[:, :N], xT[:, cc, :N], AF.Square), r=[xTr], w=[sq_r])
                        op("tensor", lambda e: e.matmul(ps[:, :N], ones_b[:], sq_[:, :N], start=(cc == 0), stop=(cc == KC - 1)),
                           r=[sq_r, ones_br], w=[psr], sig=True)
                    op("scalar", lambda e: e.activation(rs1[:, :N], ps[:, :N], AF.Sqrt, bias=epsb[:, 0:1], scale=1.0 / D), r=[psr, epsr], w=[rs1r])
                    op("vector", lambda e: e.reciprocal(rs1[:, :N], rs1[:, :N]), r=[rs1r], w=[rs1r])
                    for cc in range(KC):
                        tt, ttr = tmpa[cc % 2]
                        op("vector", lambda e: e.scalar_tensor_tensor(tt[:, :N], xT[:, cc, :N], Asc[:, j, cc, wi:wi + 1], rs1[:, :N], ALU.mult, ALU.mult),
                           r=[xTr, Ascr, rs1r], w=[ttr])
                        op("scalar", lambda e: e.activation(h[:, cc, :N], tt[:, :N], AF.Identity, bias=modv[:, 3 * j, cc, wi:wi + 1], scale=1.0),
                           r=[ttr, modr], w=[hr])

                issue_load(0)
                if len(tiles) > 1:
                    issue_load(1)
                prep(0)
                for idx, (ti, tok0, wi) in enumerate(tiles):
                    xT, xTr = xTs[idx % 2]
                    for f in range(FC):
                        pu, pur = PS[f % 2]
                        pv, pvr = PS[2 + f % 2]
                        s_, s_r = ss_[f % 2]
                        for kk in range(KC):
                            op("tensor", lambda e: e.matmul(pu[:, :N], w1[:, kk, f * 128:(f + 1) * 128], h[:, kk, :N], start=(kk == 0), stop=(kk == KC - 1)),
                               r=[w1rs[f // 8], hr], w=[pur], sig=(kk == KC - 1))
                        for kk in range(KC):
                            op("tensor", lambda e: e.matmul(pv[:, :N], w3[:, kk, f * 128:(f + 1) * 128], h[:, kk, :N], start=(kk == 0), stop=(kk == KC - 1)),
                               r=[w3rs[f // 8], hr], w=[pvr], sig=(kk == KC - 1))
                        op("scalar", lambda e: e.activation(s_[:, :N], pu[:, :N], AF.Silu), r=[pur], w=[s_r])
                        op("vector", lambda e: e.tensor_tensor(a[:, f, :N], s_[:, :N], pv[:, :N], ALU.mult), r=[s_r, pvr], w=[ar])
                    if idx + 1 < len(tiles):
                        prep(idx + 1)
                    pst, pstr = PS[6]
                    for m in range(KC):
                        py, pyr = PS[4 + m % 2]
                        for f in range(FC):
                            op("tensor", lambda e: e.matmul(py[:, :N], w2[:, f, m * 128:(m + 1) * 128], a[:, f, :N], start=(f == 0), stop=(f == FC - 1)),
                               r=[w2rs[f // 8], ar], w=[pyr], sig=(f == FC - 1))
                        op("scalar", lambda e: e.copy(ysb[:, m, :N], py[:, :N]), r=[pyr], w=[ysbr])
                        sq_, sq_r = sqb[m % 2]
                        op("gpsimd", lambda e: e.tensor_tensor(sq_[:, :N], ysb[:, m, :N], ysb[:, m, :N], ALU.mult), r=[ysbr], w=[sq_r])
                        op("tensor", lambda e: e.matmul(pst[:, :N], ones_b[:], sq_[:, :N], start=(m == 0), stop=(m == KC - 1)),
                           r=[sq_r, ones_br], w=[pstr], sig=True)
                    op("scalar", lambda e: e.activation(rs2[:, :N], pst[:, :N], AF.Sqrt, bias=epsb[:, 0:1], scale=1.0 / D), r=[pstr, epsr], w=[rs2r])
                    op("vector", lambda e: e.reciprocal(rs2[:, :N], rs2[:, :N]), r=[rs2r], w=[rs2r])
                    for m in range(KC):
                        tt, ttr = tmpb[m % 2]
                        op("vector", lambda e: e.scalar_tensor_tensor(tt[:, :N], ysb[:, m, :N], Gsc[:, j, m, wi:wi + 1], rs2[:, :N], ALU.mult, ALU.mult),
                           r=[ysbr, Gscr, rs2r], w=[ttr])
                        op("gpsimd", lambda e: e.tensor_tensor(xT[:, m, :N], tt[:, :N], xT[:, m, :N], ALU.add), r=[ttr, xTr], w=[xTr])
                    if final:
                        store_tile_tokmajor(out_d, tok0, N, xT, xTr, tm, tmr)
                    else:
                        dma("sync", xTd_v[:, :, tok0:tok0 + N], xT[:, :, :N], sb=xTr, r=[xTr], w=[xTd_reg[ti]])
                    if fuse_h2:
                        ps3, ps3r = PS[6]
                        for cc in range(KC):
                            sq_, sq_r = sqc[cc % 2]
                            op("scalar", lambda e: e.activation(sq_[:, :N], xT[:, cc, :N], AF.Square), r=[xTr], w=[sq_r])
                            op("tensor", lambda e: e.matmul(ps3[:, :N], ones_b[:], sq_[:, :N], start=(cc == 0), stop=(cc == KC - 1)),
                               r=[sq_r, ones_br], w=[ps3r], sig=True)
                        op("scalar", lambda e: e.activation(rs3[:, :N], ps3[:, :N], AF.Sqrt, bias=epsb[:, 0:1], scale=1.0 / D), r=[ps3r, epsr], w=[rs3r])
                        op("vector", lambda e: e.reciprocal(rs3[:, :N], rs3[:, :N]), r=[rs3r], w=[rs3r])
                        for cc in range(KC):
                            tt, ttr = tmpc[cc % 2]
                            op("vector", lambda e: e.scalar_tensor_tensor(tt[:, :N], xT[:, cc, :N], Asc[:, 1, cc, wi:wi + 1], rs3[:, :N], ALU.mult, ALU.mult),
                               r=[xTr, Ascr, rs3r], w=[ttr])
                            op("scalar", lambda e: e.activation(h2t[:, cc, :N], tt[:, :N], AF.Identity, bias=modv[:, 3, cc, wi:wi + 1], scale=1.0),
                               r=[ttr, modr], w=[h2tr])
                        dma("sync", h2d_v[:, :, tok0:tok0 + N], h2t[:, :, :N], sb=h2tr, r=[h2tr])
                    if idx + 2 < len(tiles):
                        issue_load(idx + 2)
                    if which == 1 and l + 1 < DEPTH:
                        if idx == 0:
                            pc_units = precast_units(l + 1)
                            b2 = stg3[2][0][:, :].bitcast(BF16)
                            pc_f = [(stg3[0][0][:, :], stg3[0][1]), (stg3[1][0][:, :], stg3[1][1])]
                            pc_b = [(b2[:, 0:1024], stg3[2][1])]
                        per = (len(pc_units) + len(tiles) - 1) // len(tiles)
                        precast_emit(l + 1, pc_units[idx * per:(idx + 1) * per], pc_f, pc_b)

        xTd_reg = [fw.reg("xTd%d" % i) for i in range(L // NT + 1)]

        def h2_phase(l):
            with Phase():
                xTs = [fw.sbuf("xT%d" % i, [128, KC, NP], F32) for i in range(2)]
                hs = [fw.sbuf("h%d" % i, [128, KC, NP], BF16) for i in range(2)]
                sq, sqr = fw.sbuf("sq", [128, KC, NP], BF16)
                tmp_ = [fw.sbuf("tmp%d" % i, [128, NP], F32) for i in range(2)]
                tmp, tmpr = [t[0] for t in tmp_], [t[1] for t in tmp_]
                rs, rsr = fw.sbuf("rs", [128, NP], F32)
                for pi in range(9):
                    tok0, N, wi = (pi * NP, NP, 0) if pi < 8 else (L, LC, 1)
                    xT, xTr = xTs[pi % 2]
                    h, hr = hs[pi % 2]
                    dma("sync", xT[:, :, :N], xTd_v[:, :, tok0:tok0 + N], sb=xTr, w=[xTr])
                    pre_norm(1, wi, xT, xTr, sq, sqr, rs, rsr, tmp, tmpr, h, hr, N)
                    dma("sync", h2d_v[:, :, tok0:tok0 + N], h[:, :, :N], sb=hr, r=[hr])

        def pieces():
            for pi in range(9):
                yield (pi, pi * NP, NP) if pi < 8 else (pi, L, LC)

        def load_w(dst, dstr, src_cols_ap_fn, ncols):
            for kk in range(KC):
                dma("sync", dst[:, kk, :ncols], src_cols_ap_fn(kk), sb=dstr, r=wcast_reg, w=[dstr], piece=True)

        def load_w2(dst, dstr, src2d, ncols):
            dma("sync", dst[:, :, :ncols], src2d.rearrange("(k p) c -> p k c", p=128), sb=dstr, r=wcast_reg, w=[dstr])

        def fm(ps, psr, wt, wtr, c0, M, hp, hpr, N):
            for kk in range(KC):
                op("tensor", lambda e: e.matmul(ps[0:M, :N], wt[:, kk, c0:c0 + M], hp[:, kk, :N], start=(kk == 0), stop=(kk == KC - 1)),
                   r=[wtr, hpr], w=[psr], sig=(kk == KC - 1))

        def tmj(pso, psr, wt, wtr, c0, ncol, hp, hpr, t0, nt):
            for kk in range(KC):
                op("tensor", lambda e: e.matmul(pso, hp[:, kk, t0:t0 + nt], wt[:, kk, c0:c0 + ncol], start=(kk == 0), stop=(kk == KC - 1)),
                   r=[wtr, hpr], w=[psr], sig=(kk == KC - 1))

        def att_phase(l, g):
            with Phase():
                hps = alloc_hps(NP)
                wq, wqr = fw.sbuf("wq", [128, KC, 256], BF16)
                wqs, wqsr = fw.sbuf("wqs", [128, KC, 256], BF16)
                wk, wkr = fw.sbuf("wk", [128, KC, 64], BF16)
                wks, wksr = fw.sbuf("wks", [128, KC, 64], BF16)
                wv, wvr = fw.sbuf("wv", [128, KC, 64], BF16)
                load_w2(wq, wqr, w_inb[l][:, C_AQ + g * 256: C_AQ + (g + 1) * 256], 256)
                load_w2(wqs, wqsr, w_swb[l][:, g * 256:(g + 1) * 256], 256)
                load_w2(wk, wkr, w_inb[l][:, C_AK + g * 64: C_AK + (g + 1) * 64], 64)
                load_w2(wks, wksr, w_swb[l][:, 512 + g * 64: 512 + (g + 1) * 64], 64)
                load_w2(wv, wvr, w_inb[l][:, C_AV + g * 64: C_AV + (g + 1) * 64], 64)
                rcos, rcosr = fw.sbuf("rcos", [64, L], F32)
                rsin, rsinr = fw.sbuf("rsin", [64, L], F32)
                dma("sync", rcos[:], c_rcos, sb=rcosr, w=[rcosr])
                dma("sync", rsin[:], c_rsin, sb=rsinr, w=[rsinr])
                esk, eskr = fw.sbuf("esk", [64, 8], F32)
                dma("sync", esk[:], att_sink[l].partition_broadcast(64), sb=eskr, w=[eskr])
                op("scalar", lambda e: e.activation(esk[:], esk[:], AF.Exp), r=[eskr], w=[eskr])
                QT, _ = fw.sbuf("QT", [64, 4, T], BF16)
                KT, _ = fw.sbuf("KT", [64, T], BF16)
                Vt, _ = fw.sbuf("Vt", [128, T // 128, 64], BF16)
                QTr = [fw.reg("QT%d" % i) for i in range(9)]
                KTr = [fw.reg("KT%d" % i) for i in range(9)]
                Vtr = [fw.reg("Vt%d" % i) for i in range(9)]

                def pc_of(blk):
                    return 8 if blk >= 32 else blk // 4
                t1s = [fw.sbuf("t1_%d" % i, [64, NP], F32) for i in range(2)]
                t2s = [fw.sbuf("t2_%d" % i, [64, NP], F32) for i in range(2)]
                cnt = 0
                plist = list(pieces())
                plist = [plist[8]] + plist[:8]
                for pidx, (pi, tok0, N) in enumerate(plist):
                    hp, hpr = hps[pidx % 2]
                    dma("sync", hp[:, :, :N], h2d_v[:, :, tok0:tok0 + N], sb=hpr, w=[hpr])
                    for r_ in range(5):
                        wa, war, wb_, wbr_, c0 = (wq, wqr, wqs, wqsr, r_ * 64) if r_ < 4 else (wk, wkr, wks, wksr, 0)
                        dst = QT[:, r_, tok0:tok0 + N] if r_ < 4 else KT[:, tok0:tok0 + N]
                        dstr = QTr[pi] if r_ < 4 else KTr[pi]
                        p1, p1r = PS[cnt % 2]
                        p2, p2r = PS[2 + cnt % 2]
                        t1, t1r = t1s[cnt % 2]
                        t2, t2r = t2s[cnt % 2]
                        cnt += 1
                        fm(p1, p1r, wa, war, c0, 64, hp, hpr, N)
                        if tok0 < L:
                            fm(p2, p2r, wb_, wbr_, c0, 64, hp, hpr, N)
                            op("vector", lambda e: e.tensor_tensor(t1[:, :N], p1[0:64, :N], rcos[:, tok0:tok0 + N], ALU.mult), r=[p1r, rcosr], w=[t1r])
                            op("vector", lambda e: e.tensor_tensor(t2[:, :N], p2[0:64, :N], rsin[:, tok0:tok0 + N], ALU.mult), r=[p2r, rsinr], w=[t2r])
                            op("gpsimd", lambda e: e.tensor_tensor(dst, t1[:, :N], t2[:, :N], ALU.add), r=[t1r, t2r], w=[dstr])
                        else:
                            op("scalar", lambda e: e.copy(dst, p1[0:64, :N]), r=[p1r], w=[dstr])
                    for b_ in range(N // 128):
                        blk = tok0 // 128 + b_
                        pv, pvr = PS[4 + blk % 2]
                        tmj(pv[:, 0:64], pvr, wv, wvr, 0, 64, hp, hpr, b_ * 128, 128)
                        op("scalar", lambda e: e.copy(Vt[:, blk, :], pv[:, 0:64]), r=[pvr], w=[Vtr[pi]])
                Es = [fw.sbuf("E%d" % i, [128, 512], BF16) for i in range(3)]
                dtmp, dtmpr = fw.sbuf("dtmp", [64, 512], F32)
                osts = [fw.sbuf("ost%d" % i, [64, 4, 512], BF16) for i in range(2)]
                qblocks = list(range(32)) + ([32, 33] if l < DEPTH - 1 else [])
                ecnt = 0
                for qi, i in enumerate(qblocks):
                    if i < 32:
                        keys = [j for j in (i - 1, i, i + 1) if 0 <= j < 32] + [32, 33]
                    else:
                        keys = [32, 33]
                    pn, pnr = PS[3 + qi % 2]
                    pd, pdr = PS[5 + qi % 2]
                    ost, ostr = osts[(qi // 4) % 2]
                    for idx, j in enumerate(keys):
                        ps, psr = PS[ecnt % 3]
                        E, Er = Es[ecnt % 3]
                        ecnt += 1
                        op("tensor", lambda e: e.matmul(ps[:, :].rearrange("p (r q) -> p r q", r=4), KT[:, j * 128:(j + 1) * 128],
                                                        QT[:, :, i * 128:(i + 1) * 128], start=True, stop=True),
                           r=[KTr[pc_of(j)], QTr[pc_of(i)]], w=[psr])
                        op("scalar", lambda e: e.activation(E[:], ps[:], AF.Exp, scale=0.125), r=[psr], w=[Er])
                        if i < 32 and j == i - 1:
                            op("vector", lambda e: e.tensor_tensor(E[:].rearrange("p (r q) -> p r q", r=4), E[:].rearrange("p (r q) -> p r q", r=4),
                                                                   maskl_b[:].unsqueeze(1).to_broadcast([128, 4, 128]), ALU.mult), r=[Er, maskl_br], w=[Er])
                        if i < 32 and j == i + 1 and j < 32:
                            op("vector", lambda e: e.tensor_tensor(E[:].rearrange("p (r q) -> p r q", r=4), E[:].rearrange("p (r q) -> p r q", r=4),
                                                                   masku_b[:].unsqueeze(1).to_broadcast([128, 4, 128]), ALU.mult), r=[Er, masku_br], w=[Er])
                        last = (idx == len(keys) - 1)
                        op("tensor", lambda e: e.matmul(pn[0:64, :], Vt[:, j, :], E[:], start=(idx == 0), stop=last), r=[Vtr[pc_of(j)], Er], w=[pnr], sig=last)
                        op("tensor", lambda e: e.matmul(pd[0:64, :], ones_b[:, 0:64], E[:], start=(idx == 0), stop=last), r=[ones_br, Er], w=[pdr], sig=last)
                    op("vector", lambda e: e.tensor_tensor(dtmp[:].rearrange("p (r q) -> p r q", r=4), pd[0:64, :].rearrange("p (r q) -> p r q", r=4),
                                                           esk[:, g * 4:(g + 1) * 4].unsqueeze(2).to_broadcast([64, 4, 128]), ALU.add), r=[pdr, eskr], w=[dtmpr])
                    op("scalar", lambda e: e.activation(dtmp[:], dtmp[:], AF.Ln), r=[dtmpr], w=[dtmpr])
                    op("scalar", lambda e: e.activation(dtmp[:], dtmp[:], AF.Exp, scale=-1.0), r=[dtmpr], w=[dtmpr])
                    sl = (qi % 4) * 128
                    op("vector", lambda e: e.tensor_tensor(ost[:, :, sl:sl + 128], pn[0:64, :].rearrange("p (r q) -> p r q", r=4),
                                                           dtmp[:].rearrange("p (r q) -> p r q", r=4), ALU.mult), r=[pnr, dtmpr], w=[ostr])
                    endgrp = (qi % 4 == 3) or (qi == len(qblocks) - 1)
                    if endgrp:
                        nb = qi % 4 + 1
                        t0 = (i - nb + 1) * 128
                        dma("sync", attTd[g * 256:(g + 1) * 256, t0:t0 + nb * 128].rearrange("(r d) t -> d r t", d=64),
                            ost[:, :, 0:nb * 128], sb=ostr, r=[ostr])

        NB = T // 128
        gA, gAr = fw.sbuf("gA", [128, 2, NB, 4], F32)
        gEb, gEbr = fw.sbuf("gEb", [128, 2, NB, 4], F32)
        gEbL, gEbLr = fw.sbuf("gEbL", [128, 2, NB, 4], F32)
        gW, gWr = fw.sbuf("gW", [128, 2, NB, 4], F32)
        ln8, ln8r = fw.sbuf("ln8", [128, 1], F32)
        op("vector", lambda e: e.memset(ln8[:], float(np.log(0.125))), w=[ln8r])
        oneb, onebr = fw.sbuf("oneb", [128, 1], F32)
        op("vector", lambda e: e.memset(oneb[:], 1.0), w=[onebr])

        def gates_phase(l):
            with Phase():
                hps = alloc_hps(NP)
                wg, wgr = fw.sbuf("wg", [128, KC, 16], BF16)
                load_w2(wg, wgr, w_inb[l][:, C_MI:C_MI + 16], 16)
                fb, fbr = fw.sbuf("fb", [128, 8], F32)
                dma("sync", fb[:], ml_fb[l].partition_broadcast(128), sb=fbr, w=[fbr])
                Gt, Gtr = fw.sbuf("Gt", [128, NB, 16], F32)
                for pi, tok0, N in pieces():
                    hp, hpr = hps[pi % 2]
                    dma("sync", hp[:, :, :N], h2d_v[:, :, tok0:tok0 + N], sb=hpr, w=[hpr])
                    pg, pgr = PS[pi % 2]
                    nb_ = N // 128
                    for b_ in range(nb_):
                        tmj(pg[:, b_ * 16:(b_ + 1) * 16], pgr, wg, wgr, 0, 16, hp, hpr, b_ * 128, 128)
                    blk0 = tok0 // 128
                    op("scalar", lambda e: e.copy(Gt[:, blk0:blk0 + nb_, :], pg[:, 0:nb_ * 16].rearrange("p (b c) -> p b c", c=16)), r=[pgr], w=[Gtr])
                zf, zfr = fw.sbuf("zf", [128, NB, 8], F32)
                spd = [fw.sbuf("spd%d" % i, [128, NB, 4], F32) for i in range(2)]
                tg, tgr = fw.sbuf("tg", [128, NB, 4], F32)
                op("vector", lambda e: e.tensor_tensor(zf[:], Gt[:, :, 8:16], fb[:].unsqueeze(1).to_broadcast([128, NB, 8]), ALU.add), r=[Gtr, fbr], w=[zfr])
                op("scalar", lambda e: e.activation(zf[:], zf[:], AF.Exp, scale=-1.0), r=[zfr], w=[zfr])
                op("scalar", lambda e: e.activation(zf[:], zf[:], AF.Ln, bias=oneb[:, 0:1], scale=1.0), r=[zfr, onebr], w=[zfr])
                for dd in range(2):
                    sp_, spr = spd[dd]
                    op("vector", lambda e: e.tensor_copy(sp_[:], zf[:, :, dd * 4:(dd + 1) * 4]), r=[zfr], w=[spr])
                    pc, pcr = PS[2 + dd]
                    msk, mskr = (masku_f, masku_fr) if dd == 0 else (maskl_f, maskl_fr)
                    spf = sp_[:].rearrange("p b c -> p (b c)")
                    W4 = NB * 4
                    op("tensor", lambda e: e.matmul(pc[:, 0:W4], msk[:], spf, start=True, stop=True), r=[mskr, spr], w=[pcr])
                    op("tensor", lambda e: e.matmul(pc[:, W4:2 * W4], ones_f[:], spf, start=True, stop=True), r=[ones_fr, spr], w=[pcr])
                    nbv = pc[:, 0:W4].rearrange("p (b c) -> p b c", c=4)
                    nbLv = pc[:, W4:2 * W4].rearrange("p (b c) -> p b c", c=4)
                    op("vector", lambda e: e.tensor_tensor(tg[:], Gt[:, :, dd * 4:(dd + 1) * 4], nbv, ALU.add), r=[Gtr, pcr], w=[tgr])
                    op("scalar", lambda e: e.activation(gA[:, dd], tg[:], AF.Exp, bias=ln8[:, 0:1], scale=1.0), r=[tgr, ln8r], w=[gAr])
                    op("scalar", lambda e: e.activation(gEb[:, dd], nbv, AF.Exp, scale=-1.0), r=[pcr], w=[gEbr])
                    op("scalar", lambda e: e.activation(gEbL[:, dd], nbLv, AF.Exp, scale=-1.0), r=[pcr], w=[gEbLr])
                    op("vector", lambda e: e.tensor_tensor(gW[:, dd], gA[:, dd], gEbL[:, dd], ALU.mult), r=[gAr, gEbLr], w=[gWr])


        def rsqrt_small(buf, bufr, n, scale):
            op("scalar", lambda e: e.activation(buf, buf, AF.Sqrt, bias=epsb[0:n, 0:1], scale=scale), r=[bufr, epsr], w=[bufr])
            op("vector", lambda e: e.reciprocal(buf, buf), r=[bufr], w=[bufr])

        def mlstm_phase(l, hh):
            with Phase():
                hps = alloc_hps(NP)
                wmq, wmqr = fw.sbuf("wmq", [128, KC, 64], BF16)
                wmk, wmkr = fw.sbuf("wmk", [128, KC, 64], BF16)
                wmv, wmvr = fw.sbuf("wmv", [128, KC, 128], BF16)
                wmo, wmor = fw.sbuf("wmo", [128, KC, 128], BF16)
                load_w2(wmq, wmqr, w_inb[l][:, C_MQ + hh * 64:C_MQ + (hh + 1) * 64], 64)
                load_w2(wmk, wmkr, w_inb[l][:, C_MK + hh * 64:C_MK + (hh + 1) * 64], 64)
                load_w2(wmv, wmvr, w_inb[l][:, C_MV + hh * 128:C_MV + (hh + 1) * 128], 128)
                load_w2(wmo, wmor, w_inb[l][:, C_MO + hh * 128:C_MO + (hh + 1) * 128], 128)
                gml, gmlr = fw.sbuf("gml", [128, 128], F32)
                dma("sync", gml[:], ml_norm[l, :, hh * 128:(hh + 1) * 128].partition_broadcast(128), sb=gmlr, w=[gmlr])
                cw, cwr = fw.sbuf("cw", [64, 2, 3], F32)
                dma("sync", cw[:, 0, :], ml_conv[l, hh * 64:(hh + 1) * 64, :], sb=cwr, w=[cwr], piece=True)
                dma("sync", cw[:, 1, :], ml_conv[l, 256 + hh * 64:256 + (hh + 1) * 64, :], sb=cwr, w=[cwr], piece=True)
                PW = T + 4
                pre = [fw.sbuf("pre%d" % i, [64, PW], F32) for i in range(2)]
                QK = [fw.sbuf("QK%d" % i, [64, T], BF16) for i in range(2)]
                Vx, _ = fw.sbuf("Vx", [128, NB, 129], BF16)
                Vxp = [fw.reg("Vx%d" % i) for i in range(9)]
                Vx1r = fw.reg("Vx1")
                og, ogr = fw.sbuf("og", [128, NB, 128], F32)
                prr = [[fw.reg("pre%d_%d" % (z, i)) for i in range(9)] for z in range(2)]
                padr = [fw.reg("pad%d" % z) for z in range(2)]
                QKr = [[fw.reg("QK%d_%d" % (z, i)) for i in range(5)] for z in range(2)]
                accs = [fw.sbuf("acc%d" % i, [64, 1024], F32) for i in range(2)]
                for z in range(2):
                    pz, _pz = pre[z]
                    for c_ in (0, L + 1, L + 2, PW - 1):
                        op("vector", lambda e: e.memset(pz[:, c_:c_ + 1], 0.0), w=[padr[z]], piece=True)
                op("vector", lambda e: e.memset(Vx[:, :, 128:129], 1.0), w=[Vx1r])

                def pcb(blk):
                    return 8 if blk >= 32 else blk // 4

                def ckb(blk):
                    return 4 if blk >= 32 else blk // 8

                def conv_chunk(c):
                    s0, d0, n = [(0, 0, 1024), (1024, 1024, 1024), (2048, 2048, 1024), (3072, 3072, 1024), (L + 2, L, LC)][c]
                    for z in range(2):
                        pz, _pz = pre[z]
                        qk, _qk = QK[z]
                        acc, accr = accs[z]
                        if c < 3:
                            rr = [prr[z][2 * c], prr[z][2 * c + 1], prr[z][2 * c + 2], padr[z]]
                        elif c == 3:
                            rr = [prr[z][6], prr[z][7], padr[z]]
                        else:
                            rr = [prr[z][8], padr[z]]
                        op("vector", lambda e: e.tensor_scalar(acc[:, :n], pz[:, s0:s0 + n], cw[:, z, 0:1], None, ALU.mult), r=rr + [cwr], w=[accr])
                        op("vector", lambda e: e.scalar_tensor_tensor(acc[:, :n], pz[:, s0 + 1:s0 + 1 + n], cw[:, z, 1:2], acc[:, :n], ALU.mult, ALU.add), r=rr + [cwr, accr], w=[accr])
                        op("vector", lambda e: e.scalar_tensor_tensor(acc[:, :n], pz[:, s0 + 2:s0 + 2 + n], cw[:, z, 2:3], acc[:, :n], ALU.mult, ALU.add), r=rr + [cwr, accr], w=[accr])
                        op("scalar", lambda e: e.activation(qk[:, d0:d0 + n], acc[:, :n], AF.Silu), r=[accr], w=[QKr[z][c]])

                def poff(tok0):
                    return 1 + tok0 if tok0 < L else L + 3 + (tok0 - L)

                for pi, tok0, N in pieces():
                    hp, hpr = hps[pi % 2]
                    dma("sync", hp[:, :, :N], h2d_v[:, :, tok0:tok0 + N], sb=hpr, w=[hpr])
                    for z, (wt, wtr) in enumerate(((wmq, wmqr), (wmk, wmkr))):
                        pp, ppr = PS[z]
                        fm(pp, ppr, wt, wtr, 0, 64, hp, hpr, N)
                        pz, _pz = pre[z]
                        o_ = poff(tok0)
                        op("scalar", lambda e: e.copy(pz[:, o_:o_ + N], pp[0:64, :N]), r=[ppr], w=[prr[z][pi]])
                    for b_ in range(N // 128):
                        blk = tok0 // 128 + b_
                        pv, pvr = PS[2 + blk % 2]
                        tmj(pv[:, 0:128], pvr, wmv, wmvr, 0, 128, hp, hpr, b_ * 128, 128)
                        op("vector", lambda e: e.tensor_copy(Vx[:, blk, 0:128], pv[:, 0:128]), r=[pvr], w=[Vxp[pi]], piece=True)
                        po, por = PS[4 + blk % 2]
                        tmj(po[:, 0:128], por, wmo, wmor, 0, 128, hp, hpr, b_ * 128, 128)
                        op("scalar", lambda e: e.activation(og[:, blk, :], po[:, 0:128], AF.Sigmoid), r=[por], w=[ogr])
                    if pi in (2, 4, 6):
                        conv_chunk(pi // 2 - 1)
                    elif pi == 7:
                        conv_chunk(3)
                    elif pi == 8:
                        conv_chunk(4)
                QT, _qt = QK[0]
                KT, _kt = QK[1]
                QTr, KTr = QKr[0], QKr[1]
                Kt, Ktr = fw.sbuf("Kt", [128, NB, 64], BF16)
                for b0 in range(0, NB, 16):
                    nb_ = min(16, NB - b0)
                    for b_ in range(nb_):
                        blk = b0 + b_
                        op("tensor", lambda e: e.transpose(psB[:, b_ * 64:(b_ + 1) * 64], KT[:, blk * 128:(blk + 1) * 128], ident_b[0:64, 0:64]),
                           r=[KTr[ckb(blk)], ident_br], w=[psBr], sig=(b_ == nb_ - 1))
                    op("scalar", lambda e: e.copy(Kt[:, b0:b0 + nb_, :], psB[:, 0:nb_ * 64].rearrange("p (b c) -> p b c", c=64)), r=[psBr], w=[Ktr])
                Kw, Kwr = fw.sbuf("Kw", [128, 2, NB, 64], BF16)
                for dd in range(2):
                    op("vector", lambda e: e.tensor_tensor(Kw[:, dd], Kt[:], gW[:, dd, :, hh:hh + 1].to_broadcast([128, NB, 64]), ALU.mult), r=[Ktr, gWr], w=[Kwr])
                hraw, hrawr = fw.sbuf("hraw", [128, 2, NB, 129], F32)
                hregs = [fw.reg("hraw%d" % i) for i in range(2)]
                SpA = [fw.sbuf("SpA%d" % i, [128, NB, 128], BF16) for i in range(2)]
                stmp = [fw.sbuf("stmp%d" % i, [128, 4, 128], F32) for i in range(2)]
                bk = 0
                for b0 in range(0, NB, 4):
                    nb_ = min(4, NB - b0)
                    pst, pstr = PS[bk % 2]
                    bk += 1
                    for b_ in range(nb_):
                        blk = b0 + b_
                        op("tensor", lambda e: e.matmul(pst[:, b_ * 128:(b_ + 1) * 128], KT[:, blk * 128:(blk + 1) * 128], QT[:, blk * 128:(blk + 1) * 128], start=True, stop=True),
                           r=[KTr[ckb(blk)], QTr[ckb(blk)]], w=[pstr], sig=(b_ == nb_ - 1))
                    for dd in range(2):
                        msk, mskr = (masku_f, masku_fr) if dd == 0 else (maskl_f, maskl_fr)
                        st_, st_r = stmp[dd]
                        op("vector", lambda e: e.tensor_tensor(st_[:, 0:nb_, :], pst[:, 0:nb_ * 128].rearrange("p (b t) -> p b t", t=128),
                                                               gA[:, dd, b0:b0 + nb_, hh:hh + 1].to_broadcast([128, nb_, 128]), ALU.mult), r=[pstr, gAr], w=[st_r])
                        op("gpsimd", lambda e: e.tensor_tensor(SpA[dd][0][:, b0:b0 + nb_, :], st_[:, 0:nb_, :], msk[:].unsqueeze(1).to_broadcast([128, nb_, 128]), ALU.mult),
                           r=[st_r, mskr], w=[SpA[dd][1]])
                for dd in range(2):
                    for b0 in range(0, NB, 3):
                        nb_ = min(3, NB - b0)
                        pso, psor = PS[2 + bk % 2]
                        bk += 1
                        for b_ in range(nb_):
                            blk = b0 + b_
                            op("tensor", lambda e: e.matmul(pso[:, b_ * 129:(b_ + 1) * 129], SpA[dd][0][:, blk, :], Vx[:, blk, :], start=True, stop=True),
                               r=[SpA[dd][1], Vxp[pcb(blk)], Vx1r], w=[psor], sig=(b_ == nb_ - 1))
                        op("scalar", lambda e: e.copy(hraw[:, dd, b0:b0 + nb_, :], pso[:, 0:nb_ * 129].rearrange("p (b v) -> p b v", v=129)), r=[psor], w=[hregs[dd]])
                Csts = [fw.sbuf("Cst%d" % i, [64, 129], F32) for i in range(2)]
                Cbf = [[fw.sbuf("Cbf%d_%d" % (i, j), [64, 129], BF16) for j in range(2)] for i in range(2)]
                for dd in range(2):
                    op("vector", lambda e: e.memset(Csts[dd][0][:], 0.0), w=[Csts[dd][1]])
                    for j in range(2):
                        op("gpsimd", lambda e: e.memset(Cbf[dd][j][0][:], 0.0), w=[Cbf[dd][j][1]])
                lat_f = [list(range(b, min(b + 3, 32))) for b in range(0, 32, 3)]
                groups = [[[32, 33]] + lat_f, [[33, 32]] + [list(reversed(g_)) for g_ in reversed(lat_f)]]
                PS7 = (psB[:, :].bitcast(F32), psBr)
                pibanks = [[PS[4], PS[6]], [PS[5], PS7]]
                for dd in range(2):
                    assert sum(len(g_) for g_ in groups[dd]) == NB
                j = [0, 0]
                for gi in range(len(groups[0])):
                    for dd in range(2):
                        grp = groups[dd][gi]
                        bmin = min(grp)
                        pin, pinr = pibanks[dd][gi % 2]
                        Cst, Cstr = Csts[dd]
                        for blk in grp:
                            jj = j[dd]
                            j[dd] += 1
                            psc, pscr = PS[(jj % 2) * 2 + dd]
                            op("tensor", lambda e: e.matmul(psc[0:64, 0:129], Kw[:, dd, blk, :], Vx[:, blk, :], start=True, stop=True), r=[Kwr, Vxp[pcb(blk)], Vx1r], w=[pscr])
                            cprev, cprevr = Cbf[dd][(jj + 1) % 2]
                            sl_ = (blk - bmin) * 129
                            op("tensor", lambda e: e.matmul(pin[:, sl_:sl_ + 129], QT[:, blk * 128:(blk + 1) * 128], cprev[:], start=True, stop=True), r=[QTr[ckb(blk)], cprevr], w=[pinr])
                            op("vector", lambda e: e.scalar_tensor_tensor(Cst[:], Cst[:], gEbL[0:64, dd, blk, hh:hh + 1], psc[0:64, 0:129], ALU.mult, ALU.add),
                               r=[Cstr, gEbLr, pscr], w=[Cstr])
                            cnext, cnextr = Cbf[dd][jj % 2]
                            op("scalar", lambda e: e.copy(cnext[:], Cst[:]), r=[Cstr], w=[cnextr])
                        n_ = len(grp)
                        op("vector", lambda e: e.tensor_tensor(hraw[:, dd, bmin:bmin + n_, :], hraw[:, dd, bmin:bmin + n_, :],
                                                               pin[:, 0:n_ * 129].rearrange("p (b v) -> p b v", v=129), ALU.add), r=[pinr, hregs[dd]], w=[hregs[dd]])
                op("vector", lambda e: e.tensor_copy(hraw[:, 0, 0, 0:1], hraw[:, 0, 0, 0:1]), r=[hregs[0], hregs[1]], w=[hrawr])
                dn, dnr = fw.sbuf("dn", [128, 2, NB], F32)
                for dd in range(2):
                    op("vector", lambda e: e.tensor_tensor(dn[:, dd, :], hraw[:, dd, :, 128], gEb[:, dd, :, hh], ALU.mult), r=[hrawr, gEbr], w=[dnr])
                op("scalar", lambda e: e.activation(dn[:], dn[:], AF.Abs), r=[dnr], w=[dnr])
                op("vector", lambda e: e.tensor_scalar(dn[:], dn[:], 1.0, None, ALU.max), r=[dnr], w=[dnr])
                op("vector", lambda e: e.reciprocal(dn[:], dn[:]), r=[dnr], w=[dnr])
                for dd in range(2):
                    op("vector", lambda e: e.tensor_tensor(dn[:, dd, :], dn[:, dd, :], gEb[:, dd, :, hh], ALU.mult), r=[dnr, gEbr], w=[dnr])
                    op("vector", lambda e: e.tensor_tensor(hraw[:, dd, :, 0:128], hraw[:, dd, :, 0:128], dn[:, dd, :].unsqueeze(2).to_broadcast([128, NB, 128]), ALU.mult),
                       r=[hrawr, dnr], w=[hrawr])
                hsum = hraw[:, 0, :, 0:128]
                hsumr = hrawr
                op("gpsimd", lambda e: e.tensor_tensor(hsum, hraw[:, 0, :, 0:128], hraw[:, 1, :, 0:128], ALU.add), r=[hrawr], w=[hrawr])
                ssum, ssumr = fw.sbuf("ssum", [128, NB], F32)
                readout(hsum, hsumr, ssum, ssumr, hraw[:, 1, :, 0:128], hrawr, og, ogr, gml[:, :], gmlr, 128, NB)
                yb, ybr = fw.sbuf("yb", [128, NB, 128], BF16)
                op("vector", lambda e: e.tensor_copy(yb[:], hsum), r=[hsumr], w=[ybr])
                stg = [(hps[i][0][:, 0:2, :].rearrange("p a b -> p (a b)"), hps[i][1]) for i in range(2)]
                for gi, b0 in enumerate(range(0, NB, 8)):
                    nb_ = min(8, NB - b0)
                    for b_ in range(nb_):
                        op("tensor", lambda e: e.transpose(psB[:, b_ * 128:(b_ + 1) * 128], yb[:, b0 + b_, :], ident_b[:]), r=[ybr, ident_br], w=[psBr], sig=(b_ == nb_ - 1))
                    st, str_ = stg[gi % 2]
                    op("scalar", lambda e: e.copy(st[:, 0:nb_ * 128], psB[:, 0:nb_ * 128]), r=[psBr], w=[str_])
                    dma("sync", mlTd[hh * 128:(hh + 1) * 128, b0 * 128:(b0 + nb_) * 128], st[:, 0:nb_ * 128], sb=str_, r=[str_])

        def readout(h, hr, ssum, ssumr, scratch, scratchr, gate, gater, gvec, gvecr, P, NBK):
            op("scalar", lambda e: e.activation(scratch, h, AF.Square), r=[hr], w=[scratchr])
            op("vector", lambda e: e.tensor_reduce(ssum[:], scratch, AX.X, ALU.add), r=[scratchr], w=[ssumr])
            rsqrt_small(ssum[:], ssumr, P, 1.0 / 128)
            op("vector", lambda e: e.tensor_tensor(h, h, ssum[:].unsqueeze(2).to_broadcast([P, NBK, 128]), ALU.mult), r=[hr, ssumr], w=[hr])
            op("vector", lambda e: e.tensor_tensor(h, h, gvec[0:P].unsqueeze(1).to_broadcast([P, NBK, 128]), ALU.mult), r=[hr, gvecr], w=[hr])
            op("vector", lambda e: e.tensor_tensor(h, h, gate[:], ALU.mult), r=[hr, gater], w=[hr])

        NCH = T // 64
        NC32 = T // 32
        lbt, lbtr = fw.sbuf("lbt", [64, 2, 4], F32)
        lb1, lb1r = fw.sbuf("lb1", [64, DEPTH, 4], F32)
        oml, omlr = fw.sbuf("oml", [64, DEPTH, 4], F32)
        rmask, rmaskr = fw.sbuf("rmask", [64, 512], F32)

        def lb_setup():
            dma("sync", lbt[:], hg_lbl, sb=lbtr, w=[lbtr])
            dma("sync", rmask[:], c_rmask, sb=rmaskr, w=[rmaskr])
            op("scalar", lambda e: e.activation(lbt[:], lbt[:], AF.Exp), r=[lbtr], w=[lbtr])
            op("vector", lambda e: e.memset(lb1[:], 0.0), w=[lb1r])
            op("vector", lambda e: e.tensor_tensor(lb1[:, 1, :], lbt[:, 0, :], lbt[:, 1, :], ALU.add), r=[lbtr], w=[lb1r])
            op("vector", lambda e: e.reciprocal(lb1[:, 1, :], lb1[:, 1, :]), r=[lb1r], w=[lb1r])
            op("vector", lambda e: e.tensor_tensor(lb1[:, 1, :], lb1[:, 1, :], lbt[:, 1, :], ALU.mult), r=[lb1r, lbtr], w=[lb1r])
            op("vector", lambda e: e.tensor_scalar(oml[:], lb1[:], -1.0, 1.0, ALU.mult, ALU.add), r=[lb1r], w=[omlr])

        def hgrn_phase(l, hh):
            with Phase():
                hps = alloc_hps(NP)
                whq, whqr = fw.sbuf("whq", [128, KC, 64], BF16)
                whf = [fw.sbuf("whf%d" % i, [128, KC, 64], BF16) for i in range(2)]
                whv, whvr = fw.sbuf("whv", [128, KC, 128], BF16)
                whg, whgr = fw.sbuf("whg", [128, KC, 128], BF16)
                load_w2(whq, whqr, w_inb[l][:, C_HQ + hh * 64:C_HQ + (hh + 1) * 64], 64)
                for dd in range(2):
                    load_w2(whf[dd][0], whf[dd][1], w_inb[l][:, C_HF + dd * 256 + hh * 64:C_HF + dd * 256 + (hh + 1) * 64], 64)
                load_w2(whv, whvr, w_inb[l][:, C_HV + hh * 128:C_HV + (hh + 1) * 128], 128)
                load_w2(whg, whgr, w_inb[l][:, C_HG + hh * 128:C_HG + (hh + 1) * 128], 128)
                ghg, ghgr = fw.sbuf("ghg", [64, 128], F32)
                dma("sync", ghg[:], hg_norm[l, :, hh * 128:(hh + 1) * 128].partition_broadcast(64), sb=ghgr, w=[ghgr])
                qt_ = [fw.sbuf("qt%d" % i, [64, T], BF16) for i in range(2)]
                kt_ = [fw.sbuf("kt%d" % i, [64, T], BF16) for i in range(2)]
                PT, PTr = fw.sbuf("PT", [64, 2, NC32, 2], F32)
                Vt, Vtr = fw.sbuf("Vt", [64, NCH, 128], BF16)
                gt, gtr = fw.sbuf("gt", [64, NCH, 128], BF16)
                osum, osumr = fw.sbuf("osum", [64, NCH, 128], F32)
                graw, grawr = fw.sbuf("graw", [64, 8, 128], F32)
                gtmp, gtmpr = fw.sbuf("gtmp", [64, 8, 128], F32)
                tblk, tblkr = fw.sbuf("tblk", [64, 11 * NP], F32)
                def carve(i, nm):
                    return tblk[:, i * NP:(i + 1) * NP], fw.reg(nm)
                qs, qsr = carve(0, "qs")
                tA = [carve(1 + i, "tA%d" % i) for i in range(2)]
                tB = [carve(3 + i, "tB%d" % i) for i in range(2)]
                tC = [carve(5 + i, "tC%d" % i) for i in range(2)]
                tD = [carve(7 + i, "tD%d" % i) for i in range(2)]
                tE = [carve(9 + i, "tE%d" % i) for i in range(2)]
                for pi, tok0, N in pieces():
                    hp, hpr = hps[pi % 2]
                    dma("sync", hp[:, :, :N], h2d_v[:, :, tok0:tok0 + N], sb=hpr, w=[hpr])
                    nch = N // 64
                    ch0 = tok0 // 64
                    n32 = N // 32
                    c32 = tok0 // 32
                    pq, pqr = PS[0]
                    fm(pq, pqr, whq, whqr, 0, 64, hp, hpr, N)
                    op("scalar", lambda e: e.activation(qs[:, :N], pq[0:64, :N], AF.Exp, scale=-1.0), r=[pqr], w=[qsr])
                    op("scalar", lambda e: e.activation(qs[:, :N], qs[:, :N], AF.Ln, bias=oneb[0:64, 0:1], scale=1.0), r=[qsr, onebr], w=[qsr])
                    op("scalar", lambda e: e.activation(qs[:, :N], qs[:, :N], AF.Exp, scale=-1.0), r=[qsr], w=[qsr])
                    op("vector", lambda e: e.tensor_tensor(qs[:, :N], qs[:, :N], pq[0:64, :N], ALU.mult), r=[qsr, pqr], w=[qsr])
                    for dd in range(2):
                        pz, pzr = PS[1 + dd]
                        fm(pz, pzr, whf[dd][0], whf[dd][1], 0, 64, hp, hpr, N)
                        f_, f_r = tA[dd]
                        lf, lfr = tB[dd]
                        p_, p_r = tC[dd]
                        ud, udr = tD[dd]
                        ee, eer = tE[dd]
                        op("scalar", lambda e: e.activation(f_[:, :N], pz[0:64, :N], AF.Exp, scale=-1.0), r=[pzr], w=[f_r])
                        op("scalar", lambda e: e.activation(f_[:, :N], f_[:, :N], AF.Ln, bias=oneb[0:64, 0:1], scale=1.0), r=[f_r, onebr], w=[f_r])
                        op("scalar", lambda e: e.activation(f_[:, :N], f_[:, :N], AF.Exp, scale=-1.0), r=[f_r], w=[f_r])
                        op("vector", lambda e: e.tensor_scalar(f_[:, :N], f_[:, :N], oml[:, l, hh:hh + 1], lb1[:, l, hh:hh + 1], ALU.mult, ALU.add),
                           r=[f_r, omlr, lb1r], w=[f_r])
                        op("scalar", lambda e: e.activation(lf[:, :N], f_[:, :N], AF.Ln), r=[f_r], w=[lfr])
                        op("vector", lambda e: e.tensor_tensor_scan(p_[:, :N], rmask[:, :N], lf[:, :N], 0.0, ALU.mult, ALU.add), r=[rmaskr, lfr], w=[p_r])
                        p3 = p_[:, :N].rearrange("p (c t) -> p c t", t=64)
                        op("gpsimd", lambda e: e.tensor_copy(PT[:, dd, c32:c32 + n32, 0], p_[:, :N].rearrange("p (c t) -> p c t", t=32)[:, :, 15]), r=[p_r], w=[PTr])
                        op("gpsimd", lambda e: e.tensor_copy(PT[:, dd, c32:c32 + n32, 1], p_[:, :N].rearrange("p (c t) -> p c t", t=32)[:, :, 31]), r=[p_r], w=[PTr])
                        if dd == 1:
                            op("vector", lambda e: e.tensor_tensor(ud[:, :N], p_[:, :N], lf[:, :N], ALU.subtract), r=[p_r, lfr], w=[udr])
                            usrc = ud
                            usrcr = udr
                        else:
                            usrc = p_
                            usrcr = p_r
                        op("vector", lambda e: e.tensor_tensor(ud[:, :N].rearrange("p (c t) -> p c t", t=32), usrc[:, :N].rearrange("p (c t) -> p c t", t=32),
                                                               PT[:, dd, c32:c32 + n32, 0:1].to_broadcast([64, n32, 32]), ALU.subtract),
                           r=[usrcr, PTr], w=[udr])
                        sq_, sk_ = (1.0, -1.0) if dd == 0 else (-1.0, 1.0)
                        op("vector", lambda e: e.tensor_scalar(f_[:, :N], f_[:, :N], -1.0, 1.0, ALU.mult, ALU.add), r=[f_r], w=[f_r])
                        op("scalar", lambda e: e.activation(ee[:, :N], ud[:, :N], AF.Exp, scale=sq_), r=[udr], w=[eer])
                        op("vector", lambda e: e.scalar_tensor_tensor(qt_[dd][0][:, tok0:tok0 + N], qs[:, :N], 0.125, ee[:, :N], ALU.mult, ALU.mult),
                           r=[qsr, eer], w=[qt_[dd][1]])
                        op("scalar", lambda e: e.activation(ee[:, :N], ud[:, :N], AF.Exp, scale=sk_), r=[udr], w=[eer])
                        op("vector", lambda e: e.tensor_tensor(kt_[dd][0][:, tok0:tok0 + N], f_[:, :N], ee[:, :N], ALU.mult), r=[f_r, eer], w=[kt_[dd][1]])
                    for c_ in range(nch):
                        cc = ch0 + c_
                        pv, pvr = PS[3 + cc % 2]
                        tmj(pv[0:64, 0:128], pvr, whv, whvr, 0, 128, hp, hpr, c_ * 64, 64)
                        op("vector", lambda e: e.tensor_copy(Vt[:, cc, :], pv[0:64, 0:128]), r=[pvr], w=[Vtr])
                        pg, pgr = PS[5 + cc % 2]
                        tmj(pg[0:64, 0:128], pgr, whg, whgr, 0, 128, hp, hpr, c_ * 64, 64)
                        op("vector", lambda e: e.tensor_copy(graw[:, c_, :], pg[0:64, 0:128]), r=[pgr], w=[grawr])
                    op("scalar", lambda e: e.activation(gtmp[:, 0:nch, :], graw[:, 0:nch, :], AF.Exp, scale=-1.0), r=[grawr], w=[gtmpr])
                    op("scalar", lambda e: e.activation(gtmp[:, 0:nch, :], gtmp[:, 0:nch, :], AF.Ln, bias=oneb[0:64, 0:1], scale=1.0), r=[gtmpr, onebr], w=[gtmpr])
                    op("scalar", lambda e: e.activation(gtmp[:, 0:nch, :], gtmp[:, 0:nch, :], AF.Exp, scale=-1.0), r=[gtmpr], w=[gtmpr])
                    op("vector", lambda e: e.tensor_tensor(gt[:, ch0:ch0 + nch, :], gtmp[:, 0:nch, :], graw[:, 0:nch, :], ALU.mult), r=[gtmpr, grawr], w=[gtr])
                X1, X1r = fw.sbuf("X1", [64, 2, NC32], F32)
                X2, X2r = fw.sbuf("X2", [64, 2, NC32], F32)
                X3, X3r = fw.sbuf("X3", [64, 2, NC32], F32)
                op("vector", lambda e: e.tensor_tensor(X2[:], PT[:, :, :, 1], PT[:, :, :, 0], ALU.subtract), r=[PTr], w=[X2r])
                op("scalar", lambda e: e.activation(X2[:], X2[:], AF.Exp), r=[X2r], w=[X2r])
                op("scalar", lambda e: e.activation(X1[:], PT[:, :, :, 0], AF.Exp), r=[PTr], w=[X1r])
                op("scalar", lambda e: e.activation(X3[:], PT[:, :, :, 1], AF.Exp), r=[PTr], w=[X3r])
                KY, KYr = fw.sbuf("KY", [64, 2 * NCH * 64], BF16)
                ktok = KY[:].rearrange("p (d c k) -> p d c k", d=2, k=64)
                fw.barrier()
                ksv = tblk[:, :].bitcast(BF16)
                ks_ = [(ksv[:, dd * T:(dd + 1) * T], fw.reg("ks%d" % dd)) for dd in range(2)]
                for dd in range(2):
                    Xd, Xdr = (X2, X2r) if dd == 0 else (X1, X1r)
                    op("vector", lambda e: e.tensor_tensor(ks_[dd][0].rearrange("p (c t) -> p c t", t=32), kt_[dd][0][:, :].rearrange("p (c t) -> p c t", t=32),
                                                           Xd[:, dd, :].unsqueeze(2).to_broadcast([64, NC32, 32]), ALU.mult), r=[kt_[dd][1], Xdr], w=[ks_[dd][1]])
                for dd in range(2):
                    for c0 in range(0, NCH, 16):
                        n_ = min(16, NCH - c0)
                        for c_ in range(n_):
                            cc = c0 + c_
                            op("tensor", lambda e: e.transpose(psB[0:64, c_ * 64:(c_ + 1) * 64], ks_[dd][0][:, cc * 64:(cc + 1) * 64], ident_b[0:64, 0:64]),
                               r=[ks_[dd][1], ident_br], w=[psBr], sig=(c_ == n_ - 1))
                        op("scalar", lambda e: e.copy(ktok[:, dd, c0:c0 + n_, :], psB[0:64, 0:n_ * 64].rearrange("p (c k) -> p c k", k=64)), r=[psBr], w=[KYr])
                mt = [fw.sbuf("mt%d" % i, [64, 32], F32) for i in range(2)]
                for dd in range(2):
                    msk, mskr = (masku_f, masku_fr) if dd == 0 else (maskl_f, maskl_fr)
                    op("vector", lambda e: e.tensor_copy(mt[dd][0][0:32, :], msk[0:32, 0:32]), r=[mskr], w=[mt[dd][1]])
                    op("vector", lambda e: e.tensor_copy(mt[dd][0][32:64, :], msk[32:64, 32:64]), r=[mskr], w=[mt[dd][1]])
                AmA = [fw.sbuf("AmA%d" % i, [64, NCH, 32], BF16) for i in range(2)]
                bk = 0
                for dd in range(2):
                    qd, qdr = qt_[dd]
                    kd, kdr = kt_[dd]
                    Aall, Aallr = AmA[dd]
                    for g0 in range(0, NCH, 16):
                        n_ = min(16, NCH - g0)
                        pa, par = PS[bk % 3]
                        bk += 1
                        for j_ in range(n_):
                            for a_ in range(2):
                                cc = (g0 + j_) * 2 + a_
                                P_ = slice(a_ * 32, (a_ + 1) * 32)
                                sl = slice(cc * 32, (cc + 1) * 32)
                                op("tensor", lambda e: e.matmul(pa[P_, j_ * 32:(j_ + 1) * 32], kd[:, sl], qd[:, sl], start=True, stop=True),
                                   r=[kdr, qdr], w=[par], sig=(j_ == n_ - 1 and a_ == 1))
                        op("vector", lambda e: e.tensor_tensor(Aall[:, g0:g0 + n_, :], pa[0:64, 0:n_ * 32].rearrange("p (c t) -> p c t", t=32),
                                                               mt[dd][0][:].unsqueeze(1).to_broadcast([64, n_, 32]), ALU.mult), r=[par, mt[dd][1]], w=[Aallr])
                for dd in range(2):
                    Aall, Aallr = AmA[dd]
                    for g0 in range(0, NCH, 4):
                        po, por = PS[3 + bk % 3]
                        bk += 1
                        for j_ in range(4):
                            c64 = g0 + j_
                            for a_ in range(2):
                                P_ = slice(a_ * 32, (a_ + 1) * 32)
                                op("tensor", lambda e: e.matmul(po[P_, j_ * 128:(j_ + 1) * 128], Aall[P_, c64, :], Vt[P_, c64, :], start=True, stop=True),
                                   r=[Aallr, Vtr], w=[por], sig=(j_ == 3 and a_ == 1))
                        if dd == 0:
                            op("scalar", lambda e: e.copy(osum[:, g0:g0 + 4, :], po[0:64, :].rearrange("p (c v) -> p c v", v=128)), r=[por], w=[osumr])
                        else:
                            op("vector", lambda e: e.tensor_tensor(osum[:, g0:g0 + 4, :], osum[:, g0:g0 + 4, :], po[0:64, :].rearrange("p (c v) -> p c v", v=128), ALU.add),
                               r=[por, osumr], w=[osumr])
                for dd in range(2):
                    Xin, Xinr = (X1, X1r) if dd == 0 else (X2, X2r)
                    op("vector", lambda e: e.tensor_tensor(qt_[dd][0][:, :].rearrange("p (c t) -> p c t", t=32), qt_[dd][0][:, :].rearrange("p (c t) -> p c t", t=32),
                                                           Xin[:, dd, :].unsqueeze(2).to_broadcast([64, NC32, 32]), ALU.mult), r=[qt_[dd][1], Xinr], w=[qt_[dd][1]])
                Sst = [fw.sbuf("Sst%d" % i, [64, 128], F32) for i in range(2)]
                Sbf = [[fw.sbuf("Sbf%d_%d" % (i, j), [64, 128], BF16) for j in range(2)] for i in range(2)]
                for dd in range(2):
                    op("vector", lambda e: e.memset(Sst[dd][0][:], 0.0), w=[Sst[dd][1]])
                    for j in range(2):
                        op("gpsimd", lambda e: e.memset(Sbf[dd][j][0][:], 0.0), w=[Sbf[dd][j][1]])
                c0c = L // 32
                orders = [list(range(c0c, NC32)) + list(range(c0c)), list(range(NC32 - 1, c0c - 1, -1)) + list(range(c0c - 1, -1, -1))]
                PS7 = (psB[:, :].bitcast(F32), psBr)
                pobanks = [[PS[4], PS[6]], [PS[5], PS7]]
                for j in range(NC32):
                    for dd in range(2):
                        qd, qdr = qt_[dd]
                        cc = orders[dd][j]
                        c64, a_ = cc // 2, cc % 2
                        P_ = slice(a_ * 32, (a_ + 1) * 32)
                        sl = slice(cc * 32, (cc + 1) * 32)
                        grp, slot = c64 // 4, c64 % 4
                        st_, st_r = Sst[dd]
                        pdl, pdlr = PS[(j % 2) * 2 + dd]
                        op("tensor", lambda e: e.matmul(pdl[0:64, 0:128], ktok[P_, dd, c64, :], Vt[P_, c64, :], start=True, stop=True), r=[KYr, Vtr], w=[pdlr])
                        sprev, sprevr = Sbf[dd][(j + 1) % 2]
                        po, por = pobanks[dd][(j // 8) % 2]
                        op("tensor", lambda e: e.matmul(po[P_, slot * 128:(slot + 1) * 128], qd[:, sl], sprev[:], start=True, stop=True), r=[qdr, sprevr], w=[por])
                        op("vector", lambda e: e.scalar_tensor_tensor(st_[:], st_[:], X3[:, dd, cc:cc + 1], pdl[0:64, 0:128], ALU.mult, ALU.add), r=[st_r, X3r, pdlr], w=[st_r])
                        if j + 1 < NC32:
                            snext, snextr = Sbf[dd][j % 2]
                            op("scalar", lambda e: e.copy(snext[:], st_[:]), r=[st_r], w=[snextr])
                        if j % 8 == 7:
                            op("vector", lambda e: e.tensor_tensor(osum[:, grp * 4:(grp + 1) * 4, :], osum[:, grp * 4:(grp + 1) * 4, :],
                                                                   po[0:64, :].rearrange("p (c v) -> p c v", v=128), ALU.add), r=[por, osumr], w=[osumr])
                ssum, ssumr = fw.sbuf("ssum", [64, NCH], F32)
                fw.barrier()
                sqt, sqtr = tblk[:, 0:17 * 128].rearrange("p (c v) -> p c v", v=128), tblkr
                for c0 in range(0, NCH, 17):
                    op("scalar", lambda e: e.activation(sqt, osum[:, c0:c0 + 17, :], AF.Square), r=[osumr], w=[sqtr])
                    op("vector", lambda e: e.tensor_reduce(ssum[:, c0:c0 + 17], sqt, AX.X, ALU.add), r=[sqtr], w=[ssumr])
                rsqrt_small(ssum[:], ssumr, 64, 1.0 / 128)
                op("vector", lambda e: e.tensor_tensor(osum[:], osum[:], ssum[:].unsqueeze(2).to_broadcast([64, NCH, 128]), ALU.mult), r=[osumr, ssumr], w=[osumr])
                op("vector", lambda e: e.tensor_tensor(osum[:], osum[:], ghg[:, :].unsqueeze(1).to_broadcast([64, NCH, 128]), ALU.mult),
                   r=[osumr, ghgr], w=[osumr])
                yh = KY[:].rearrange("p (c v) -> p c v", v=128)
                op("vector", lambda e: e.tensor_tensor(yh, osum[:], gt[:], ALU.mult), r=[osumr, gtr], w=[KYr])
                stg = [(hps[i][0][:, 0:2, :].rearrange("p a b -> p (a b)"), hps[i][1]) for i in range(2)]
                for gi, c0 in enumerate(range(0, NCH, 16)):
                    n_ = min(16, NCH - c0)
                    for c_ in range(n_):
                        op("tensor", lambda e: e.transpose(psB[:, c_ * 64:(c_ + 1) * 64], yh[:, c0 + c_, :], ident_b[0:64, 0:64]), r=[KYr, ident_br], w=[psBr], sig=(c_ == n_ - 1))
                    st, str_ = stg[gi % 2]
                    op("scalar", lambda e: e.copy(st[:, 0:n_ * 64], psB[:, 0:n_ * 64]), r=[psBr], w=[str_])
                    dma("sync", hgTd[hh * 128:(hh + 1) * 128, c0 * 64:(c0 + n_) * 64], st[:, 0:n_ * 64], sb=str_, r=[str_])

        def merge_phase(l):
            last = (l == DEPTH - 1)
            with Phase():
                hps = alloc_hps(NT)
                wbs = []
                for nm, src in (("wba", wb_attb), ("wbm", wb_mlb), ("wbh", wb_hgb)):
                    wt, wtr = fw.sbuf(nm, [128, 4, D], BF16)
                    for kk in range(4):
                        dma("sync", wt[:, kk, :], src[l, kk * 128:(kk + 1) * 128, :], sb=wtr, r=wcast_reg, w=[wtr], piece=True)
                    wbs.append((wt, wtr))
                wbg, wbgr = fw.sbuf("wbg", [128, KC, 3 * D], BF16)
                for kk in range(KC):
                    for c0 in range(0, 3 * D, 1024):
                        dma("sync", wbg[:, kk, c0:c0 + 1024], w_inb[l, kk * 128:(kk + 1) * 128, C_BR + c0:C_BR + c0 + 1024], sb=wbgr, r=wcast_reg, w=[wbgr], piece=True)
                wo, wor = fw.sbuf("wo", [128, KC, D], BF16)
                for kk in range(KC):
                    dma("sync", wo[:, kk, :], w_outb[l, kk * 128:(kk + 1) * 128, :], sb=wor, r=wcast_reg, w=[wor], piece=True)
                brs = [[fw.sbuf("br%d_%d" % (b_, i), [128, 4, NT], BF16) for i in range(2)] for b_ in range(3)]
                xTs = [fw.sbuf("xT%d" % i, [128, KC, NT], F32) for i in range(2)]
                ym, ymr = fw.sbuf("ym", [128, KC, NT], BF16)
                sq, sqr = fw.sbuf("sq", [128, KC, NT], BF16)
                ysb, ysbr = fw.sbuf("ysb", [128, KC, NT], F32)
                gsb = [fw.sbuf("gsb%d" % i, [128, NT], F32) for i in range(3)]
                yacc, yaccr = fw.sbuf("yacc", [128, NT], F32)
                tmp_ = [fw.sbuf("tmp%d" % i, [128, NT], F32) for i in range(2)]
                tmp, tmpr = [t[0] for t in tmp_], [t[1] for t in tmp_]
                rs, rsr = fw.sbuf("rs", [128, NT], F32)
                srcs = (attTd, mlTd, hgTd)
                ntiles = L // NT + (0 if last else 1)
                for ti in range(ntiles):
                    tok0, wi = tile_info(ti)
                    N = NT
                    hp, hpr = hps[ti % 2]
                    xT, xTr = xTs[ti % 2]
                    dma("sync", hp[:, :, :N], h2d_v[:, :, tok0:tok0 + N], sb=hpr, w=[hpr])
                    dma("sync", xT[:, :, :N], xTd_v[:, :, tok0:tok0 + N], sb=xTr, w=[xTr])
                    for b_ in range(3):
                        bt, btr = brs[b_][ti % 2]
                        dma("sync", bt[:, :, :N], srcs[b_][:, tok0:tok0 + N].rearrange("(c p) t -> p c t", p=128), sb=btr, w=[btr])
                    for m in range(KC):
                        for b_ in range(3):
                            bt, btr = brs[b_][ti % 2]
                            wt, wtr = wbs[b_]
                            pz, pzr = PS[b_ % 3]
                            pg, pgr = PS[3 + b_ % 3]
                            for kk in range(4):
                                op("tensor", lambda e: e.matmul(pz[:, :N], wt[:, kk, m * 128:(m + 1) * 128], bt[:, kk, :N], start=(kk == 0), stop=(kk == 3)),
                                   r=[wtr, btr], w=[pzr], sig=(kk == 3))
                            fm(pg, pgr, wbg, wbgr, b_ * D + m * 128, 128, hp, hpr, N)
                            gs, gsr = gsb[b_]
                            op("scalar", lambda e: e.activation(gs[:, :N], pg[:, :N], AF.Sigmoid), r=[pgr], w=[gsr])
                            if b_ == 0:
                                op("vector", lambda e: e.tensor_tensor(yacc[:, :N], gs[:, :N], pz[:, :N], ALU.mult), r=[gsr, pzr], w=[yaccr])
                            else:
                                op("vector", lambda e: e.tensor_tensor(gs[:, :N], gs[:, :N], pz[:, :N], ALU.mult), r=[gsr, pzr], w=[gsr])
                                if b_ == 1:
                                    op("gpsimd", lambda e: e.tensor_tensor(yacc[:, :N], yacc[:, :N], gs[:, :N], ALU.add), r=[gsr, yaccr], w=[yaccr])
                                else:
                                    op("gpsimd", lambda e: e.tensor_tensor(ym[:, m, :N], yacc[:, :N], gs[:, :N], ALU.add), r=[gsr, yaccr], w=[ymr])
                    for m2 in range(KC):
                        py, pyr = PS[m2 % 2]
                        for m in range(KC):
                            op("tensor", lambda e: e.matmul(py[:, :N], wo[:, m, m2 * 128:(m2 + 1) * 128], ym[:, m, :N], start=(m == 0), stop=(m == KC - 1)),
                               r=[wor, ymr], w=[pyr], sig=(m == KC - 1))
                        op("scalar", lambda e: e.activation(sq[:, m2, :N], py[:, :N], AF.Square), r=[pyr], w=[sqr])
                        op("vector", lambda e: e.tensor_copy(ysb[:, m2, :N], py[:, :N]), r=[pyr], w=[ysbr])
                    post_norm_add(1, wi, ysb, ysbr, sq, sqr, rs, rsr, tmp, tmpr, xT, xTr, N)
                    dma("sync", xTd_v[:, :, tok0:tok0 + N], xT[:, :, :N], sb=xTr, r=[xTr], w=[xTd_reg[ti]])

        lb_setup()
        stages = []
        for l in range(DEPTH):
            stages.append(("mod%d" % l, lambda l=l: mod_phase(l)))
            stages.append(("ffn1_%d" % l, lambda l=l: ffn_phase(l, 0)))
            for g in range(2):
                stages.append(("att%d_%d" % (l, g), lambda l=l, g=g: att_phase(l, g)))
            stages.append(("gates%d" % l, lambda l=l: gates_phase(l)))
            for hh in range(4):
                stages.append(("ml%d_%d" % (l, hh), lambda l=l, hh=hh: mlstm_phase(l, hh)))
            for hh in range(4):
                stages.append(("hg%d_%d" % (l, hh), lambda l=l, hh=hh: hgrn_phase(l, hh)))
            stages.append(("merge%d" % l, lambda l=l: merge_phase(l)))
            stages.append(("ffn2_%d" % l, lambda l=l: ffn_phase(l, 1)))
        k.stage_names = [s[0] for s in stages]
        for name, fn in stages:
            if name.startswith("gates") or name.startswith("ml"):
                pass
            fn()
            if debug and name.startswith("mod"):
                dma("sync", dbg[:, 0:144], modT[:].rearrange("p m w -> p (m w)"), sb=modr, r=[modr])
            if stop is not None and name == stop:
                break
        fw.barrier()
        fw.finish()
        k.ninst = fw.ninst
        k.nwait = fw.nwait
    k.nc = nc
    return k


def host_inputs(inputs):
    f32 = np.float32
    g = {k_: np.asarray(v) for k_, v in inputs.items()}
    cst = host_constants()
    shared = {}
    shared["w_ada"] = np.ascontiguousarray(g["w_ada"], f32)
    shared["b_ada"] = np.ascontiguousarray(g["b_ada"].reshape(DEPTH, 72, 128).transpose(0, 2, 1), f32)
    shared["g_pre"] = np.ascontiguousarray(g["norm_pre"].reshape(DEPTH, 3, 8, 128).transpose(0, 3, 1, 2), f32)
    shared["g_post"] = np.ascontiguousarray(g["norm_post"].reshape(DEPTH, 3, 8, 128).transpose(0, 3, 1, 2), f32)
    shared["ffn_w1"] = np.ascontiguousarray(g["ffn_w1"], f32)
    shared["ffn_w3"] = np.ascontiguousarray(g["ffn_w3"], f32)
    shared["ffn_w2"] = np.ascontiguousarray(g["ffn_w2"], f32)
    shared["w_in"] = np.ascontiguousarray(g["w_in"], f32)
    qcols = np.concatenate([h * 64 + _partner(np.arange(64)) for h in range(8)])
    kcols = np.concatenate([C_AK + h * 64 + _partner(np.arange(64)) for h in range(2)])
    shared["w_sw"] = np.ascontiguousarray(g["w_in"][:, :, np.concatenate([qcols, kcols])], f32)
    shared["att_sink"] = np.ascontiguousarray(g["att_sink"].reshape(DEPTH, 1, 8), f32)
    shared["ml_conv"] = np.ascontiguousarray(g["ml_conv"].transpose(0, 2, 1), f32)
    shared["ml_fb"] = np.ascontiguousarray(g["ml_f_bias"].reshape(DEPTH, 1, 8), f32)
    shared["ml_norm"] = np.ascontiguousarray(g["ml_norm"].reshape(DEPTH, 1, 512), f32)
    shared["hg_lbl"] = np.ascontiguousarray(g["hg_lb_logits"].reshape(DEPTH, 4, 64).transpose(2, 0, 1), f32)
    shared["hg_norm"] = np.ascontiguousarray(g["hg_norm"].reshape(DEPTH, 1, 512), f32)
    shared["wb_att"] = np.ascontiguousarray(g["w_branch_att"], f32)
    shared["wb_ml"] = np.ascontiguousarray(g["w_branch_ml"], f32)
    shared["wb_hg"] = np.ascontiguousarray(g["w_branch_hg"], f32)
    shared["w_out"] = np.ascontiguousarray(g["w_out"], f32)
    for nm in ("ident", "ones", "masku", "maskl", "rcos", "rsin", "rmask"):
        shared["c_" + nm] = cst[nm]
    maps = []
    cc = np.asarray(g["c_ctx"], f32).reshape(8, 128).T
    for b in range(g["x"].shape[0]):
        m = dict(shared)
        m["x"] = np.ascontiguousarray(g["x"][b], f32)
        m["ctx"] = np.ascontiguousarray(g["ctx"][b], f32)
        cb = np.asarray(g["c"][b], f32).reshape(8, 128).T
        m["cvec"] = np.ascontiguousarray(np.stack([cb, cc], axis=-1), f32)
        maps.append(m)
    return maps


_CACHE = {}


def kernel(**inputs):
    maps = host_inputs(inputs)
    if "k" not in _CACHE:
        _CACHE["k"] = build_nc()
    k = _CACHE["k"]
    res = run_bass_kernel_spmd(k.nc, maps, core_ids=list(range(len(maps))))
    return np.stack([np.asarray(r["out"], np.float32) for r in res.results], axis=0)
```
